# Optimizing a Trainium2 kernel written in Bass

```python
import jax
import jax.numpy as jnp
from jax import lax
import numpy as np

D_MODEL = 1024
BATCH = 8
SEQ = 4096
DEPTH = 4

GRID_W = 64
CTX_LEN = 256
HEAD_DIM = 64
N_RWKV_HEADS = 8
D_RWKV = N_RWKV_HEADS * HEAD_DIM
D_POOL = D_MODEL - D_RWKV
POOL_WINDOWS = (2, 4, 8, 16)
POOL_GROUP = D_POOL // len(POOL_WINDOWS)
D_DECAY_LORA = 32
D_AAA_LORA = 64
D_GATE_LORA = 96
D_RWKV_PROJ = 3 * D_RWKV + 2 * D_DECAY_LORA + 2 * D_AAA_LORA + D_GATE_LORA
D_EVEN_IN = D_RWKV_PROJ + D_POOL
D_FF = 2816
RMS_EPS = 1e-6
GN_EPS = 64e-5

kernel_name = 'hybrid_rwkv7_pool_shortconv_dit_block'


def rmsnorm(x, g):
    xf = x.astype(jnp.float32)
    y = xf * lax.rsqrt(jnp.mean(xf * xf, axis=-1, keepdims=True) + RMS_EPS)
    return (y * g).astype(x.dtype)


def modulate(h, shift, scale):
    return h * (1.0 + scale) + shift


def neighbours(x, grid):
    n_rows, row_len = grid
    bsz, _, ch = x.shape
    xp = jnp.pad(x.reshape(bsz, n_rows, row_len, ch), ((0, 0), (0, 0), (1, 1), (0, 0)))
    return xp[:, :, :-2].reshape(x.shape), xp[:, :, 2:].reshape(x.shape)


def dwconv3(x, w, grid):
    prev, nxt = neighbours(x, grid)
    return prev * w[0] + x * w[1] + nxt * w[2]


def multiscale_pool(x, w_group, scale, grid):
    n_rows, row_len = grid
    bsz, _, ch = x.shape
    xr = x.reshape(bsz, n_rows, row_len, ch).astype(jnp.float32)
    cs = jnp.concatenate([jnp.zeros_like(xr[:, :, :1]), jnp.cumsum(xr, axis=2)], axis=2)
    pos = jnp.arange(row_len)
    outs = []
    for gi, win in enumerate(POOL_WINDOWS):
        sl = slice(gi * POOL_GROUP, (gi + 1) * POOL_GROUP)
        lo = jnp.clip(pos - win // 2, 0, row_len)
        hi = jnp.clip(pos + win // 2, 0, row_len)
        csg = cs[..., sl]
        mean = (jnp.take(csg, hi, axis=2) - jnp.take(csg, lo, axis=2)) / (hi - lo).astype(jnp.float32)[:, None]
        d = (mean - xr[..., sl]).astype(x.dtype)
        outs.append(jnp.einsum('brlc,cd->brld', d, w_group[gi]))
    return jnp.concatenate(outs, axis=-1).reshape(x.shape) * scale


def to_heads(t):
    return t.reshape(t.shape[:-1] + (N_RWKV_HEADS, HEAD_DIM))


def rwkv_inputs(pr, mu, w0, w_up, a0, a_up, g_up, k_k, k_a):
    bsz, seq, _ = pr.shape
    prev, nxt = neighbours(pr, (1, seq))
    pr = pr + mu[0] * (prev - pr) + mu[1] * (nxt - pr)
    o1, o2, o3 = D_RWKV, 2 * D_RWKV, 3 * D_RWKV
    o4 = o3 + 2 * D_DECAY_LORA
    o5 = o4 + 2 * D_AAA_LORA
    r, k, v = pr[..., :o1], pr[..., o1:o2], pr[..., o2:o3]
    lw = pr[..., o3:o4].reshape(bsz, seq, 2, D_DECAY_LORA)
    la = pr[..., o4:o5].reshape(bsz, seq, 2, D_AAA_LORA)
    lg = pr[..., o5:]
    w = w0 + jnp.einsum('bsdr,drc->bsdc', jnp.tanh(lw), w_up)
    w = -jax.nn.softplus(-w.astype(jnp.float32)) - 0.5
    decay = jnp.exp(-jnp.exp(w))
    a = jax.nn.sigmoid(a0 + jnp.einsum('bsdr,drc->bsdc', la, a_up))
    kk = to_heads((k * k_k).astype(jnp.float32))
    kk = kk * lax.rsqrt(jnp.maximum(jnp.sum(kk * kk, axis=-1, keepdims=True), 1e-24))
    kd = k[:, :, None, :] * (1.0 + (a - 1.0) * k_a)
    b = kk[:, :, None] * to_heads(a)
    g = jax.nn.sigmoid(lg) @ g_up
    return (to_heads(r), to_heads(v), kk, to_heads(decay), to_heads(kd), b, g)


def wkv_scan(state, r, v, kk, decay, kd, b, reverse):
    tm = lambda t: jnp.swapaxes(t.astype(jnp.float32), 0, 1)

    def step(S, inp):
        r_t, v_t, kk_t, w_t, k_t, b_t = inp
        sa = jnp.einsum('bhvk,bhk->bhv', S, kk_t)
        S = S * w_t[:, :, None, :] - sa[..., None] * b_t[:, :, None, :] + v_t[..., None] * k_t[:, :, None, :]
        return S, jnp.einsum('bhvk,bhk->bhv', S, r_t)

    state, ys = lax.scan(step, state, (tm(r), tm(v), tm(kk), tm(decay), tm(kd), tm(b)), reverse=reverse)
    return state, jnp.swapaxes(ys, 0, 1)


def rwkv_output(y, r, v, kd, r_k, gn_w, gn_b, g):
    bsz, seq = y.shape[:2]
    mean = jnp.mean(y, axis=-1, keepdims=True)
    yc = y - mean
    yn = yc * lax.rsqrt(jnp.mean(yc * yc, axis=-1, keepdims=True) + GN_EPS)
    yn = yn.reshape(bsz, seq, D_RWKV) * gn_w + gn_b
    coef = jnp.sum(r[:, :, None].astype(jnp.float32) * kd.astype(jnp.float32) * r_k, axis=(2, 4))
    bonus = (coef[..., None] * v).reshape(bsz, seq, D_RWKV)
    return ((yn + bonus) * g).astype(g.dtype)


def even_mixer(h, hc, lat_grid, ctx_grid, want_ctx, w_in, w_out, mu, w0, w_up, a0, a_up, g_up,
               k_k, k_a, r_k, gn_w, gn_b, pool_w, pool_scale):
    p = h @ w_in
    pc = hc @ w_in
    lat = rwkv_inputs(p[..., :D_RWKV_PROJ], mu, w0, w_up, a0, a_up, g_up, k_k, k_a)
    cx = rwkv_inputs(pc[..., :D_RWKV_PROJ], mu, w0, w_up, a0, a_up, g_up, k_k, k_a)
    zero = jnp.zeros((h.shape[0], N_RWKV_HEADS, HEAD_DIM, HEAD_DIM), jnp.float32)
    ys_l, ys_c = [], []
    for d in range(2):
        s_c, y_c = wkv_scan(zero, cx[0], cx[1], cx[2], cx[3][:, :, d], cx[4][:, :, d], cx[5][:, :, d], d == 1)
        _, y_l = wkv_scan(s_c, lat[0], lat[1], lat[2], lat[3][:, :, d], lat[4][:, :, d], lat[5][:, :, d], d == 1)
        ys_l.append(y_l)
        ys_c.append(y_c)

    def merge(inp, ys, pp, grid):
        r, v, _, _, kd, _, g = inp
        y_rwkv = rwkv_output(ys[0] + ys[1], r, v, kd, r_k, gn_w, gn_b, g)
        y_pool = multiscale_pool(pp[..., D_RWKV_PROJ:], pool_w, pool_scale, grid)
        return jnp.concatenate([y_rwkv, y_pool], axis=-1) @ w_out

    y = merge(lat, ys_l, p, lat_grid)
    y_ctx = merge(cx, ys_c, pc, ctx_grid) if want_ctx else None
    return y, y_ctx


def conv_mixer(h, w_in, conv_w, w_out, grid):
    p = h @ w_in
    bg, cg, u = jnp.split(p, 3, axis=-1)
    return (bg * dwconv3(cg * u, conv_w, grid)) @ w_out


def conv_ffn(h, w_up, conv_w, w_down, grid):
    p = h @ w_up
    return (jax.nn.silu(dwconv3(p[..., :D_FF], conv_w, grid)) * p[..., D_FF:]) @ w_down


def setup_inputs(seed: int = 0) -> dict:
    key = jax.random.key(seed)
    ks = iter(jax.random.split(key, 40))
    f32 = jnp.float32
    nrm = lambda shape, s: jax.random.normal(next(ks), shape, f32) * s
    uni = lambda shape, lo, hi: jax.random.uniform(next(ks), shape, f32, lo, hi)
    D = D_MODEL
    ne = (DEPTH + 1) // 2
    no = DEPTH // 2
    return {
        'x': nrm((BATCH, SEQ, D), 1.0),
        'c': nrm((BATCH, D), 1.0),
        'ctx': nrm((BATCH, CTX_LEN, D), 1.0),
        'c_ctx': nrm((D,), 1.0),
        'w_mod': nrm((DEPTH, D, 6 * D), D ** -0.5),
        'b_mod': nrm((DEPTH, 6 * D), 0.02),
        'norm_g': 1.0 + nrm((DEPTH, 4, D), 0.05),
        'ffn_w_up': nrm((DEPTH, D, 2 * D_FF), D ** -0.5),
        'ffn_conv': nrm((DEPTH, 3, D_FF), 3 ** -0.5),
        'ffn_w_down': nrm((DEPTH, D_FF, D), D_FF ** -0.5),
        'ev_w_in': nrm((ne, D, D_EVEN_IN), D ** -0.5),
        'ev_w_out': nrm((ne, D, D), D ** -0.5),
        'ev_mu': uni((ne, 2, D_RWKV_PROJ), 0.0, 0.5),
        'ev_w0': uni((ne, 2, D_RWKV), -4.0, 0.0),
        'ev_w_up': nrm((ne, 2, D_DECAY_LORA, D_RWKV), 0.5 * D_DECAY_LORA ** -0.5),
        'ev_a0': nrm((ne, 2, D_RWKV), 0.5),
        'ev_a_up': nrm((ne, 2, D_AAA_LORA, D_RWKV), D_AAA_LORA ** -0.5),
        'ev_g_up': nrm((ne, D_GATE_LORA, D_RWKV), D_GATE_LORA ** -0.5),
        'ev_k_k': 0.85 + nrm((ne, D_RWKV), 0.05),
        'ev_k_a': 1.0 + nrm((ne, D_RWKV), 0.05),
        'ev_r_k': nrm((ne, N_RWKV_HEADS, HEAD_DIM), 0.1),
        'ev_gn_w': 1.0 + nrm((ne, D_RWKV), 0.05),
        'ev_gn_b': nrm((ne, D_RWKV), 0.02),
        'ev_pool_w': nrm((ne, len(POOL_WINDOWS), POOL_GROUP, POOL_GROUP), POOL_GROUP ** -0.5),
        'ev_pool_scale': 1.0 + nrm((ne, D_POOL), 0.05),
        'od_w_in': nrm((no, D, 3 * D), D ** -0.5),
        'od_conv': nrm((no, 3, D), 3 ** -0.5),
        'od_w_out': nrm((no, D, D), D ** -0.5),
    }


def reference(x, c, ctx, c_ctx, w_mod, b_mod, norm_g, ffn_w_up, ffn_conv, ffn_w_down,
              ev_w_in, ev_w_out, ev_mu, ev_w0, ev_w_up, ev_a0, ev_a_up, ev_g_up, ev_k_k, ev_k_a,
              ev_r_k, ev_gn_w, ev_gn_b, ev_pool_w, ev_pool_scale, od_w_in, od_conv, od_w_out):
    rows = x.shape[1] // GRID_W
    lat_grid = (rows, GRID_W)
    ctx_grid = (1, ctx.shape[1])
    for layer in range(DEPTH):
        i = layer // 2
        even = layer % 2 == 0
        ctx_later = any(j % 2 == 0 for j in range(layer + 1, DEPTH))
        mod = [m[:, None, :] for m in jnp.split(jax.nn.silu(c) @ w_mod[layer] + b_mod[layer], 6, axis=-1)]
        gn = norm_g[layer]
        if even or ctx_later:
            mod_c = jnp.split(jax.nn.silu(c_ctx) @ w_mod[layer] + b_mod[layer], 6, axis=-1)
            hc = modulate(rmsnorm(ctx, gn[0]), mod_c[0], mod_c[1])
        h = modulate(rmsnorm(x, gn[0]), mod[0], mod[1])
        if even:
            y, y_ctx = even_mixer(h, hc, lat_grid, ctx_grid, ctx_later, ev_w_in[i], ev_w_out[i], ev_mu[i],
                                  ev_w0[i], ev_w_up[i], ev_a0[i], ev_a_up[i], ev_g_up[i], ev_k_k[i],
                                  ev_k_a[i], ev_r_k[i], ev_gn_w[i], ev_gn_b[i], ev_pool_w[i], ev_pool_scale[i])
        else:
            y = conv_mixer(h, od_w_in[i], od_conv[i], od_w_out[i], lat_grid)
            y_ctx = conv_mixer(hc, od_w_in[i], od_conv[i], od_w_out[i], ctx_grid) if ctx_later else None
        x = x + mod[2] * rmsnorm(y, gn[1])
        h = modulate(rmsnorm(x, gn[2]), mod[3], mod[4])
        x = x + mod[5] * rmsnorm(conv_ffn(h, ffn_w_up[layer], ffn_conv[layer], ffn_w_down[layer], lat_grid), gn[3])
        if ctx_later:
            ctx = ctx + mod_c[2] * rmsnorm(y_ctx, gn[1])
            hc = modulate(rmsnorm(ctx, gn[2]), mod_c[3], mod_c[4])
            ctx = ctx + mod_c[5] * rmsnorm(conv_ffn(hc, ffn_w_up[layer], ffn_conv[layer], ffn_w_down[layer], ctx_grid), gn[3])
    return x
```

```python
import os
import numpy as np
import concourse.bass as bass
import concourse.mybir as mybir
from concourse.bass_utils import run_bass_kernel_spmd

F32 = mybir.dt.float32
BF16 = mybir.dt.bfloat16
AF = mybir.ActivationFunctionType
ALU = mybir.AluOpType

D = 1024
SEQ = 4096
CTX = 256
TALL = SEQ + CTX
SEG = 256
NSEG = TALL // SEG
CH = 64
NCH = TALL // CH
DFF = 2816
NFF = DFF // 128
DEPTH = 4
RMS_EPS = 1e-6
GN_EPS = 64e-5
EM05 = float(np.exp(-0.5))
KLIM = int(os.environ.get('KLIM', '99'))

PGRP = [(i * 128, 128) for i in range(12)] + [(1536, 64), (1600, 64), (1664, 64), (1728, 96)]

CO = {}
def _build_consts():
    cols = []
    def add(name, arr):
        CO[name] = sum(a.shape[1] for a in cols)
        cols.append(arr.astype(np.float32))
    p = np.arange(128)[:, None]
    j = np.arange(128)[None, :]
    add("ident", (p == j))
    add("bd", ((p // 64) == (j // 64)))
    add("ones", np.ones((128, 128)))
    add("restart", np.tile(((np.arange(256) % 64) != 0)[None, :], (128, 1)))
    def inv(row_len):
        pos = np.arange(row_len)
        out = []
        for win in (2, 4, 8, 16):
            lo = np.clip(pos - win // 2, 0, row_len)
            hi = np.clip(pos + win // 2, 0, row_len)
            out.append(1.0 / (hi - lo))
        return np.tile(np.concatenate(out)[None, :], (128, 1))
    add("inv_lat", inv(64))
    add("inv_ctx", inv(256))
    s = np.arange(128)[:, None] % 64
    t = np.arange(64)[None, :]
    add("m1f", np.concatenate([(s < t), (s <= t)], 1))
    add("m1b", np.concatenate([(s > t), (s >= t)], 1))
    add("m3f", (t < s))
    add("m3b", (t > s))
    add("id64", (s == t))
    return np.concatenate(cols, 1)
CONSTS = _build_consts()
NCONST = CONSTS.shape[1]

PV = {}
def _pv_layout():
    n = 0
    def add(name, w):
        nonlocal n
        PV[name] = n
        n += w
    add("c", 8); add("cctx", 8)
    for L in range(DEPTH):
        for jn in range(4):
            add(f"g{L}_{jn}", 8)
        add(f"bmod{L}", 48)
        for tp in range(3):
            add(f"fconv{L}_{tp}", NFF)
        if L % 2 == 0:
            add(f"mu{L}_0", 16); add(f"mu{L}_1", 16)
            for d in range(2):
                add(f"w0{L}_{d}", 4); add(f"a0{L}_{d}", 4)
            for nm in ("kk", "ka", "rk", "gnw", "gnb", "psc"):
                add(f"{nm}{L}", 4)
        else:
            for tp in range(3):
                add(f"oconv{L}_{tp}", 8)
    return n
NPV = _pv_layout()

def _colmajor(v):
    return np.ascontiguousarray(np.asarray(v, np.float32).reshape(-1, 128).T)

def build_pvec(inp, b):
    t = np.zeros((128, NPV), np.float32)
    def put(name, arr):
        t[:, PV[name]:PV[name] + arr.shape[1]] = arr
    put("c", _colmajor(inp["c"][b])); put("cctx", _colmajor(inp["c_ctx"]))
    for L in range(DEPTH):
        i = L // 2
        for jn in range(4):
            put(f"g{L}_{jn}", _colmajor(inp["norm_g"][L, jn]))
        put(f"bmod{L}", _colmajor(inp["b_mod"][L]))
        for tp in range(3):
            put(f"fconv{L}_{tp}", _colmajor(inp["ffn_conv"][L, tp]))
        if L % 2 == 0:
            for m in range(2):
                a = np.zeros((128, 16), np.float32)
                for gi, (st, sz) in enumerate(PGRP):
                    a[:sz, gi] = inp["ev_mu"][i, m, st:st + sz]
                put(f"mu{L}_{m}", a)
            for d in range(2):
                put(f"w0{L}_{d}", _colmajor(inp["ev_w0"][i, d]))
                put(f"a0{L}_{d}", _colmajor(inp["ev_a0"][i, d]))
            put(f"kk{L}", _colmajor(inp["ev_k_k"][i])); put(f"ka{L}", _colmajor(inp["ev_k_a"][i]))
            put(f"rk{L}", _colmajor(inp["ev_r_k"][i].reshape(-1)))
            put(f"gnw{L}", _colmajor(inp["ev_gn_w"][i])); put(f"gnb{L}", _colmajor(inp["ev_gn_b"][i]))
            put(f"psc{L}", _colmajor(inp["ev_pool_scale"][i]))
        else:
            for tp in range(3):
                put(f"oconv{L}_{tp}", _colmajor(inp["od_conv"][i, tp]))
    return t

class Buf:
    __slots__ = ("name", "lw", "rd", "dsem", "dcnt", "rng", "al")
    def __init__(self, name, rng=None):
        self.name = name; self.lw = None; self.rd = []; self.dsem = None; self.dcnt = 0
        self.rng = rng; self.al = []

class Sched:
    ENGS = ("pe", "act", "dve", "pool", "sp")
    def __init__(self, nc):
        self.nc = nc
        self.ops = {e: [] for e in self.ENGS}
        self.cnt = {e: 0 for e in self.ENGS}
        self.waited = {e: {} for e in self.ENGS}
        self.sems = {e: nc.alloc_semaphore("s_" + e) for e in self.ENGS}
        self.key = {e: e for e in self.ENGS}
        self.epoch = {e: 0 for e in self.ENGS}
        self.nsem = 0
        self.sb = []
    def sbuf(self, name, lo, hi):
        b = Buf(name, (lo, hi))
        for o in self.sb:
            if o.rng[0] < hi and lo < o.rng[1]:
                o.al.append(b); b.al.append(o)
        self.sb.append(b)
        return b
    def _waits(self, eng, evs, pe_self=False):
        w = self.waited[eng]; best = {}
        for ev in evs:
            if ev is None:
                continue
            k, v = ev
            if pe_self and k.startswith("pe"):
                continue
            if w.get(k, 0) >= v:
                continue
            if best.get(k, 0) < v:
                best[k] = v
        for k, v in best.items():
            w[k] = v
        return list(best.items())
    def _gather(self, reads, writes):
        evs = []
        for b in reads:
            evs.append(b.lw)
            for o in b.al:
                evs.append(o.lw)
        for b in writes:
            evs.append(b.lw); evs.extend(b.rd)
            for o in b.al:
                evs.append(o.lw); evs.extend(o.rd)
        return evs
    def op(self, eng, fn, reads=(), writes=()):
        waits = self._waits(eng, self._gather(reads, writes), pe_self=(eng == "pe"))
        if self.cnt[eng] >= 12000:
            self.epoch[eng] += 1
            self.cnt[eng] = 0
            self.key[eng] = "%s#%d" % (eng, self.epoch[eng])
            self.sems[self.key[eng]] = self.nc.alloc_semaphore("s_%s_%d" % (eng, self.epoch[eng]))
        self.cnt[eng] += 1
        ev = (self.key[eng], self.cnt[eng])
        for b in reads:
            b.rd.append(ev)
        for b in writes:
            b.lw = ev; b.rd = []
        self.ops[eng].append((fn, waits, (self.key[eng], 1)))
    def dma(self, q, fn, reads=(), writes=(), n=1):
        (dst,) = writes
        if dst.dsem is None:
            key = "d%d" % self.nsem
            self.nsem += 1
            self.sems[key] = self.nc.alloc_semaphore(key)
            dst.dsem = key
        waits = self._waits(q, self._gather(reads, writes))
        dst.dcnt += 16 * n
        ev = (dst.dsem, dst.dcnt)
        for b in reads:
            b.rd.append(ev)
        dst.lw = ev; dst.rd = []
        self.ops[q].append((fn, waits, (dst.dsem, 16)))
    def finish(self, eng, bufs):
        self.ops[eng].append((None, self._waits(eng, [b.lw for b in bufs]), None))
    def emit(self):
        nc = self.nc; sems = self.sems
        with nc.Block() as block:
            def run(name, e):
                for fn, waits, inc in self.ops[name]:
                    for k, v in waits:
                        e.wait_ge(sems[k], v)
                    if fn is None:
                        continue
                    r = fn(e)
                    if isinstance(r, (list, tuple)):
                        for ins in r:
                            ins.then_inc(sems[inc[0]], inc[1])
                    else:
                        r.then_inc(sems[inc[0]], inc[1])
            @block.tensor
            def _(e): run("pe", e)
            @block.scalar
            def _(e): run("act", e)
            @block.vector
            def _(e): run("dve", e)
            @block.gpsimd
            def _(e): run("pool", e)
            @block.sync
            def _(e): run("sp", e)

_DS = {F32: 4, BF16: 2}

class T:
    __slots__ = ("ap", "b")
    def __init__(self, ap, b):
        self.ap = ap; self.b = b

def build_program(debug_stop=None, debug_outs=()):
    nc = bass.Bass("TRN2", target_bir_lowering=False)
    S = Sched(nc)
    dram = {}
    def din(name, shape, dt=F32):
        dram[name] = nc.dram_tensor(name, list(shape), dt, kind="ExternalInput").ap()
        return dram[name]
    x_in = din("x", [SEQ, D]); ctx_in = din("ctx", [CTX, D])
    consts_in = din("consts", [128, NCONST]); pvec_in = din("pvec", [128, NPV])
    w_mod = din("w_mod", [DEPTH, D, 6 * D]); ffn_w_up = din("ffn_w_up", [DEPTH, D, 2 * DFF])
    ffn_w_down = din("ffn_w_down", [DEPTH, DFF, D])
    ev_w_in = din("ev_w_in", [2, D, 2336]); ev_w_out = din("ev_w_out", [2, D, D])
    ev_w_up = din("ev_w_up", [2, 2, 32, 512]); ev_a_up = din("ev_a_up", [2, 2, 64, 512])
    ev_g_up = din("ev_g_up", [2, 96, 512]); ev_pool_w = din("ev_pool_w", [2, 4, 128, 128])
    od_w_in = din("od_w_in", [2, D, 3 * D]); od_w_out = din("od_w_out", [2, D, D])
    out_d = nc.dram_tensor("out", [SEQ, D], F32, kind="ExternalOutput").ap()
    IN = Buf("inputs")
    OUTB = T(None, Buf("out"))
    scr_list = []
    def dscr(name, shape, dt):
        t = T(nc.dram_tensor(name, list(shape), dt, kind=("ExternalOutput" if name in debug_outs else "Internal")).ap(), Buf(name))
        scr_list.append(t)
        return t
    XS = dscr("XS", [NSEG, 128, 8 * SEG], F32)
    PR = dscr("PR", [NSEG, 128, 16 * SEG], F32)
    YP = dscr("YP", [NSEG, 128, 4 * SEG], BF16)
    FM = [dscr(f"FM{d}", [NSEG, 128, 4 * 1024], BF16) for d in range(2)]
    TMD = dscr("TMD", [TALL, 5, 512], BF16)
    GC = [dscr(f"GC{d}", [NSEG, 128, 16], F32) for d in range(2)]
    BGD = dscr("BGD", [NSEG, 128, 2 * 4 * SEG], BF16)
    YD = [dscr(f"YD{d}", [NSEG, 128, 4 * SEG], F32) for d in range(2)]

    cur = [0]
    def setoff(o):
        cur[0] = o
    tcache = {}
    def tile(shape, dt, name):
        nbytes = int(np.prod(shape[1:])) * _DS[dt]
        nbytes = (nbytes + 31) // 32 * 32
        off = cur[0]
        cur[0] += nbytes
        assert cur[0] <= 229376, (name, cur[0])
        if name in tcache:
            assert tcache[name][1] == off, name
            return tcache[name][0]
        h = nc.alloc_sbuf_tensor_at(name, list(shape), dt, offset=off)
        t = T(h.ap(), S.sbuf(name, off, off + nbytes))
        tcache[name] = (t, off)
        return t
    banks = []
    for i in range(7):
        banks.append(T(nc.alloc_psum_tensor(f"bank{i}", [128, 512], F32).ap(), Buf(f"bank{i}")))
    bankb_ap = nc.alloc_psum_tensor("bankb", [128, 1024], BF16).ap()
    _bb = Buf("bankb")
    bankb = [T(bankb_ap[:, 0:512], _bb), T(bankb_ap[:, 512:1024], _bb)]
    bki = [0, 0]
    def bank():
        bki[0] = (bki[0] + 1) % 7
        return banks[bki[0]]
    def bankbf():
        bki[1] = (bki[1] + 1) % 2
        return bankb[bki[1]]

    def bl(ts):
        return [t.b for t in ts]
    def TTo(eng, out, a, b, op, r, w):
        S.op(eng, lambda e: e.tensor_tensor(out=out, in0=a, in1=b, op=op), reads=bl(r), writes=bl(w))
    def STT(eng, out, in0, scalar, in1, op0, op1, r, w):
        S.op(eng, lambda e: e.scalar_tensor_tensor(out=out, in0=in0, scalar=scalar, in1=in1, op0=op0, op1=op1),
             reads=bl(r), writes=bl(w))
    def TS(eng, out, in0, s1, s2, op0, op1, r, w):
        if op1 is None:
            S.op(eng, lambda e: e.tensor_scalar(out=out, in0=in0, scalar1=s1, scalar2=None, op0=op0),
                 reads=bl(r), writes=bl(w))
        else:
            S.op(eng, lambda e: e.tensor_scalar(out=out, in0=in0, scalar1=s1, scalar2=s2, op0=op0, op1=op1),
                 reads=bl(r), writes=bl(w))
    def ACT(out, in_, func, r, w, scale=None, bias=None):
        kw = {}
        if scale is not None:
            kw["scale"] = scale
        if bias is not None:
            kw["bias"] = bias
        S.op("act", lambda e: e.activation(out=out, in_=in_, func=func, **kw), reads=bl(r), writes=bl(w))
    def CP(eng, out, in_, r, w):
        if eng == "act":
            S.op("act", lambda e: e.copy(out=out, in_=in_), reads=bl(r), writes=bl(w))
        else:
            S.op(eng, lambda e: e.tensor_copy(out=out, in_=in_), reads=bl(r), writes=bl(w))
    def RECIP(out, in_, r, w):
        S.op("dve", lambda e: e.reciprocal(out=out, in_=in_), reads=bl(r), writes=bl(w))
    def MM(out, pairs, r, w):
        def fn(e):
            ins = None
            n = len(pairs)
            for i, (l, rh) in enumerate(pairs):
                ins = e.matmul(out, lhsT=l, rhs=rh, start=(i == 0), stop=(i == n - 1))
            return ins
        S.op("pe", fn, reads=bl(r), writes=bl(w))
    def MMS(items, r, w):
        def fn(e):
            ins = None
            for out, pairs in items:
                n = len(pairs)
                for i, (l, rh) in enumerate(pairs):
                    ins = e.matmul(out, lhsT=l, rhs=rh, start=(i == 0), stop=(i == n - 1))
            return ins
        S.op("pe", fn, reads=bl(r), writes=bl(w))
    def TRS(items, ident, r, w):
        def fn(e):
            ins = None
            for out, in_ in items:
                ins = e.transpose(out, in_, ident)
            return ins
        S.op("pe", fn, reads=bl(r), writes=bl(w))
    def DMA(q, out, in_, r, w, **kw):
        S.dma(q, lambda e: e.dma_start(out=out, in_=in_, **kw), reads=bl(r), writes=bl(w))
    def DMAS(q, pairs, r, w):
        S.dma(q, lambda e: [e.dma_start(out=o, in_=i) for (o, i) in pairs], reads=bl(r), writes=bl(w), n=len(pairs))
    TIN = T(None, IN)

    setoff(16640)
    cst = tile([128, NCONST], F32, "cst")
    pv = tile([128, NPV], F32, "pv")
    identb = tile([128, 128], BF16, "identb")
    bdb = tile([128, 128], BF16, "bdb")
    onesb = tile([128, 128], BF16, "onesb")
    modt = tile([128, DEPTH, 48, 2], F32, "modt")
    HALO = tile([128, 16, NSEG, 2], F32, "HALO")
    mdv = tile([128, DEPTH, 6, 8, 2], F32, "mdv")
    dvec = tile([128, 2, 48], F32, "dvec")
    silc = tile([128, 8, 2], F32, "silc")
    wupb = tile([64, 512], BF16, "wupb")
    aupb = [tile([64, 512], BF16, f"aupb{d}") for d in range(2)]
    gupb = tile([96, 512], BF16, "gupb")
    poolwb = tile([128, 4, 128], BF16, "poolwb")
    CONST_END = cur[0]
    W_BASE = (CONST_END + 63) // 64 * 64
    WA_BYTES = 8 * 5632 * 2
    WB_BYTES = NFF * 1024 * 2
    setoff(W_BASE)
    WA = tile([128, 8, 5632], BF16, "WA")
    WB = tile([128, NFF, 1024], BF16, "WB")
    A_BASE = cur[0]

    def cc(name, n=128):
        return cst.ap[:, CO[name]:CO[name] + n]
    def pvc(name, j, parts=128):
        return pv.ap[0:parts, PV[name] + j:PV[name] + j + 1]

    DMA("sp", cst.ap, consts_in, [TIN], [cst])
    DMA("sp", pv.ap, pvec_in, [TIN], [pv])
    CP("dve", identb.ap, cc("ident"), [cst], [identb])
    CP("dve", bdb.ap, cc("bd"), [cst], [bdb])
    CP("dve", onesb.ap, cc("ones"), [cst], [onesb])
    ident = cc("ident")

    setoff(W_BASE)
    wm = [tile([128, 8, 768], F32, f"wm{i}") for i in range(2)]
    ACT(silc.ap[:, :, 0], pv.ap[:, PV["c"]:PV["c"] + 8], AF.Silu, [pv], [silc])
    ACT(silc.ap[:, :, 1], pv.ap[:, PV["cctx"]:PV["cctx"] + 8], AF.Silu, [pv], [silc])
    pi = 0
    for L in range(1 if debug_stop in ("mod1", "prep_only", "scan_only", "mini") else (2 if debug_stop in ("mini2", "testB") else (1 if debug_stop == "testC" else DEPTH))):
        wv = w_mod[L].rearrange("(k p) n -> p k n", p=128)
        bk = bank()
        for pc in range(8):
            wt = wm[pi % 2]; pi += 1
            DMA("sp", wt.ap, wv[:, :, pc * 768:(pc + 1) * 768], [TIN], [wt])
            items = []
            for j in range(6):
                nchunk = pc * 6 + j
                items.append((bk.ap[:, nchunk * 2:nchunk * 2 + 2],
                              [(wt.ap[:, k, j * 128:(j + 1) * 128], silc.ap[:, k, :]) for k in range(8)]))
            MMS(items, [wt, silc], [bk])
        bm = pv.ap[:, PV[f"bmod{L}"]:PV[f"bmod{L}"] + 48].unsqueeze(2).to_broadcast([128, 48, 2])
        TTo("dve", modt.ap[:, L], bk.ap[:, 0:96].rearrange("p (n c) -> p n c", c=2), bm, ALU.add, [bk, pv], [modt])
        def gb(jn):
            return pv.ap[:, PV[f"g{L}_{jn}"]:PV[f"g{L}_{jn}"] + 8].unsqueeze(2).to_broadcast([128, 8, 2])
        def mo(k):
            return modt.ap[:, L, k * 8:(k + 1) * 8, :]
        for (dst, g, sc) in ((0, 0, 1), (3, 2, 4)):
            STT("dve", mdv.ap[:, L, dst], mo(sc), 1.0, gb(g), ALU.add, ALU.mult, [modt, pv], [mdv])
        for (dst, src) in ((1, 0), (4, 3)):
            CP("dve", mdv.ap[:, L, dst], mo(src), [modt], [mdv])
        for (dst, g, gt) in ((2, 1, 2), (5, 3, 5)):
            TTo("dve", mdv.ap[:, L, dst], mo(gt), gb(g), ALU.mult, [modt, pv], [mdv])
    def MV(L, kind, fc, col):
        return mdv.ap[:, L, kind, fc, col:col + 1]

    setoff(A_BASE)
    XT = [tile([128, 8, SEG], F32, f"XT{i}") for i in range(2)]
    Hb = tile([128, 8, SEG], BF16, "Hb")
    SQ = tile([128, 8, SEG], BF16, "SQ")
    Yt = tile([128, 8, SEG], F32, "Yt")
    ACTF = tile([128, NFF, SEG], BF16, "ACTF")
    RSTD = tile([128, SEG], F32, "RSTD")
    TMP = [tile([128, SEG], F32, f"TMP{i}") for i in range(6)]
    ACT_END = cur[0]
    tmpi = [0]
    def tmp():
        tmpi[0] = (tmpi[0] + 1) % 6
        return TMP[tmpi[0]]

    def segcol(s):
        return s * SEG

    def rms_rstd(src, srcbufs, nchunks, lhs_ones, scale, eps, dstr):
        bk = bank()
        for fc in range(nchunks):
            ACT(SQ.ap[:, fc, :], src[:, fc, :], AF.Square, srcbufs, [SQ])
        MM(bk.ap[:, 0:SEG], [(lhs_ones, SQ.ap[:, fc, :]) for fc in range(nchunks)], [SQ, onesb, bdb], [bk])
        t1 = tmp()
        TS("dve", t1.ap, bk.ap[:, 0:SEG], scale, eps, ALU.mult, ALU.add, [bk], [t1])
        ACT(t1.ap, t1.ap, AF.Sqrt, [t1], [t1])
        RECIP(dstr.ap, t1.ap, [t1], [dstr])

    def norm_mod(xt, L, ka, kb, col):
        rms_rstd(xt.ap, [xt], 8, onesb.ap, 1.0 / D, RMS_EPS, RSTD)
        for fc in range(8):
            t1 = tmp()
            TTo("dve", t1.ap, xt.ap[:, fc, :], RSTD.ap, ALU.mult, [xt, RSTD], [t1])
            ACT(Hb.ap[:, fc, :], t1.ap, AF.Identity, [t1, mdv], [Hb], scale=MV(L, ka, fc, col), bias=MV(L, kb, fc, col))

    def residual(xt, L, kg, col):
        rms_rstd(Yt.ap, [Yt], 8, onesb.ap, 1.0 / D, RMS_EPS, RSTD)
        for fc in range(8):
            t1 = tmp()
            STT("dve", t1.ap, Yt.ap[:, fc, :], MV(L, kg, fc, col), RSTD.ap, ALU.mult, ALU.mult, [Yt, mdv, RSTD], [t1])
            TTo("pool", xt.ap[:, fc, :], xt.ap[:, fc, :], t1.ap, ALU.add, [xt, t1], [xt])

    def load_w(dst, dview, ncols, kch):
        c0 = 0
        while c0 < ncols:
            c1 = min(ncols, c0 + 2048)
            for k0 in range(0, kch, 4):
                k1 = min(kch, k0 + 4)
                DMA("pool", dst.ap[:, k0:k1, c0:c1], dview[:, k0:k1, c0:c1], [TIN], [dst])
            c0 = c1

    def load_x(xt, s):
        DMA("sp", xt.ap.rearrange("p k t -> p (k t)"), XS.ap[s], [XS], [xt])
    def store_x(xt, s):
        DMA("sp", XS.ap[s], xt.ap.rearrange("p k t -> p (k t)"), [xt], [XS])

    def conv3(dst, src, srcbufs, src_is_psum, w0, w1, w2, nrows, rl):
        ACT(dst.ap, src, AF.Identity, srcbufs, [dst], scale=w1)
        dv = dst.ap.rearrange("p (r c) -> p r c", c=rl)
        sv = src.rearrange("p (r c) -> p r c", c=rl)
        STT("dve", dv[:, :, 1:rl], sv[:, :, 0:rl - 1], w0, dv[:, :, 1:rl], ALU.mult, ALU.add, srcbufs + [dst], [dst])
        STT("dve", dv[:, :, 0:rl - 1], sv[:, :, 1:rl], w2, dv[:, :, 0:rl - 1], ALU.mult, ALU.add, srcbufs + [dst], [dst])

    def grid(s):
        return (1, 256) if s == 0 else (4, 64)

    def pass_E1(L, segs):
        i = L // 2
        load_w(WA, ev_w_in[i].rearrange("(k p) n -> p k n", p=128), 2336, 8)
        DMA("pool", poolwb.ap, ev_pool_w[i].rearrange("g c d -> c g d"), [TIN], [poolwb])
        setoff(W_BASE + WA_BYTES)
        TM = tile([128, 2, D], F32, "E1_TM")
        PRS = tile([128, 16, SEG], F32, "E1_PRS")
        XP = tile([128, 4, 384], F32, "E1_XP")
        S2 = tile([128, 384], F32, "E1_S2"); S4 = tile([128, 384], F32, "E1_S4")
        S8 = tile([128, 384], F32, "E1_S8"); S16 = tile([128, 384], F32, "E1_S16")
        DB = tile([128, SEG], BF16, "E1_DB")
        YPT = tile([128, 4, SEG], BF16, "E1_YPT")
        stg_i = 0
        for s in segs:
            if s in (0, 1):
                S.op("pool", lambda e: e.memset(XP.ap, 0.0), writes=[XP.b])
            if s == 0:
                S.op("pool", lambda e: e.memset(PRS.ap, 0.0), writes=[PRS.b])
                S.op("pool", lambda e: e.memset(HALO.ap, 0.0), writes=[HALO.b])
            xt = XT[s % 2]
            col = 0 if s > 0 else 1
            if L == 0:
                src = ctx_in if s == 0 else x_in[(s - 1) * SEG:s * SEG]
                DMA("sp", TM.ap, src.rearrange("(h p) f -> p h f", p=128), [TIN], [TM])
                for fc in range(8):
                    bk = bank()
                    TRS([(bk.ap[:, hh * 128:(hh + 1) * 128], TM.ap[:, hh, fc * 128:(fc + 1) * 128]) for hh in range(2)],
                        ident, [TM, cst], [bk])
                    CP("act" if fc % 2 else "dve", xt.ap[:, fc, :], bk.ap[:, 0:SEG], [bk], [xt])
                store_x(xt, s)
            else:
                load_x(xt, s)
            norm_mod(xt, L, 0, 1, col)
            for gi, (st, sz) in enumerate(PGRP):
                bk = bank()
                MM(bk.ap[0:sz, 0:SEG], [(WA.ap[:, k, st:st + sz], Hb.ap[:, k, :]) for k in range(8)], [WA, Hb], [bk])
                CP("act", PRS.ap[0:sz, gi, :], bk.ap[0:sz, 0:SEG], [bk], [PRS])
            CP("pool", HALO.ap[:, :, s, 0], PRS.ap[:, :, 0], [PRS], [HALO])
            CP("pool", HALO.ap[:, :, s, 1], PRS.ap[:, :, SEG - 1], [PRS], [HALO])
            DMA("sp", PR.ap[s], PRS.ap.rearrange("p g t -> p (g t)"), [PRS], [PR])
            nr, rl = grid(s)
            pw = rl + 32
            invn = "inv_ctx" if s == 0 else "inv_lat"
            for gi in range(4):
                win = (2, 4, 8, 16)[gi]
                bk = bank()
                c0 = 1824 + gi * 128
                MM(bk.ap[:, 0:SEG], [(WA.ap[:, k, c0:c0 + 128], Hb.ap[:, k, :]) for k in range(8)], [WA, Hb], [bk])
                xpv = XP.ap[:, gi, 0:nr * pw].rearrange("p (r c) -> p r c", c=pw)
                CP("act", xpv[:, :, 16:16 + rl], bk.ap[:, 0:SEG].rearrange("p (r c) -> p r c", c=rl), [bk], [XP])
                prev = xpv; prevb = XP; sh = 1
                for (St, wn) in ((S2, 2), (S4, 4), (S8, 8), (S16, 16)):
                    if wn > win:
                        break
                    sv = St.ap[:, 0:nr * pw].rearrange("p (r c) -> p r c", c=pw)
                    lo = wn - 1
                    TTo("pool" if gi % 2 else "dve", sv[:, :, lo:pw], prev[:, :, lo:pw], prev[:, :, lo - sh:pw - sh], ALU.add, [prevb], [St])
                    prev = sv; prevb = St; sh = wn
                o = 16 + win // 2 - 1
                t1 = tmp()
                t1v = t1.ap.rearrange("p (r c) -> p r c", c=rl)
                iv = cst.ap[:, CO[invn] + gi * rl:CO[invn] + (gi + 1) * rl].unsqueeze(1).to_broadcast([128, nr, rl])
                TTo("dve", t1v, prev[:, :, o:o + rl], iv, ALU.mult, [prevb, cst], [t1])
                TTo("dve", DB.ap.rearrange("p (r c) -> p r c", c=rl), t1v, xpv[:, :, 16:16 + rl], ALU.subtract, [t1, XP], [DB])
                bk2 = bank()
                MM(bk2.ap[:, 0:SEG], [(poolwb.ap[:, gi, :], DB.ap)], [poolwb, DB], [bk2])
                ACT(YPT.ap[:, gi, :], bk2.ap[:, 0:SEG], AF.Identity, [bk2, pv], [YPT], scale=pvc(f"psc{L}", gi))
            DMA("sp", YP.ap[s], YPT.ap.rearrange("p k t -> p (k t)"), [YPT], [YP])

    def pass_prep(L, segs):
        i = L // 2
        DMA("pool", wupb.ap, ev_w_up[i].rearrange("d r c -> (d r) c"), [TIN], [wupb])
        for d in range(2):
            DMA("pool", aupb[d].ap, ev_a_up[i, d], [TIN], [aupb[d]])
        DMA("pool", gupb.ap, ev_g_up[i], [TIN], [gupb])
        m0 = pv.ap[:, PV[f"mu{L}_0"]:PV[f"mu{L}_0"] + 16]
        m1 = pv.ap[:, PV[f"mu{L}_1"]:PV[f"mu{L}_1"] + 16]
        dv = dvec.ap[:, i]
        TTo("dve", dv[:, 0:16], m0, m1, ALU.add, [pv], [dvec])
        TS("dve", dv[:, 0:16], dv[:, 0:16], -1.0, 1.0, ALU.mult, ALU.add, [dvec], [dvec])
        TS("dve", dv[:, 16:20], pv.ap[:, PV[f"ka{L}"]:PV[f"ka{L}"] + 4], -1.0, 1.0, ALU.mult, ALU.add, [pv], [dvec])
        setoff(W_BASE)
        PRM = [tile([128, 16, SEG], F32, f"P_PRM{k}") for k in range(2)]
        PRT1 = tile([128, 16, SEG + 2], F32, "P_PRT")
        SH = tile([128, 16, SEG], F32, "P_SH")
        KKt = tile([128, 4, SEG], F32, "P_KK")
        At = [tile([128, 4, SEG], F32, f"P_A{d}") for d in range(2)]
        KDt = [tile([128, 4, SEG], F32, f"P_KD{d}") for d in range(2)]
        Bt = [tile([128, 4, SEG], F32, f"P_B{d}") for d in range(2)]
        SG = [tile([128, 4, SEG], F32, f"P_SG{d}") for d in range(2)]
        CUM = tile([128, SEG], F32, "P_CUM")
        E = [tile([128, SEG], F32, f"P_E{k}") for k in range(4)]
        TLb = tile([64, SEG], BF16, "P_TLb")
        LAb = [tile([64, SEG], BF16, f"P_LAb{d}") for d in range(2)]
        LGb = tile([96, SEG], BF16, "P_LGb")
        SQb = tile([128, SEG], BF16, "P_SQb")
        FMo = [tile([128, 4, 1024], BF16, f"P_FMo{d}") for d in range(2)]
        KGo = [tile([128, 4, SEG], BF16, f"P_KGo{d}") for d in range(2)]
        BGo = [tile([128, 4, SEG], BF16, f"P_BGo{d}") for d in range(2)]
        Vb = tile([128, 4, SEG], BF16, "P_Vb")
        GCo = [tile([128, 4, 4], F32, f"P_GCo{d}") for d in range(2)]
        BGo2 = tile([128, 2, 4, SEG], BF16, "P_BGo2")
        TMo = [tile([128, 5, 512], BF16, f"P_TMo{k}") for k in range(2)]
        tmo_i = [0]

        def load_pr(s):
            DMA("sp", PRM[s % 2].ap.rearrange("p g t -> p (g t)"), PR.ap[s], [PR], [PRM[s % 2]])

        load_pr(segs[0])
        for si, s in enumerate(segs):
            if si + 1 < len(segs):
                load_pr(segs[si + 1])
            pt = PRT1
            t0 = segcol(s)
            c0ch = t0 // CH
            first = (s == 0 or s == 1); last = (s == 0 or s == NSEG - 1)
            CP("pool", pt.ap[:, :, 1:SEG + 1], PRM[s % 2].ap, [PRM[s % 2]], [pt])
            if first:
                S.op("pool", lambda e: e.memset(pt.ap[:, :, 0:1], 0.0), writes=[pt.b])
            else:
                CP("pool", pt.ap[:, :, 0], HALO.ap[:, :, s - 1, 1], [HALO], [pt])
            if last:
                S.op("pool", lambda e: e.memset(pt.ap[:, :, SEG + 1:SEG + 2], 0.0), writes=[pt.b])
            else:
                CP("pool", pt.ap[:, :, SEG + 1], HALO.ap[:, :, s + 1, 0], [HALO], [pt])
            for gi, (st, sz) in enumerate(PGRP):
                dst = SH.ap[0:sz, gi, :]
                ACT(dst, pt.ap[0:sz, gi, 1:SEG + 1], AF.Identity, [pt, dvec], [SH], scale=dvec.ap[0:sz, i, gi:gi + 1])
                STT("dve", dst, pt.ap[0:sz, gi, 0:SEG], pvc(f"mu{L}_0", gi, sz), dst, ALU.mult, ALU.add, [pt, pv, SH], [SH])
                STT("dve", dst, pt.ap[0:sz, gi, 2:SEG + 2], pvc(f"mu{L}_1", gi, sz), dst, ALU.mult, ALU.add, [pt, pv, SH], [SH])
            if KLIM <= 1:
                continue
            Rv = SH.ap[:, 0:4, :]; Kv = SH.ap[:, 4:8, :]; Vv = SH.ap[:, 8:12, :]
            ACT(TLb.ap, SH.ap[0:64, 12, :], AF.Tanh, [SH], [TLb])
            CP("pool", LAb[0].ap, SH.ap[0:64, 13, :], [SH], [LAb[0]])
            CP("pool", LAb[1].ap, SH.ap[0:64, 14, :], [SH], [LAb[1]])
            ACT(LGb.ap, SH.ap[0:96, 15, :], AF.Sigmoid, [SH], [LGb])
            CP("pool", Vb.ap, Vv, [SH], [Vb])
            for c4 in range(4):
                csl = slice(c4 * 128, (c4 + 1) * 128)
                bk = bank()
                MM(bk.ap[:, 0:SEG], [(gupb.ap[:, csl], LGb.ap)], [gupb, LGb], [bk])
                CP("act", BGo2.ap[:, 1, c4, :], bk.ap[:, 0:SEG], [bk], [BGo2])
                t1 = tmp()
                TS("dve", t1.ap, Kv[:, c4, :], pvc(f"kk{L}", c4), None, ALU.mult, None, [SH, pv], [t1])
                ACT(SQb.ap, t1.ap, AF.Square, [t1], [SQb])
                bk = bank()
                MM(bk.ap[:, 0:SEG], [(bdb.ap, SQb.ap)], [bdb, SQb], [bk])
                t2 = tmp()
                TS("dve", t2.ap, bk.ap[:, 0:SEG], 1e-24, None, ALU.max, None, [bk], [t2])
                ACT(t2.ap, t2.ap, AF.Sqrt, [t2], [t2])
                RECIP(t2.ap, t2.ap, [t2], [t2])
                TTo("dve", KKt.ap[:, c4, :], t1.ap, t2.ap, ALU.mult, [t1, t2], [KKt])
                for d in range(2):
                    bk = bank()
                    MM(bk.ap[:, 0:SEG], [(wupb.ap[32 * d:32 * d + 32, csl], TLb.ap[32 * d:32 * d + 32, :])], [wupb, TLb], [bk])
                    ACT(SG[d].ap[:, c4, :], bk.ap[:, 0:SEG], AF.Sigmoid, [bk, pv], [SG[d]], bias=pvc(f"w0{L}_{d}", c4))
                    bk = bank()
                    MM(bk.ap[:, 0:SEG], [(aupb[d].ap[:, csl], LAb[d].ap)], [aupb[d], LAb[d]], [bk])
                    ACT(At[d].ap[:, c4, :], bk.ap[:, 0:SEG], AF.Sigmoid, [bk, pv], [At[d]], bias=pvc(f"a0{L}_{d}", c4))
                    t3 = tmp()
                    TS("dve", t3.ap, At[d].ap[:, c4, :], pvc(f"ka{L}", c4), dvec.ap[:, i, 16 + c4:17 + c4], ALU.mult, ALU.add,
                       [At[d], pv, dvec], [t3])
                    TTo("pool", KDt[d].ap[:, c4, :], Kv[:, c4, :], t3.ap, ALU.mult, [SH, t3], [KDt[d]])
                    TTo("pool", Bt[d].ap[:, c4, :], KKt.ap[:, c4, :], At[d].ap[:, c4, :], ALU.mult, [KKt, At[d]], [Bt[d]])
                    S.op("dve", lambda e, d=d, c4=c4: e.tensor_tensor_scan(out=CUM.ap, data0=cc("restart", 256), data1=SG[d].ap[:, c4, :],
                                                                            initial=0.0, op0=ALU.mult, op1=ALU.add),
                         reads=[cst.b, SG[d].b], writes=[CUM.b])
                    cumv = CUM.ap.rearrange("p (c t) -> p c t", t=CH)
                    sgv = SG[d].ap[:, c4, :].rearrange("p (c t) -> p c t", t=CH)
                    totb = cumv[:, :, CH - 1:CH].to_broadcast([128, 4, CH])
                    e0, e1, e2, e3 = E
                    if d == 1:
                        t4 = tmp()
                        t4v = t4.ap.rearrange("p (c t) -> p c t", t=CH)
                        TTo("dve", t4v, totb, cumv, ALU.subtract, [CUM], [t4])
                        ACT(GCo[d].ap[:, c4, :], cumv[:, :, CH - 1], AF.Exp, [CUM], [GCo[d]], scale=-EM05)
                        TTo("dve", CUM.ap, t4.ap, SG[d].ap[:, c4, :], ALU.add, [t4, SG[d]], [CUM])
                        t5 = tmp()
                        t5v = t5.ap.rearrange("p (c t) -> p c t", t=CH)
                        TTo("dve", t5v, cumv[:, :, 0:1].to_broadcast([128, 4, CH]), cumv, ALU.subtract, [CUM], [t5])
                        tmc = t5
                    else:
                        ACT(GCo[d].ap[:, c4, :], cumv[:, :, CH - 1], AF.Exp, [CUM], [GCo[d]], scale=-EM05)
                        t5 = tmp()
                        t5v = t5.ap.rearrange("p (c t) -> p c t", t=CH)
                        TTo("dve", t5v, totb, cumv, ALU.subtract, [CUM], [t5])
                        tmc = t5
                    t6 = tmp()
                    TTo("pool", t6.ap, CUM.ap, SG[d].ap[:, c4, :], ALU.subtract, [CUM, SG[d]], [t6])
                    ACT(e0.ap, CUM.ap, AF.Exp, [CUM], [e0], scale=-EM05)
                    ACT(e1.ap, t6.ap, AF.Exp, [t6], [e1], scale=-EM05)
                    ACT(e2.ap, CUM.ap, AF.Exp, [CUM], [e2], scale=EM05)
                    ACT(e3.ap, tmc.ap, AF.Exp, [tmc], [e3], scale=-EM05)
                    kqv = FMo[d].ap[:, c4, 0:512].rearrange("p (c a t) -> p c a t", a=2, t=CH)
                    TTo("dve", kqv[:, :, 0, :], KKt.ap[:, c4, :].rearrange("p (c t) -> p c t", t=CH),
                        e1.ap.rearrange("p (c t) -> p c t", t=CH), ALU.mult, [KKt, e1], [FMo[d]])
                    TTo("dve", kqv[:, :, 1, :], Rv[:, c4, :].rearrange("p (c t) -> p c t", t=CH),
                        e0.ap.rearrange("p (c t) -> p c t", t=CH), ALU.mult, [SH, e0], [FMo[d]])
                    TTo("pool", FMo[d].ap[:, c4, 512:768], KDt[d].ap[:, c4, :], e2.ap, ALU.mult, [KDt[d], e2], [FMo[d]])
                    TTo("pool", FMo[d].ap[:, c4, 768:1024], Bt[d].ap[:, c4, :], e2.ap, ALU.mult, [Bt[d], e2], [FMo[d]])
                    TTo("dve", KGo[d].ap[:, c4, :], KDt[d].ap[:, c4, :], e3.ap, ALU.mult, [KDt[d], e3], [KGo[d]])
                    TTo("pool", BGo[d].ap[:, c4, :], Bt[d].ap[:, c4, :], e3.ap, ALU.mult, [Bt[d], e3], [BGo[d]])
                t7 = tmp()
                TTo("pool", t7.ap, KDt[0].ap[:, c4, :], KDt[1].ap[:, c4, :], ALU.add, [KDt[0], KDt[1]], [t7])
                STT("dve", SQb.ap, t7.ap, pvc(f"rk{L}", c4), Rv[:, c4, :], ALU.mult, ALU.mult, [t7, pv, SH], [SQb])
                bk = bank()
                MM(bk.ap[:, 0:SEG], [(bdb.ap, SQb.ap)], [bdb, SQb], [bk])
                TTo("dve", BGo2.ap[:, 0, c4, :], bk.ap[:, 0:SEG], Vv[:, c4, :], ALU.mult, [bk, SH], [BGo2])
            if KLIM <= 2:
                continue
            for d in range(2):
                DMA("sp", FM[d].ap[s], FMo[d].ap.rearrange("p j n -> p (j n)"), [FMo[d]], [FM[d]])
                DMA("sp", GC[d].ap[s], GCo[d].ap.rearrange("p j c -> p (j c)"), [GCo[d]], [GC[d]])
            DMA("sp", BGD.ap[s], BGo2.ap.rearrange("p a j t -> p (a j t)"), [BGo2], [BGD])
            if KLIM <= 3:
                continue
            for hh in range(2):
                to = TMo[tmo_i[0] % 2]; tmo_i[0] += 1
                for qi, srct in enumerate((Vb, KGo[0], BGo[0], KGo[1], BGo[1])):
                    bb = bankbf()
                    TRS([(bb.ap[:, c4 * 128:(c4 + 1) * 128], srct.ap[:, c4, hh * 128:(hh + 1) * 128]) for c4 in range(4)],
                        identb.ap, [srct, identb], [bb])
                    CP("act" if qi % 2 else "dve", to.ap[:, qi, :], bb.ap, [bb], [to])
                DMA("sp", TMD.ap[t0 + hh * 128:t0 + (hh + 1) * 128].rearrange("t q n -> t (q n)"),
                    to.ap.rearrange("p q n -> p (q n)"), [to], [TMD])

    def pass_scan(L, d, store_ctx, segs_override=None):
        setoff(W_BASE)
        LD = []
        for k in range(2):
            LD.append(dict(FM=tile([64, 4, 2, 1024], BF16, f"S_FM{k}"), TV=tile([64, 4, 512], BF16, f"S_TV{k}"),
                           TG=tile([64, 4, 2, 512], BF16, f"S_TG{k}"), GC=tile([64, 4, 2, 4], F32, f"S_GC{k}")))
        AT1 = tile([64, 8, 128], BF16, "S_AT1"); AT2 = tile([64, 8, 128], BF16, "S_AT2")
        ZF = [tile([64, 8, 128], F32, f"S_ZF{k}") for k in range(2)]
        NN = [tile([64, 8, 64], F32, f"S_NN{k}") for k in range(2)]
        RHSb = tile([64, 8, 64], F32, "S_RHS"); UNb = tile([64, 8, 64], BF16, "S_UN")
        Hf = tile([64, 8, 64], F32, "S_H"); Hh = tile([64, 8, 64], BF16, "S_Hb")
        YO = [tile([64, 8, SEG], F32, f"S_YO{k}") for k in range(2)]
        S.op("dve", lambda e: e.memset(Hf.ap, 0.0), writes=[Hf.b])
        S.op("pool", lambda e: e.memset(Hh.ap, 0.0), writes=[Hh.b])
        m1 = cst.ap[0:64, CO["m1b" if d else "m1f"]:CO["m1b" if d else "m1f"] + 128].unsqueeze(1).to_broadcast([64, 8, 128])
        m3 = cst.ap[0:64, CO["m3b" if d else "m3f"]:CO["m3b" if d else "m3f"] + 64].unsqueeze(1).to_broadcast([64, 8, 64])
        id64 = cst.ap[0:64, CO["id64"]:CO["id64"] + 64].unsqueeze(1).to_broadcast([64, 8, 64])
        segs = [0] + (list(range(NSEG - 1, 0, -1)) if d else list(range(1, NSEG)))
        if segs_override is not None:
            segs = segs_override

        def load(si):
            s = segs[si]; ld = LD[si % 2]
            t0 = segcol(s)
            for hh in range(2):
                DMA("sp", ld["FM"].ap[:, :, hh, :], FM[d].ap[s, hh * 64:(hh + 1) * 64, :].rearrange("k (j n) -> k j n", j=4),
                    [FM[d]], [ld["FM"]])
                DMA("sp", ld["GC"].ap[:, :, hh, :], GC[d].ap[s, hh * 64:(hh + 1) * 64, :].rearrange("k (j c) -> k j c", j=4),
                    [GC[d]], [ld["GC"]])
            tmv = TMD.ap[t0:t0 + SEG].rearrange("(c s) q n -> s c q n", s=64)
            DMA("sp", ld["TV"].ap, tmv[:, :, 0, :], [TMD], [ld["TV"]])
            DMA("sp", ld["TG"].ap, tmv[:, :, 1 + 2 * d:3 + 2 * d, :], [TMD], [ld["TG"]])

        load(0)
        for si, s in enumerate(segs):
            if si + 1 < len(segs):
                load(si + 1)
            ld = LD[si % 2]
            yo = YO[si % 2]
            if s == 0:
                if si > 0:
                    pass
            chunks = [3, 2, 1, 0] if d else [0, 1, 2, 3]
            for c in chunks:
                fmv = ld["FM"].ap.rearrange("k j two n -> k (j two) n")
                kq = fmv[:, :, c * 128:(c + 1) * 128]
                kh = fmv[:, :, 512 + c * CH:512 + (c + 1) * CH]
                bh = fmv[:, :, 768 + c * CH:768 + (c + 1) * CH]
                vt = ld["TV"].ap[:, c, :].rearrange("s (h v) -> s h v", v=64)
                kg = ld["TG"].ap[:, c, 0, :].rearrange("s (h v) -> s h v", v=64)
                bg = ld["TG"].ap[:, c, 1, :].rearrange("s (h v) -> s h v", v=64)
                pa1 = [bank(), bank()]; pa2 = [bank(), bank()]; pa3 = bank()
                def hv(pb, h, w):
                    return pb[h // 4].ap[0:64, (h % 4) * w:(h % 4 + 1) * w]
                MMS([(hv(pa1, h, 128), [(kh[:, h, :], kq[:, h, :])]) for h in range(8)], [ld["FM"]], pa1)
                MMS([(hv(pa2, h, 128), [(bh[:, h, :], kq[:, h, :])]) for h in range(8)], [ld["FM"]], pa2)
                MMS([(pa3.ap[0:64, h * 64:(h + 1) * 64], [(kq[:, h, 0:64], bh[:, h, :])]) for h in range(8)], [ld["FM"]], [pa3])
                for hf in range(2):
                    TTo("dve", AT1.ap[:, hf * 4:(hf + 1) * 4, :], pa1[hf].ap[0:64, :].rearrange("p (h x) -> p h x", x=128),
                        m1[:, 0:4, :], ALU.mult, [pa1[hf], cst], [AT1])
                    TTo("dve", AT2.ap[:, hf * 4:(hf + 1) * 4, :], pa2[hf].ap[0:64, :].rearrange("p (h x) -> p h x", x=128),
                        m1[:, 0:4, :], ALU.mult, [pa2[hf], cst], [AT2])
                zf, nn = ZF[0], NN[0]
                TTo("dve", nn.ap, pa3.ap[0:64, :].rearrange("p (h x) -> p h x", x=64), m3, ALU.mult, [pa3, cst], [nn])
                for hf in range(2):
                    TTo("dve", zf.ap[:, hf * 4:(hf + 1) * 4, 0:64], pa2[hf].ap[0:64, :].rearrange("p (h x) -> p h x", x=128)[:, :, 0:64],
                        m1[:, 0:4, 0:64], ALU.mult, [pa2[hf], cst], [zf])
                TTo("pool", zf.ap[:, :, 64:128], id64, zf.ap[:, :, 0:64], ALU.subtract, [cst, zf], [zf])
                cur_i = 0
                for lev in range(6):
                    zf = ZF[cur_i]; nn = NN[cur_i]; zf2 = ZF[1 - cur_i]; nn2 = NN[1 - cur_i]
                    pz = [bank(), bank()]
                    last = (lev == 5)
                    if lev == 0:
                        MMS([(hv(pz, h, 128)[:, 0:64], [(nn.ap[:, h, :], zf.ap[:, h, 0:64])]) for h in range(8)], [nn, zf], pz)
                    elif not last:
                        MMS([(hv(pz, h, 128), [(nn.ap[:, h, :], zf.ap[:, h, :])]) for h in range(8)], [nn, zf], pz)
                    else:
                        MMS([(hv(pz, h, 128)[:, 64:128], [(nn.ap[:, h, :], zf.ap[:, h, 64:128])]) for h in range(8)], [nn, zf], pz)
                    if not last:
                        pn = bank()
                        MMS([(pn.ap[0:64, h * 64:(h + 1) * 64], [(zf.ap[:, h, 0:64], nn.ap[:, h, :])]) for h in range(8)], [nn, zf], [pn])
                        CP("act", nn2.ap, pn.ap[0:64, :].rearrange("p (h x) -> p h x", x=64), [pn], [nn2])
                    for hf in range(2):
                        pzv = pz[hf].ap[0:64, :].rearrange("p (h x) -> p h x", x=128)
                        hs = slice(hf * 4, (hf + 1) * 4)
                        if not last:
                            CP("act", zf2.ap[:, hs, 0:64], pzv[:, :, 0:64], [pz[hf]], [zf2])
                        if lev == 0:
                            CP("pool", zf2.ap[:, hs, 64:128], zf.ap[:, hs, 64:128], [zf], [zf2])
                        else:
                            TTo("dve", zf2.ap[:, hs, 64:128], pzv[:, :, 64:128], zf.ap[:, hs, 64:128], ALU.add, [pz[hf], zf], [zf2])
                    cur_i = 1 - cur_i
                Fm = ZF[cur_i]
                pr = bank()
                MMS([(pr.ap[0:64, h * 64:(h + 1) * 64], [(kq[:, h, 0:64], Hh.ap[:, h, :]), (AT1.ap[:, h, 0:64], vt[:, h, :])])
                     for h in range(8)], [ld["FM"], Hh, AT1, ld["TV"]], [pr])
                CP("act", RHSb.ap, pr.ap[0:64, :].rearrange("p (h x) -> p h x", x=64), [pr], [RHSb])
                pu = bank()
                MMS([(pu.ap[0:64, h * 64:(h + 1) * 64], [(Fm.ap[:, h, 64:128], RHSb.ap[:, h, :])]) for h in range(8)], [Fm, RHSb], [pu])
                S.op("act", lambda e, pu=pu: e.mul(out=UNb.ap, in_=pu.ap[0:64, :].rearrange("p (h x) -> p h x", x=64), mul=-1.0),
                     reads=[pu.b], writes=[UNb.b])
                py = bank()
                MMS([(py.ap[0:64, h * 64:(h + 1) * 64],
                      [(Hh.ap[:, h, :], kq[:, h, 64:128]), (vt[:, h, :], AT1.ap[:, h, 64:128]), (UNb.ap[:, h, :], AT2.ap[:, h, 64:128])])
                     for h in range(8)], [Hh, ld["FM"], ld["TV"], AT1, UNb, AT2], [py])
                CP("act", yo.ap[:, :, c * CH:(c + 1) * CH], py.ap[0:64, :].rearrange("p (h x) -> p h x", x=64), [py], [yo])
                ph = bank()
                MMS([(ph.ap[0:64, h * 64:(h + 1) * 64], [(kg[:, h, :], vt[:, h, :]), (bg[:, h, :], UNb.ap[:, h, :])]) for h in range(8)],
                    [ld["TG"], ld["TV"], UNb], [ph])
                gcb = ld["GC"].ap.rearrange("k j two c -> k (j two) c")[:, :, c:c + 1].to_broadcast([64, 8, 64])
                TTo("dve", Hf.ap, Hf.ap, gcb, ALU.mult, [Hf, ld["GC"]], [Hf])
                TTo("dve", Hf.ap, Hf.ap, ph.ap[0:64, :].rearrange("p (h x) -> p h x", x=64), ALU.add, [Hf, ph], [Hf])
                CP("pool", Hh.ap, Hf.ap, [Hf], [Hh])
            if s > 0 or store_ctx:
                t0 = segcol(s)
                for hh in range(2):
                    DMA("sp", YD[d].ap[s, hh * 64:(hh + 1) * 64, :].rearrange("v (j t) -> v j t", j=4),
                        yo.ap.rearrange("v (j two) t -> v j two t", two=2)[:, :, hh, :], [yo], [YD[d]])

    def pass_E3(L, segs):
        i = L // 2
        load_w(WB, ev_w_out[i].rearrange("(k p) n -> p k n", p=128), D, 8)
        setoff(W_BASE)
        Y0 = tile([128, 4, SEG], F32, "E3_Y0"); Y1 = tile([128, 4, SEG], F32, "E3_Y1")
        BGt = tile([128, 2, 4, SEG], BF16, "E3_BG")
        YC = tile([128, 4, SEG], F32, "E3_YC")
        MIX = ACTF
        for s in segs:
            xt = XT[s % 2]
            col = 0 if s > 0 else 1
            t0 = segcol(s)
            load_x(xt, s)
            DMA("sp", Y0.ap.rearrange("p k t -> p (k t)"), YD[0].ap[s], [YD[0]], [Y0])
            DMA("sp", Y1.ap.rearrange("p k t -> p (k t)"), YD[1].ap[s], [YD[1]], [Y1])
            DMA("sp", BGt.ap.rearrange("p a k t -> p (a k t)"), BGD.ap[s], [BGD], [BGt])
            DMA("sp", MIX.ap[:, 4:8, :].rearrange("p k t -> p (k t)"), YP.ap[s], [YP], [MIX])
            TTo("pool", Y0.ap, Y0.ap, Y1.ap, ALU.add, [Y0, Y1], [Y0])
            for c4 in range(4):
                CP("act", SQ.ap[:, c4, :], Y0.ap[:, c4, :], [Y0], [SQ])
                bk = bank()
                MM(bk.ap[:, 0:SEG], [(bdb.ap, SQ.ap[:, c4, :])], [bdb, SQ], [bk])
                STT("dve", YC.ap[:, c4, :], bk.ap[:, 0:SEG], -1.0 / 64, Y0.ap[:, c4, :], ALU.mult, ALU.add, [bk, Y0], [YC])
                ACT(SQ.ap[:, c4, :], YC.ap[:, c4, :], AF.Square, [YC], [SQ])
                bk = bank()
                MM(bk.ap[:, 0:SEG], [(bdb.ap, SQ.ap[:, c4, :])], [bdb, SQ], [bk])
                t1 = tmp()
                TS("dve", t1.ap, bk.ap[:, 0:SEG], 1.0 / 64, GN_EPS, ALU.mult, ALU.add, [bk], [t1])
                ACT(t1.ap, t1.ap, AF.Sqrt, [t1], [t1])
                RECIP(t1.ap, t1.ap, [t1], [t1])
                TTo("dve", YC.ap[:, c4, :], YC.ap[:, c4, :], t1.ap, ALU.mult, [YC, t1], [YC])
                ACT(YC.ap[:, c4, :], YC.ap[:, c4, :], AF.Identity, [YC, pv], [YC], scale=pvc(f"gnw{L}", c4), bias=pvc(f"gnb{L}", c4))
                TTo("pool", YC.ap[:, c4, :], YC.ap[:, c4, :], BGt.ap[:, 0, c4, :], ALU.add, [YC, BGt], [YC])
                TTo("dve", MIX.ap[:, c4, :], YC.ap[:, c4, :], BGt.ap[:, 1, c4, :], ALU.mult, [YC, BGt], [MIX])
            for oc in range(8):
                bk = bank()
                MM(bk.ap[:, 0:SEG], [(WB.ap[:, k, oc * 128:(oc + 1) * 128], MIX.ap[:, k, :]) for k in range(8)], [WB, MIX], [bk])
                CP("act", Yt.ap[:, oc, :], bk.ap[:, 0:SEG], [bk], [Yt])
            residual(xt, L, 2, col)
            store_x(xt, s)

    def pass_O(L, segs):
        i = L // 2
        load_w(WA, od_w_in[i].rearrange("(k p) n -> p k n", p=128), 3 * D, 8)
        load_w(WB, od_w_out[i].rearrange("(k p) n -> p k n", p=128), D, 8)
        MIX = ACTF
        for s in segs:
            xt = XT[s % 2]
            col = 0 if s > 0 else 1
            nr, rl = grid(s)
            load_x(xt, s)
            norm_mod(xt, L, 0, 1, col)
            for j in range(8):
                pb = bank(); pc = bank(); pu = bank()
                for (bk, c0) in ((pb, j * 128), (pc, D + j * 128), (pu, 2 * D + j * 128)):
                    MM(bk.ap[:, 0:SEG], [(WA.ap[:, k, c0:c0 + 128], Hb.ap[:, k, :]) for k in range(8)], [WA, Hb], [bk])
                t1 = tmp(); t2 = tmp(); t3 = tmp()
                CP("act", t1.ap, pu.ap[:, 0:SEG], [pu], [t1])
                TTo("dve", t2.ap, pc.ap[:, 0:SEG], t1.ap, ALU.mult, [pc, t1], [t2])
                conv3(t3, t2.ap, [t2, pv], False, pvc(f"oconv{L}_0", j), pvc(f"oconv{L}_1", j), pvc(f"oconv{L}_2", j), nr, rl)
                TTo("dve", MIX.ap[:, j, :], pb.ap[:, 0:SEG], t3.ap, ALU.mult, [pb, t3], [MIX])
            for oc in range(8):
                bk = bank()
                MM(bk.ap[:, 0:SEG], [(WB.ap[:, k, oc * 128:(oc + 1) * 128], MIX.ap[:, k, :]) for k in range(8)], [WB, MIX], [bk])
                CP("act", Yt.ap[:, oc, :], bk.ap[:, 0:SEG], [bk], [Yt])
            residual(xt, L, 2, col)
            store_x(xt, s)

    def pass_F(L, segs):
        load_w(WA, ffn_w_up[L].rearrange("(k p) n -> p k n", p=128), 2 * DFF, 8)
        load_w(WB, ffn_w_down[L].rearrange("(k p) n -> p k n", p=128), D, NFF)
        for s in segs:
            xt = XT[s % 2]
            col = 0 if s > 0 else 1
            nr, rl = grid(s)
            load_x(xt, s)
            norm_mod(xt, L, 3, 4, col)
            for j in range(NFF):
                pc = bank(); pg = bank()
                for (bk, c0) in ((pc, j * 128), (pg, DFF + j * 128)):
                    MM(bk.ap[:, 0:SEG], [(WA.ap[:, k, c0:c0 + 128], Hb.ap[:, k, :]) for k in range(8)], [WA, Hb], [bk])
                t1 = tmp(); t2 = tmp()
                conv3(t1, pc.ap[:, 0:SEG], [pc, pv], True, pvc(f"fconv{L}_0", j), pvc(f"fconv{L}_1", j), pvc(f"fconv{L}_2", j), nr, rl)
                ACT(t2.ap, t1.ap, AF.Silu, [t1], [t2])
                TTo("dve", ACTF.ap[:, j, :], pg.ap[:, 0:SEG], t2.ap, ALU.mult, [pg, t2], [ACTF])
            for oc in range(8):
                bk = bank()
                MM(bk.ap[:, 0:SEG], [(WB.ap[:, k, oc * 128:(oc + 1) * 128], ACTF.ap[:, k, :]) for k in range(NFF)], [WB, ACTF], [bk])
                CP("act", Yt.ap[:, oc, :], bk.ap[:, 0:SEG], [bk], [Yt])
            residual(xt, L, 5, col)
            store_x(xt, s)

    def pass_out():
        setoff(W_BASE)
        OT = [tile([128, 2, D], F32, f"O_OT{k}") for k in range(2)]
        for s in range(1, NSEG):
            xt = XT[s % 2]
            ot = OT[s % 2]
            load_x(xt, s)
            for hh in range(2):
                for q in range(2):
                    bk = bank()
                    TRS([(bk.ap[:, f * 128:(f + 1) * 128], xt.ap[:, q * 4 + f, hh * 128:(hh + 1) * 128]) for f in range(4)],
                        ident, [xt, cst], [bk])
                    CP("act" if q else "dve", ot.ap[:, hh, q * 512:(q + 1) * 512], bk.ap, [bk], [ot])
            DMA("sp", out_d[(s - 1) * SEG:s * SEG].rearrange("(h p) f -> p h f", p=128), ot.ap, [ot], [OUTB])

    ALLS = list(range(NSEG)); LAT = list(range(1, NSEG))
    plist = []
    for L in range(DEPTH):
        ctx_later = L in (0, 1)
        if L % 2 == 0:
            plist.append((f"E1_{L}", lambda L=L: pass_E1(L, ALLS)))
            plist.append((f"prep_{L}", lambda L=L: pass_prep(L, ALLS)))
            plist.append((f"scan0_{L}", lambda L=L, c=ctx_later: pass_scan(L, 0, c)))
            plist.append((f"scan1_{L}", lambda L=L, c=ctx_later: pass_scan(L, 1, c)))
            plist.append((f"E3_{L}", lambda L=L, c=ctx_later: pass_E3(L, ALLS if c else LAT)))
        else:
            plist.append((f"O_{L}", lambda L=L, c=ctx_later: pass_O(L, ALLS if c else LAT)))
        plist.append((f"F_{L}", lambda L=L, c=ctx_later: pass_F(L, ALLS if c else LAT)))
    plist.append(("out", pass_out))
    if debug_stop in ("mod", "mod1"):
        plist = []
    if debug_stop == "prep_only":
        plist = [("prep_only", lambda: pass_prep(0, list(range(int(os.environ.get("PSEGS", "2"))))))]
    if debug_stop == "mini":
        plist = [("a", lambda: pass_E1(0, [0, 1])), ("b", lambda: pass_prep(0, [0, 1])),
                 ("c", lambda: pass_scan(0, 0, True, [0, 1])), ("mini", lambda: pass_scan(0, 1, True, [0]))]
    if debug_stop == "mini2":
        plist = [("a", lambda: pass_E1(0, [0, 1])), ("b", lambda: pass_prep(0, [0, 1])),
                 ("c", lambda: pass_scan(0, 0, True, [0, 1])), ("d", lambda: pass_scan(0, 1, True, [0])),
                 ("e", lambda: pass_E3(0, [0])), ("f", lambda: pass_F(0, [0])), ("g", lambda: pass_O(1, [0])), ("mini2", lambda: pass_F(1, [0]))]
    if debug_stop == "testB":
        plist = [("f", lambda: pass_F(0, [1])), ("g", lambda: pass_O(1, [1])), ("testB", lambda: pass_F(1, [1]))]
    if debug_stop == "testC":
        plist = [("testC", pass_out)]
    if debug_stop == "scan_only":
        plist = [("scan_only", lambda: pass_scan(0, 0, True))]
    snaps = {}
    for name, fn in plist:
        fn()
        if debug_stop in ("mini2", "testB") and name in ("e", "f", "g", "mini2", "testB"):
            sn = nc.dram_tensor("snap_" + name, [128, 8 * SEG], F32, kind="Internal").ap()
            sb = T(sn, Buf("snap_" + name))
            DMA("sp", sn, XS.ap[0 if debug_stop == "mini2" else 1], [XS], [sb])
            snaps[name] = sb
        if debug_stop == name:
            break
    if debug_stop:
        dbg = nc.dram_tensor("dbg_mdv", [128, DEPTH * 6 * 8 * 2], F32, kind="ExternalOutput").ap()
        dbt = T(dbg, Buf("dbg"))
        DMA("sp", dbg, mdv.ap.rearrange("p a b c d -> p (a b c d)"), [mdv], [dbt])
        fin = [t.b for t in scr_list if t.b.lw is not None and t.b.name in debug_outs] + [dbt.b]
        if OUTB.b.lw is not None:
            fin.append(OUTB.b)
        S.finish("sp", fin)
    else:
        S.finish("sp", [OUTB.b])
    S.emit()
    return nc

_NC_CACHE = {}

def kernel(**inputs):
    inp = {k: np.asarray(v) for k, v in inputs.items()}
    if "nc" not in _NC_CACHE:
        _NC_CACHE["nc"] = build_program()
    nc = _NC_CACHE["nc"]
    f = lambda a: np.ascontiguousarray(a, dtype=np.float32)
    shared = {k: f(inp[k]) for k in ("w_mod", "ffn_w_up", "ffn_w_down", "ev_w_in", "ev_w_out", "ev_w_up", "ev_a_up",
                                     "ev_g_up", "ev_pool_w", "od_w_in", "od_w_out")}
    shared["consts"] = CONSTS
    in_maps = []
    for b in range(8):
        m = dict(shared)
        m["x"] = f(inp["x"][b]); m["ctx"] = f(inp["ctx"][b]); m["pvec"] = build_pvec(inp, b)
        in_maps.append(m)
    res = run_bass_kernel_spmd(nc, in_maps, core_ids=list(range(8)))
    return np.stack([np.asarray(r["out"], dtype=np.float32) for r in res.results], axis=0)
```

```python
import os
import numpy as np
import concourse.bass as bass
import concourse.mybir as mybir
from concourse.bass_utils import run_bass_kernel_spmd

F32 = mybir.dt.float32
BF16 = mybir.dt.bfloat16
AF = mybir.ActivationFunctionType
ALU = mybir.AluOpType

D = 1024
SEQ = 4096
CTX = 256
TALL = SEQ + CTX
SEG = 256
NSEG = TALL // SEG
CH = 64
NCH = TALL // CH
DFF = 2816
NFF = DFF // 128
DEPTH = 4
RMS_EPS = 1e-6
GN_EPS = 64e-5
EM05 = float(np.exp(-0.5))
KLIM = int(os.environ.get('KLIM', '99'))

PGRP = [(i * 128, 128) for i in range(12)] + [(1536, 64), (1600, 64), (1664, 64), (1728, 96)]

CO = {}
def _build_consts():
    cols = []
    def add(name, arr):
        CO[name] = sum(a.shape[1] for a in cols)
        cols.append(arr.astype(np.float32))
    p = np.arange(128)[:, None]
    j = np.arange(128)[None, :]
    add("ident", (p == j))
    add("bd", ((p // 64) == (j // 64)))
    add("ones", np.ones((128, 128)))
    add("restart", np.tile(((np.arange(256) % 64) != 0)[None, :], (128, 1)))
    def inv(row_len):
        pos = np.arange(row_len)
        out = []
        for win in (2, 4, 8, 16):
            lo = np.clip(pos - win // 2, 0, row_len)
            hi = np.clip(pos + win // 2, 0, row_len)
            out.append(1.0 / (hi - lo))
        return np.tile(np.concatenate(out)[None, :], (128, 1))
    add("inv_lat", inv(64))
    add("inv_ctx", inv(256))
    s = np.arange(128)[:, None] % 64
    t = np.arange(64)[None, :]
    add("m1f", np.concatenate([(s < t), (s <= t)], 1))
    add("m1b", np.concatenate([(s > t), (s >= t)], 1))
    add("m3f", (t < s))
    add("m3b", (t > s))
    add("id64", (s == t))
    return np.concatenate(cols, 1)
CONSTS = _build_consts()
NCONST = CONSTS.shape[1]

PV = {}
def _pv_layout():
    n = 0
    def add(name, w):
        nonlocal n
        PV[name] = n
        n += w
    add("c", 8); add("cctx", 8)
    for L in range(DEPTH):
        for jn in range(4):
            add(f"g{L}_{jn}", 8)
        add(f"bmod{L}", 48)
        for tp in range(3):
            add(f"fconv{L}_{tp}", NFF)
        if L % 2 == 0:
            add(f"mu{L}_0", 16); add(f"mu{L}_1", 16)
            for d in range(2):
                add(f"w0{L}_{d}", 4); add(f"a0{L}_{d}", 4)
            for nm in ("kk", "ka", "rk", "gnw", "gnb", "psc"):
                add(f"{nm}{L}", 4)
        else:
            for tp in range(3):
                add(f"oconv{L}_{tp}", 8)
    return n
NPV = _pv_layout()

def _colmajor(v):
    return np.ascontiguousarray(np.asarray(v, np.float32).reshape(-1, 128).T)

def build_pvec(inp, b):
    t = np.zeros((128, NPV), np.float32)
    def put(name, arr):
        t[:, PV[name]:PV[name] + arr.shape[1]] = arr
    put("c", _colmajor(inp["c"][b])); put("cctx", _colmajor(inp["c_ctx"]))
    for L in range(DEPTH):
        i = L // 2
        for jn in range(4):
            put(f"g{L}_{jn}", _colmajor(inp["norm_g"][L, jn]))
        put(f"bmod{L}", _colmajor(inp["b_mod"][L]))
        for tp in range(3):
            put(f"fconv{L}_{tp}", _colmajor(inp["ffn_conv"][L, tp]))
        if L % 2 == 0:
            for m in range(2):
                a = np.zeros((128, 16), np.float32)
                for gi, (st, sz) in enumerate(PGRP):
                    a[:sz, gi] = inp["ev_mu"][i, m, st:st + sz]
                put(f"mu{L}_{m}", a)
            for d in range(2):
                put(f"w0{L}_{d}", _colmajor(inp["ev_w0"][i, d]))
                put(f"a0{L}_{d}", _colmajor(inp["ev_a0"][i, d]))
            put(f"kk{L}", _colmajor(inp["ev_k_k"][i])); put(f"ka{L}", _colmajor(inp["ev_k_a"][i]))
            put(f"rk{L}", _colmajor(inp["ev_r_k"][i].reshape(-1)))
            put(f"gnw{L}", _colmajor(inp["ev_gn_w"][i])); put(f"gnb{L}", _colmajor(inp["ev_gn_b"][i]))
            put(f"psc{L}", _colmajor(inp["ev_pool_scale"][i]))
        else:
            for tp in range(3):
                put(f"oconv{L}_{tp}", _colmajor(inp["od_conv"][i, tp]))
    return t

class Buf:
    __slots__ = ("name", "lw", "rd", "dsem", "dcnt", "rng", "al")
    def __init__(self, name, rng=None):
        self.name = name; self.lw = None; self.rd = []; self.dsem = None; self.dcnt = 0
        self.rng = rng; self.al = []

class Sched:
    ENGS = ("pe", "act", "dve", "pool", "sp")
    def __init__(self, nc):
        self.nc = nc
        self.ops = {e: [] for e in self.ENGS}
        self.cnt = {e: 0 for e in self.ENGS}
        self.waited = {e: {} for e in self.ENGS}
        self.sems = {e: nc.alloc_semaphore("s_" + e) for e in self.ENGS}
        self.key = {e: e for e in self.ENGS}
        self.epoch = {e: 0 for e in self.ENGS}
        self.nsem = 0
        self.sb = []
    def sbuf(self, name, lo, hi):
        b = Buf(name, (lo, hi))
        for o in self.sb:
            if o.rng[0] < hi and lo < o.rng[1]:
                o.al.append(b); b.al.append(o)
        self.sb.append(b)
        return b
    def _waits(self, eng, evs, pe_self=False):
        w = self.waited[eng]; best = {}
        for ev in evs:
            if ev is None:
                continue
            k, v = ev
            if pe_self and k.startswith("pe"):
                continue
            if w.get(k, 0) >= v:
                continue
            if best.get(k, 0) < v:
                best[k] = v
        for k, v in best.items():
            w[k] = v
        return list(best.items())
    def _gather(self, reads, writes):
        evs = []
        for b in reads:
            evs.append(b.lw)
            for o in b.al:
                evs.append(o.lw)
        for b in writes:
            evs.append(b.lw); evs.extend(b.rd)
            for o in b.al:
                evs.append(o.lw); evs.extend(o.rd)
        return evs
    def op(self, eng, fn, reads=(), writes=()):
        waits = self._waits(eng, self._gather(reads, writes), pe_self=(eng == "pe"))
        if self.cnt[eng] >= 12000:
            self.epoch[eng] += 1
            self.cnt[eng] = 0
            self.key[eng] = "%s#%d" % (eng, self.epoch[eng])
            self.sems[self.key[eng]] = self.nc.alloc_semaphore("s_%s_%d" % (eng, self.epoch[eng]))
        self.cnt[eng] += 1
        ev = (self.key[eng], self.cnt[eng])
        for b in reads:
            b.rd.append(ev)
        for b in writes:
            b.lw = ev; b.rd = []
        self.ops[eng].append((fn, waits, (self.key[eng], 1)))
    def dma(self, q, fn, reads=(), writes=(), n=1):
        (dst,) = writes
        if dst.dsem is None:
            key = "d%d" % self.nsem
            self.nsem += 1
            self.sems[key] = self.nc.alloc_semaphore(key)
            dst.dsem = key
        waits = self._waits(q, self._gather(reads, writes))
        dst.dcnt += 16 * n
        ev = (dst.dsem, dst.dcnt)
        for b in reads:
            b.rd.append(ev)
        dst.lw = ev; dst.rd = []
        self.ops[q].append((fn, waits, (dst.dsem, 16)))
    def finish(self, eng, bufs):
        self.ops[eng].append((None, self._waits(eng, [b.lw for b in bufs]), None))
    def emit(self):
        nc = self.nc; sems = self.sems
        with nc.Block() as block:
            def run(name, e):
                for fn, waits, inc in self.ops[name]:
                    for k, v in waits:
                        e.wait_ge(sems[k], v)
                    if fn is None:
                        continue
                    r = fn(e)
                    if isinstance(r, (list, tuple)):
                        for ins in r:
                            ins.then_inc(sems[inc[0]], inc[1])
                    else:
                        r.then_inc(sems[inc[0]], inc[1])
            @block.tensor
            def _(e): run("pe", e)
            @block.scalar
            def _(e): run("act", e)
            @block.vector
            def _(e): run("dve", e)
            @block.gpsimd
            def _(e): run("pool", e)
            @block.sync
            def _(e): run("sp", e)

_DS = {F32: 4, BF16: 2}

class T:
    __slots__ = ("ap", "b")
    def __init__(self, ap, b):
        self.ap = ap; self.b = b

def build_program(debug_stop=None, debug_outs=()):
    nc = bass.Bass("TRN2", target_bir_lowering=False)
    S = Sched(nc)
    dram = {}
    def din(name, shape, dt=F32):
        dram[name] = nc.dram_tensor(name, list(shape), dt, kind="ExternalInput").ap()
        return dram[name]
    x_in = din("x", [SEQ, D]); ctx_in = din("ctx", [CTX, D])
    consts_in = din("consts", [128, NCONST]); pvec_in = din("pvec", [128, NPV])
    w_mod = din("w_mod", [DEPTH, D, 6 * D]); ffn_w_up = din("ffn_w_up", [DEPTH, D, 2 * DFF])
    ffn_w_down = din("ffn_w_down", [DEPTH, DFF, D])
    ev_w_in = din("ev_w_in", [2, D, 2336]); ev_w_out = din("ev_w_out", [2, D, D])
    ev_w_up = din("ev_w_up", [2, 2, 32, 512]); ev_a_up = din("ev_a_up", [2, 2, 64, 512])
    ev_g_up = din("ev_g_up", [2, 96, 512]); ev_pool_w = din("ev_pool_w", [2, 4, 128, 128])
    od_w_in = din("od_w_in", [2, D, 3 * D]); od_w_out = din("od_w_out", [2, D, D])
    out_d = nc.dram_tensor("out", [SEQ, D], F32, kind="ExternalOutput").ap()
    IN = Buf("inputs")
    OUTB = T(None, Buf("out"))
    scr_list = []
    def dscr(name, shape, dt):
        t = T(nc.dram_tensor(name, list(shape), dt, kind=("ExternalOutput" if name in debug_outs else "Internal")).ap(), Buf(name))
        scr_list.append(t)
        return t
    XS = dscr("XS", [NSEG, 128, 8 * SEG], F32)
    PR = dscr("PR", [NSEG, 128, 16 * SEG], F32)
    YP = dscr("YP", [NSEG, 128, 4 * SEG], BF16)
    FM = [dscr(f"FM{d}", [NSEG, 128, 4 * 1024], BF16) for d in range(2)]
    TMD = dscr("TMD", [TALL, 5, 512], BF16)
    GC = [dscr(f"GC{d}", [NSEG, 128, 16], F32) for d in range(2)]
    BGD = dscr("BGD", [NSEG, 128, 2 * 4 * SEG], BF16)
    YD = [dscr(f"YD{d}", [NSEG, 128, 4 * SEG], F32) for d in range(2)]

    cur = [0]
    def setoff(o):
        cur[0] = o
    tcache = {}
    def tile(shape, dt, name):
        nbytes = int(np.prod(shape[1:])) * _DS[dt]
        nbytes = (nbytes + 31) // 32 * 32
        off = cur[0]
        cur[0] += nbytes
        assert cur[0] <= 229376, (name, cur[0])
        if name in tcache:
            assert tcache[name][1] == off, name
            return tcache[name][0]
        h = nc.alloc_sbuf_tensor_at(name, list(shape), dt, offset=off)
        t = T(h.ap(), S.sbuf(name, off, off + nbytes))
        tcache[name] = (t, off)
        return t
    banks = []
    for i in range(7):
        banks.append(T(nc.alloc_psum_tensor(f"bank{i}", [128, 512], F32).ap(), Buf(f"bank{i}")))
    bankb_ap = nc.alloc_psum_tensor("bankb", [128, 1024], BF16).ap()
    _bb = Buf("bankb")
    bankb = [T(bankb_ap[:, 0:512], _bb), T(bankb_ap[:, 512:1024], _bb)]
    bki = [0, 0]
    def bank():
        bki[0] = (bki[0] + 1) % 7
        return banks[bki[0]]
    def bankbf():
        bki[1] = (bki[1] + 1) % 2
        return bankb[bki[1]]

    def bl(ts):
        return [t.b for t in ts]
    def TTo(eng, out, a, b, op, r, w):
        S.op(eng, lambda e: e.tensor_tensor(out=out, in0=a, in1=b, op=op), reads=bl(r), writes=bl(w))
    def STT(eng, out, in0, scalar, in1, op0, op1, r, w):
        S.op(eng, lambda e: e.scalar_tensor_tensor(out=out, in0=in0, scalar=scalar, in1=in1, op0=op0, op1=op1),
             reads=bl(r), writes=bl(w))
    def TS(eng, out, in0, s1, s2, op0, op1, r, w):
        if op1 is None:
            S.op(eng, lambda e: e.tensor_scalar(out=out, in0=in0, scalar1=s1, scalar2=None, op0=op0),
                 reads=bl(r), writes=bl(w))
        else:
            S.op(eng, lambda e: e.tensor_scalar(out=out, in0=in0, scalar1=s1, scalar2=s2, op0=op0, op1=op1),
                 reads=bl(r), writes=bl(w))
    def ACT(out, in_, func, r, w, scale=None, bias=None):
        kw = {}
        if scale is not None:
            kw["scale"] = scale
        if bias is not None:
            kw["bias"] = bias
        S.op("act", lambda e: e.activation(out=out, in_=in_, func=func, **kw), reads=bl(r), writes=bl(w))
    def CP(eng, out, in_, r, w):
        if eng == "act":
            S.op("act", lambda e: e.copy(out=out, in_=in_), reads=bl(r), writes=bl(w))
        else:
            S.op(eng, lambda e: e.tensor_copy(out=out, in_=in_), reads=bl(r), writes=bl(w))
    def RECIP(out, in_, r, w):
        S.op("dve", lambda e: e.reciprocal(out=out, in_=in_), reads=bl(r), writes=bl(w))
    def MM(out, pairs, r, w):
        def fn(e):
            ins = None
            n = len(pairs)
            for i, (l, rh) in enumerate(pairs):
                ins = e.matmul(out, lhsT=l, rhs=rh, start=(i == 0), stop=(i == n - 1))
            return ins
        S.op("pe", fn, reads=bl(r), writes=bl(w))
    def MMS(items, r, w):
        def fn(e):
            ins = None
            for out, pairs in items:
                n = len(pairs)
                for i, (l, rh) in enumerate(pairs):
                    ins = e.matmul(out, lhsT=l, rhs=rh, start=(i == 0), stop=(i == n - 1))
            return ins
        S.op("pe", fn, reads=bl(r), writes=bl(w))
    def TRS(items, ident, r, w):
        def fn(e):
            ins = None
            for out, in_ in items:
                ins = e.transpose(out, in_, ident)
            return ins
        S.op("pe", fn, reads=bl(r), writes=bl(w))
    def DMA(q, out, in_, r, w, **kw):
        S.dma(q, lambda e: e.dma_start(out=out, in_=in_, **kw), reads=bl(r), writes=bl(w))
    def DMAS(q, pairs, r, w):
        S.dma(q, lambda e: [e.dma_start(out=o, in_=i) for (o, i) in pairs], reads=bl(r), writes=bl(w), n=len(pairs))
    TIN = T(None, IN)

    setoff(16640)
    cst = tile([128, NCONST], F32, "cst")
    pv = tile([128, NPV], F32, "pv")
    identb = tile([128, 128], BF16, "identb")
    bdb = tile([128, 128], BF16, "bdb")
    onesb = tile([128, 128], BF16, "onesb")
    modt = tile([128, DEPTH, 48, 2], F32, "modt")
    HALO = tile([128, 16, NSEG, 2], F32, "HALO")
    mdv = tile([128, DEPTH, 6, 8, 2], F32, "mdv")
    dvec = tile([128, 2, 48], F32, "dvec")
    silc = tile([128, 8, 2], F32, "silc")
    wupb = tile([64, 512], BF16, "wupb")
    aupb = [tile([64, 512], BF16, f"aupb{d}") for d in range(2)]
    gupb = tile([96, 512], BF16, "gupb")
    poolwb = tile([128, 4, 128], BF16, "poolwb")
    CONST_END = cur[0]
    W_BASE = (CONST_END + 63) // 64 * 64
    WA_BYTES = 8 * 5632 * 2
    WB_BYTES = NFF * 1024 * 2
    setoff(W_BASE)
    WA = tile([128, 8, 5632], BF16, "WA")
    WB = tile([128, NFF, 1024], BF16, "WB")
    A_BASE = cur[0]

    def cc(name, n=128):
        return cst.ap[:, CO[name]:CO[name] + n]
    def pvc(name, j, parts=128):
        return pv.ap[0:parts, PV[name] + j:PV[name] + j + 1]

    DMA("sp", cst.ap, consts_in, [TIN], [cst])
    DMA("sp", pv.ap, pvec_in, [TIN], [pv])
    CP("dve", identb.ap, cc("ident"), [cst], [identb])
    CP("dve", bdb.ap, cc("bd"), [cst], [bdb])
    CP("dve", onesb.ap, cc("ones"), [cst], [onesb])
    ident = cc("ident")

    setoff(W_BASE)
    wm = [tile([128, 8, 768], F32, f"wm{i}") for i in range(2)]
    ACT(silc.ap[:, :, 0], pv.ap[:, PV["c"]:PV["c"] + 8], AF.Silu, [pv], [silc])
    ACT(silc.ap[:, :, 1], pv.ap[:, PV["cctx"]:PV["cctx"] + 8], AF.Silu, [pv], [silc])
    pi = 0
    for L in range(1 if debug_stop in ("mod1", "prep_only", "scan_only", "mini") else (2 if debug_stop in ("mini2", "testB") else (1 if debug_stop == "testC" else DEPTH))):
        wv = w_mod[L].rearrange("(k p) n -> p k n", p=128)
        bk = bank()
        for pc in range(8):
            wt = wm[pi % 2]; pi += 1
            DMA("sp", wt.ap, wv[:, :, pc * 768:(pc + 1) * 768], [TIN], [wt])
            items = []
            for j in range(6):
                nchunk = pc * 6 + j
                items.append((bk.ap[:, nchunk * 2:nchunk * 2 + 2],
                              [(wt.ap[:, k, j * 128:(j + 1) * 128], silc.ap[:, k, :]) for k in range(8)]))
            MMS(items, [wt, silc], [bk])
        bm = pv.ap[:, PV[f"bmod{L}"]:PV[f"bmod{L}"] + 48].unsqueeze(2).to_broadcast([128, 48, 2])
        TTo("dve", modt.ap[:, L], bk.ap[:, 0:96].rearrange("p (n c) -> p n c", c=2), bm, ALU.add, [bk, pv], [modt])
        def gb(jn):
            return pv.ap[:, PV[f"g{L}_{jn}"]:PV[f"g{L}_{jn}"] + 8].unsqueeze(2).to_broadcast([128, 8, 2])
        def mo(k):
            return modt.ap[:, L, k * 8:(k + 1) * 8, :]
        for (dst, g, sc) in ((0, 0, 1), (3, 2, 4)):
            STT("dve", mdv.ap[:, L, dst], mo(sc), 1.0, gb(g), ALU.add, ALU.mult, [modt, pv], [mdv])
        for (dst, src) in ((1, 0), (4, 3)):
            CP("dve", mdv.ap[:, L, dst], mo(src), [modt], [mdv])
        for (dst, g, gt) in ((2, 1, 2), (5, 3, 5)):
            TTo("dve", mdv.ap[:, L, dst], mo(gt), gb(g), ALU.mult, [modt, pv], [mdv])
    def MV(L, kind, fc, col):
        return mdv.ap[:, L, kind, fc, col:col + 1]

    setoff(A_BASE)
    XT = [tile([128, 8, SEG], F32, f"XT{i}") for i in range(2)]
    Hb = tile([128, 8, SEG], BF16, "Hb")
    SQ = tile([128, 8, SEG], BF16, "SQ")
    Yt = tile([128, 8, SEG], F32, "Yt")
    ACTF = tile([128, NFF, SEG], BF16, "ACTF")
    RSTD = tile([128, SEG], F32, "RSTD")
    TMP = [tile([128, SEG], F32, f"TMP{i}") for i in range(6)]
    ACT_END = cur[0]
    tmpi = [0]
    def tmp():
        tmpi[0] = (tmpi[0] + 1) % 6
        return TMP[tmpi[0]]

    def segcol(s):
        return s * SEG

    def rms_rstd(src, srcbufs, nchunks, lhs_ones, scale, eps, dstr):
        bk = bank()
        for fc in range(nchunks):
            ACT(SQ.ap[:, fc, :], src[:, fc, :], AF.Square, srcbufs, [SQ])
        MM(bk.ap[:, 0:SEG], [(lhs_ones, SQ.ap[:, fc, :]) for fc in range(nchunks)], [SQ, onesb, bdb], [bk])
        t1 = tmp()
        TS("dve", t1.ap, bk.ap[:, 0:SEG], scale, eps, ALU.mult, ALU.add, [bk], [t1])
        ACT(t1.ap, t1.ap, AF.Sqrt, [t1], [t1])
        RECIP(dstr.ap, t1.ap, [t1], [dstr])

    def norm_mod(xt, L, ka, kb, col):
        rms_rstd(xt.ap, [xt], 8, onesb.ap, 1.0 / D, RMS_EPS, RSTD)
        for fc in range(8):
            t1 = tmp()
            TTo("dve", t1.ap, xt.ap[:, fc, :], RSTD.ap, ALU.mult, [xt, RSTD], [t1])
            ACT(Hb.ap[:, fc, :], t1.ap, AF.Identity, [t1, mdv], [Hb], scale=MV(L, ka, fc, col), bias=MV(L, kb, fc, col))

    def residual(xt, L, kg, col):
        rms_rstd(Yt.ap, [Yt], 8, onesb.ap, 1.0 / D, RMS_EPS, RSTD)
        for fc in range(8):
            t1 = tmp()
            STT("dve", t1.ap, Yt.ap[:, fc, :], MV(L, kg, fc, col), RSTD.ap, ALU.mult, ALU.mult, [Yt, mdv, RSTD], [t1])
            TTo("pool", xt.ap[:, fc, :], xt.ap[:, fc, :], t1.ap, ALU.add, [xt, t1], [xt])

    def load_w(dst, dview, ncols, kch):
        c0 = 0
        while c0 < ncols:
            c1 = min(ncols, c0 + 2048)
            for k0 in range(0, kch, 4):
                k1 = min(kch, k0 + 4)
                DMA("pool", dst.ap[:, k0:k1, c0:c1], dview[:, k0:k1, c0:c1], [TIN], [dst])
            c0 = c1

    def load_x(xt, s):
        DMA("sp", xt.ap.rearrange("p k t -> p (k t)"), XS.ap[s], [XS], [xt])
    def store_x(xt, s):
        DMA("sp", XS.ap[s], xt.ap.rearrange("p k t -> p (k t)"), [xt], [XS])

    def conv3(dst, src, srcbufs, src_is_psum, w0, w1, w2, nrows, rl):
        ACT(dst.ap, src, AF.Identity, srcbufs, [dst], scale=w1)
        dv = dst.ap.rearrange("p (r c) -> p r c", c=rl)
        sv = src.rearrange("p (r c) -> p r c", c=rl)
        STT("dve", dv[:, :, 1:rl], sv[:, :, 0:rl - 1], w0, dv[:, :, 1:rl], ALU.mult, ALU.add, srcbufs + [dst], [dst])
        STT("dve", dv[:, :, 0:rl - 1], sv[:, :, 1:rl], w2, dv[:, :, 0:rl - 1], ALU.mult, ALU.add, srcbufs + [dst], [dst])

    def grid(s):
        return (1, 256) if s == 0 else (4, 64)

    def pass_E1(L, segs):
        i = L // 2
        load_w(WA, ev_w_in[i].rearrange("(k p) n -> p k n", p=128), 2336, 8)
        DMA("pool", poolwb.ap, ev_pool_w[i].rearrange("g c d -> c g d"), [TIN], [poolwb])
        setoff(W_BASE + WA_BYTES)
        TM = tile([128, 2, D], F32, "E1_TM")
        PRS = tile([128, 16, SEG], F32, "E1_PRS")
        XP = tile([128, 4, 384], F32, "E1_XP")
        S2 = tile([128, 384], F32, "E1_S2"); S4 = tile([128, 384], F32, "E1_S4")
        S8 = tile([128, 384], F32, "E1_S8"); S16 = tile([128, 384], F32, "E1_S16")
        DB = tile([128, SEG], BF16, "E1_DB")
        YPT = tile([128, 4, SEG], BF16, "E1_YPT")
        stg_i = 0
        for s in segs:
            if s in (0, 1):
                S.op("pool", lambda e: e.memset(XP.ap, 0.0), writes=[XP.b])
            if s == 0:
                S.op("pool", lambda e: e.memset(PRS.ap, 0.0), writes=[PRS.b])
                S.op("pool", lambda e: e.memset(HALO.ap, 0.0), writes=[HALO.b])
            xt = XT[s % 2]
            col = 0 if s > 0 else 1
            if L == 0:
                src = ctx_in if s == 0 else x_in[(s - 1) * SEG:s * SEG]
                DMA("sp", TM.ap, src.rearrange("(h p) f -> p h f", p=128), [TIN], [TM])
                for fc in range(8):
                    bk = bank()
                    TRS([(bk.ap[:, hh * 128:(hh + 1) * 128], TM.ap[:, hh, fc * 128:(fc + 1) * 128]) for hh in range(2)],
                        ident, [TM, cst], [bk])
                    CP("act" if fc % 2 else "dve", xt.ap[:, fc, :], bk.ap[:, 0:SEG], [bk], [xt])
                store_x(xt, s)
            else:
                load_x(xt, s)
            norm_mod(xt, L, 0, 1, col)
            for gi, (st, sz) in enumerate(PGRP):
                bk = bank()
                MM(bk.ap[0:sz, 0:SEG], [(WA.ap[:, k, st:st + sz], Hb.ap[:, k, :]) for k in range(8)], [WA, Hb], [bk])
                CP("act", PRS.ap[0:sz, gi, :], bk.ap[0:sz, 0:SEG], [bk], [PRS])
            CP("pool", HALO.ap[:, :, s, 0], PRS.ap[:, :, 0], [PRS], [HALO])
            CP("pool", HALO.ap[:, :, s, 1], PRS.ap[:, :, SEG - 1], [PRS], [HALO])
            DMA("sp", PR.ap[s], PRS.ap.rearrange("p g t -> p (g t)"), [PRS], [PR])
            nr, rl = grid(s)
            pw = rl + 32
            invn = "inv_ctx" if s == 0 else "inv_lat"
            for gi in range(4):
                win = (2, 4, 8, 16)[gi]
                bk = bank()
                c0 = 1824 + gi * 128
                MM(bk.ap[:, 0:SEG], [(WA.ap[:, k, c0:c0 + 128], Hb.ap[:, k, :]) for k in range(8)], [WA, Hb], [bk])
                xpv = XP.ap[:, gi, 0:nr * pw].rearrange("p (r c) -> p r c", c=pw)
                CP("act", xpv[:, :, 16:16 + rl], bk.ap[:, 0:SEG].rearrange("p (r c) -> p r c", c=rl), [bk], [XP])
                prev = xpv; prevb = XP; sh = 1
                for (St, wn) in ((S2, 2), (S4, 4), (S8, 8), (S16, 16)):
                    if wn > win:
                        break
                    sv = St.ap[:, 0:nr * pw].rearrange("p (r c) -> p r c", c=pw)
                    lo = wn - 1
                    TTo("pool" if gi % 2 else "dve", sv[:, :, lo:pw], prev[:, :, lo:pw], prev[:, :, lo - sh:pw - sh], ALU.add, [prevb], [St])
                    prev = sv; prevb = St; sh = wn
                o = 16 + win // 2 - 1
                t1 = tmp()
                t1v = t1.ap.rearrange("p (r c) -> p r c", c=rl)
                iv = cst.ap[:, CO[invn] + gi * rl:CO[invn] + (gi + 1) * rl].unsqueeze(1).to_broadcast([128, nr, rl])
                TTo("dve", t1v, prev[:, :, o:o + rl], iv, ALU.mult, [prevb, cst], [t1])
                TTo("dve", DB.ap.rearrange("p (r c) -> p r c", c=rl), t1v, xpv[:, :, 16:16 + rl], ALU.subtract, [t1, XP], [DB])
                bk2 = bank()
                MM(bk2.ap[:, 0:SEG], [(poolwb.ap[:, gi, :], DB.ap)], [poolwb, DB], [bk2])
                ACT(YPT.ap[:, gi, :], bk2.ap[:, 0:SEG], AF.Identity, [bk2, pv], [YPT], scale=pvc(f"psc{L}", gi))
            DMA("sp", YP.ap[s], YPT.ap.rearrange("p k t -> p (k t)"), [YPT], [YP])

    def pass_prep(L, segs):
        i = L // 2
        DMA("pool", wupb.ap, ev_w_up[i].rearrange("d r c -> (d r) c"), [TIN], [wupb])
        for d in range(2):
            DMA("pool", aupb[d].ap, ev_a_up[i, d], [TIN], [aupb[d]])
        DMA("pool", gupb.ap, ev_g_up[i], [TIN], [gupb])
        m0 = pv.ap[:, PV[f"mu{L}_0"]:PV[f"mu{L}_0"] + 16]
        m1 = pv.ap[:, PV[f"mu{L}_1"]:PV[f"mu{L}_1"] + 16]
        dv = dvec.ap[:, i]
        TTo("dve", dv[:, 0:16], m0, m1, ALU.add, [pv], [dvec])
        TS("dve", dv[:, 0:16], dv[:, 0:16], -1.0, 1.0, ALU.mult, ALU.add, [dvec], [dvec])
        TS("dve", dv[:, 16:20], pv.ap[:, PV[f"ka{L}"]:PV[f"ka{L}"] + 4], -1.0, 1.0, ALU.mult, ALU.add, [pv], [dvec])
        setoff(W_BASE)
        PRM = [tile([128, 16, SEG], F32, f"P_PRM{k}") for k in range(2)]
        PRT1 = tile([128, 16, SEG + 2], F32, "P_PRT")
        SH = tile([128, 16, SEG], F32, "P_SH")
        KKt = tile([128, 4, SEG], F32, "P_KK")
        At = [tile([128, 4, SEG], F32, f"P_A{d}") for d in range(2)]
        KDt = [tile([128, 4, SEG], F32, f"P_KD{d}") for d in range(2)]
        Bt = [tile([128, 4, SEG], F32, f"P_B{d}") for d in range(2)]
        SG = [tile([128, 4, SEG], F32, f"P_SG{d}") for d in range(2)]
        CUM = tile([128, SEG], F32, "P_CUM")
        E = [tile([128, SEG], F32, f"P_E{k}") for k in range(4)]
        TLb = tile([64, SEG], BF16, "P_TLb")
        LAb = [tile([64, SEG], BF16, f"P_LAb{d}") for d in range(2)]
        LGb = tile([96, SEG], BF16, "P_LGb")
        SQb = tile([128, SEG], BF16, "P_SQb")
        FMo = [tile([128, 4, 1024], BF16, f"P_FMo{d}") for d in range(2)]
        KGo = [tile([128, 4, SEG], BF16, f"P_KGo{d}") for d in range(2)]
        BGo = [tile([128, 4, SEG], BF16, f"P_BGo{d}") for d in range(2)]
        Vb = tile([128, 4, SEG], BF16, "P_Vb")
        GCo = [tile([128, 4, 4], F32, f"P_GCo{d}") for d in range(2)]
        BGo2 = tile([128, 2, 4, SEG], BF16, "P_BGo2")
        TMo = [tile([128, 5, 512], BF16, f"P_TMo{k}") for k in range(2)]
        tmo_i = [0]

        def load_pr(s):
            DMA("sp", PRM[s % 2].ap.rearrange("p g t -> p (g t)"), PR.ap[s], [PR], [PRM[s % 2]])

        load_pr(segs[0])
        for si, s in enumerate(segs):
            if si + 1 < len(segs):
                load_pr(segs[si + 1])
            pt = PRT1
            t0 = segcol(s)
            c0ch = t0 // CH
            first = (s == 0 or s == 1); last = (s == 0 or s == NSEG - 1)
            CP("pool", pt.ap[:, :, 1:SEG + 1], PRM[s % 2].ap, [PRM[s % 2]], [pt])
            if first:
                S.op("pool", lambda e: e.memset(pt.ap[:, :, 0:1], 0.0), writes=[pt.b])
            else:
                CP("pool", pt.ap[:, :, 0], HALO.ap[:, :, s - 1, 1], [HALO], [pt])
            if last:
                S.op("pool", lambda e: e.memset(pt.ap[:, :, SEG + 1:SEG + 2], 0.0), writes=[pt.b])
            else:
                CP("pool", pt.ap[:, :, SEG + 1], HALO.ap[:, :, s + 1, 0], [HALO], [pt])
            for gi, (st, sz) in enumerate(PGRP):
                dst = SH.ap[0:sz, gi, :]
                ACT(dst, pt.ap[0:sz, gi, 1:SEG + 1], AF.Identity, [pt, dvec], [SH], scale=dvec.ap[0:sz, i, gi:gi + 1])
                STT("dve", dst, pt.ap[0:sz, gi, 0:SEG], pvc(f"mu{L}_0", gi, sz), dst, ALU.mult, ALU.add, [pt, pv, SH], [SH])
                STT("dve", dst, pt.ap[0:sz, gi, 2:SEG + 2], pvc(f"mu{L}_1", gi, sz), dst, ALU.mult, ALU.add, [pt, pv, SH], [SH])
            if KLIM <= 1:
                continue
            Rv = SH.ap[:, 0:4, :]; Kv = SH.ap[:, 4:8, :]; Vv = SH.ap[:, 8:12, :]
            ACT(TLb.ap, SH.ap[0:64, 12, :], AF.Tanh, [SH], [TLb])
            CP("pool", LAb[0].ap, SH.ap[0:64, 13, :], [SH], [LAb[0]])
            CP("pool", LAb[1].ap, SH.ap[0:64, 14, :], [SH], [LAb[1]])
            ACT(LGb.ap, SH.ap[0:96, 15, :], AF.Sigmoid, [SH], [LGb])
            CP("pool", Vb.ap, Vv, [SH], [Vb])
            for c4 in range(4):
                csl = slice(c4 * 128, (c4 + 1) * 128)
                bk = bank()
                MM(bk.ap[:, 0:SEG], [(gupb.ap[:, csl], LGb.ap)], [gupb, LGb], [bk])
                CP("act", BGo2.ap[:, 1, c4, :], bk.ap[:, 0:SEG], [bk], [BGo2])
                t1 = tmp()
                TS("dve", t1.ap, Kv[:, c4, :], pvc(f"kk{L}", c4), None, ALU.mult, None, [SH, pv], [t1])
                ACT(SQb.ap, t1.ap, AF.Square, [t1], [SQb])
                bk = bank()
                MM(bk.ap[:, 0:SEG], [(bdb.ap, SQb.ap)], [bdb, SQb], [bk])
                t2 = tmp()
                TS("dve", t2.ap, bk.ap[:, 0:SEG], 1e-24, None, ALU.max, None, [bk], [t2])
                ACT(t2.ap, t2.ap, AF.Sqrt, [t2], [t2])
                RECIP(t2.ap, t2.ap, [t2], [t2])
                TTo("dve", KKt.ap[:, c4, :], t1.ap, t2.ap, ALU.mult, [t1, t2], [KKt])
                for d in range(2):
                    bk = bank()
                    MM(bk.ap[:, 0:SEG], [(wupb.ap[32 * d:32 * d + 32, csl], TLb.ap[32 * d:32 * d + 32, :])], [wupb, TLb], [bk])
                    ACT(SG[d].ap[:, c4, :], bk.ap[:, 0:SEG], AF.Sigmoid, [bk, pv], [SG[d]], bias=pvc(f"w0{L}_{d}", c4))
                    bk = bank()
                    MM(bk.ap[:, 0:SEG], [(aupb[d].ap[:, csl], LAb[d].ap)], [aupb[d], LAb[d]], [bk])
                    ACT(At[d].ap[:, c4, :], bk.ap[:, 0:SEG], AF.Sigmoid, [bk, pv], [At[d]], bias=pvc(f"a0{L}_{d}", c4))
                    t3 = tmp()
                    TS("dve", t3.ap, At[d].ap[:, c4, :], pvc(f"ka{L}", c4), dvec.ap[:, i, 16 + c4:17 + c4], ALU.mult, ALU.add,
                       [At[d], pv, dvec], [t3])
                    TTo("pool", KDt[d].ap[:, c4, :], Kv[:, c4, :], t3.ap, ALU.mult, [SH, t3], [KDt[d]])
                    TTo("pool", Bt[d].ap[:, c4, :], KKt.ap[:, c4, :], At[d].ap[:, c4, :], ALU.mult, [KKt, At[d]], [Bt[d]])
                    S.op("dve", lambda e, d=d, c4=c4: e.tensor_tensor_scan(out=CUM.ap, data0=cc("restart", 256), data1=SG[d].ap[:, c4, :],
                                                                            initial=0.0, op0=ALU.mult, op1=ALU.add),
                         reads=[cst.b, SG[d].b], writes=[CUM.b])
                    cumv = CUM.ap.rearrange("p (c t) -> p c t", t=CH)
                    sgv = SG[d].ap[:, c4, :].rearrange("p (c t) -> p c t", t=CH)
                    totb = cumv[:, :, CH - 1:CH].to_broadcast([128, 4, CH])
                    e0, e1, e2, e3 = E
                    if d == 1:
                        t4 = tmp()
                        t4v = t4.ap.rearrange("p (c t) -> p c t", t=CH)
                        TTo("dve", t4v, totb, cumv, ALU.subtract, [CUM], [t4])
                        ACT(GCo[d].ap[:, c4, :], cumv[:, :, CH - 1], AF.Exp, [CUM], [GCo[d]], scale=-EM05)
                        TTo("dve", CUM.ap, t4.ap, SG[d].ap[:, c4, :], ALU.add, [t4, SG[d]], [CUM])
                        t5 = tmp()
                        t5v = t5.ap.rearrange("p (c t) -> p c t", t=CH)
                        TTo("dve", t5v, cumv[:, :, 0:1].to_broadcast([128, 4, CH]), cumv, ALU.subtract, [CUM], [t5])
                        tmc = t5
                    else:
                        ACT(GCo[d].ap[:, c4, :], cumv[:, :, CH - 1], AF.Exp, [CUM], [GCo[d]], scale=-EM05)
                        t5 = tmp()
                        t5v = t5.ap.rearrange("p (c t) -> p c t", t=CH)
                        TTo("dve", t5v, totb, cumv, ALU.subtract, [CUM], [t5])
                        tmc = t5
                    t6 = tmp()
                    TTo("pool", t6.ap, CUM.ap, SG[d].ap[:, c4, :], ALU.subtract, [CUM, SG[d]], [t6])
                    ACT(e0.ap, CUM.ap, AF.Exp, [CUM], [e0], scale=-EM05)
                    ACT(e1.ap, t6.ap, AF.Exp, [t6], [e1], scale=-EM05)
                    ACT(e2.ap, CUM.ap, AF.Exp, [CUM], [e2], scale=EM05)
                    ACT(e3.ap, tmc.ap, AF.Exp, [tmc], [e3], scale=-EM05)
                    kqv = FMo[d].ap[:, c4, 0:512].rearrange("p (c a t) -> p c a t", a=2, t=CH)
                    TTo("dve", kqv[:, :, 0, :], KKt.ap[:, c4, :].rearrange("p (c t) -> p c t", t=CH),
                        e1.ap.rearrange("p (c t) -> p c t", t=CH), ALU.mult, [KKt, e1], [FMo[d]])
                    TTo("dve", kqv[:, :, 1, :], Rv[:, c4, :].rearrange("p (c t) -> p c t", t=CH),
                        e0.ap.rearrange("p (c t) -> p c t", t=CH), ALU.mult, [SH, e0], [FMo[d]])
                    TTo("pool", FMo[d].ap[:, c4, 512:768], KDt[d].ap[:, c4, :], e2.ap, ALU.mult, [KDt[d], e2], [FMo[d]])
                    TTo("pool", FMo[d].ap[:, c4, 768:1024], Bt[d].ap[:, c4, :], e2.ap, ALU.mult, [Bt[d], e2], [FMo[d]])
                    TTo("dve", KGo[d].ap[:, c4, :], KDt[d].ap[:, c4, :], e3.ap, ALU.mult, [KDt[d], e3], [KGo[d]])
                    TTo("pool", BGo[d].ap[:, c4, :], Bt[d].ap[:, c4, :], e3.ap, ALU.mult, [Bt[d], e3], [BGo[d]])
                t7 = tmp()
                TTo("pool", t7.ap, KDt[0].ap[:, c4, :], KDt[1].ap[:, c4, :], ALU.add, [KDt[0], KDt[1]], [t7])
                STT("dve", SQb.ap, t7.ap, pvc(f"rk{L}", c4), Rv[:, c4, :], ALU.mult, ALU.mult, [t7, pv, SH], [SQb])
                bk = bank()
                MM(bk.ap[:, 0:SEG], [(bdb.ap, SQb.ap)], [bdb, SQb], [bk])
                TTo("dve", BGo2.ap[:, 0, c4, :], bk.ap[:, 0:SEG], Vv[:, c4, :], ALU.mult, [bk, SH], [BGo2])
            if KLIM <= 2:
                continue
            for d in range(2):
                DMA("sp", FM[d].ap[s], FMo[d].ap.rearrange("p j n -> p (j n)"), [FMo[d]], [FM[d]])
                DMA("sp", GC[d].ap[s], GCo[d].ap.rearrange("p j c -> p (j c)"), [GCo[d]], [GC[d]])
            DMA("sp", BGD.ap[s], BGo2.ap.rearrange("p a j t -> p (a j t)"), [BGo2], [BGD])
            if KLIM <= 3:
                continue
            for hh in range(2):
                to = TMo[tmo_i[0] % 2]; tmo_i[0] += 1
                for qi, srct in enumerate((Vb, KGo[0], BGo[0], KGo[1], BGo[1])):
                    bb = bankbf()
                    TRS([(bb.ap[:, c4 * 128:(c4 + 1) * 128], srct.ap[:, c4, hh * 128:(hh + 1) * 128]) for c4 in range(4)],
                        identb.ap, [srct, identb], [bb])
                    CP("act" if qi % 2 else "dve", to.ap[:, qi, :], bb.ap, [bb], [to])
                DMA("sp", TMD.ap[t0 + hh * 128:t0 + (hh + 1) * 128].rearrange("t q n -> t (q n)"),
                    to.ap.rearrange("p q n -> p (q n)"), [to], [TMD])

    def pass_scan(L, d, store_ctx, segs_override=None):
        setoff(W_BASE)
        LD = []
        for k in range(2):
            LD.append(dict(FM=tile([64, 4, 2, 1024], BF16, f"S_FM{k}"), TV=tile([64, 4, 512], BF16, f"S_TV{k}"),
                           TG=tile([64, 4, 2, 512], BF16, f"S_TG{k}"), GC=tile([64, 4, 2, 4], F32, f"S_GC{k}")))
        AT1s = [tile([64, 8, 128], BF16, f"S_AT1_{c}") for c in range(4)]
        AT2s = [tile([64, 8, 128], BF16, f"S_AT2_{c}") for c in range(4)]
        ZFs = [[tile([64, 8, 128], F32, f"S_ZF{c}_{k}") for k in range(2)] for c in range(4)]
        NNs = [[tile([64, 8, 64], F32, f"S_NN{c}_{k}") for k in range(2)] for c in range(4)]
        RHSb = tile([64, 8, 64], F32, "S_RHS"); UNb = tile([64, 8, 64], BF16, "S_UN")
        Hf = tile([64, 8, 64], F32, "S_H"); Hh = tile([64, 8, 64], BF16, "S_Hb")
        YO = [tile([64, 8, SEG], F32, f"S_YO{k}") for k in range(2)]
        S.op("dve", lambda e: e.memset(Hf.ap, 0.0), writes=[Hf.b])
        S.op("pool", lambda e: e.memset(Hh.ap, 0.0), writes=[Hh.b])
        m1 = cst.ap[0:64, CO["m1b" if d else "m1f"]:CO["m1b" if d else "m1f"] + 128].unsqueeze(1).to_broadcast([64, 8, 128])
        m3 = cst.ap[0:64, CO["m3b" if d else "m3f"]:CO["m3b" if d else "m3f"] + 64].unsqueeze(1).to_broadcast([64, 8, 64])
        id64 = cst.ap[0:64, CO["id64"]:CO["id64"] + 64].unsqueeze(1).to_broadcast([64, 8, 64])
        segs = [0] + (list(range(NSEG - 1, 0, -1)) if d else list(range(1, NSEG)))
        if segs_override is not None:
            segs = segs_override

        def load(si):
            s = segs[si]; ld = LD[si % 2]
            t0 = segcol(s)
            for hh in range(2):
                DMA("sp", ld["FM"].ap[:, :, hh, :], FM[d].ap[s, hh * 64:(hh + 1) * 64, :].rearrange("k (j n) -> k j n", j=4),
                    [FM[d]], [ld["FM"]])
                DMA("sp", ld["GC"].ap[:, :, hh, :], GC[d].ap[s, hh * 64:(hh + 1) * 64, :].rearrange("k (j c) -> k j c", j=4),
                    [GC[d]], [ld["GC"]])
            tmv = TMD.ap[t0:t0 + SEG].rearrange("(c s) q n -> s c q n", s=64)
            DMA("sp", ld["TV"].ap, tmv[:, :, 0, :], [TMD], [ld["TV"]])
            DMA("sp", ld["TG"].ap, tmv[:, :, 1 + 2 * d:3 + 2 * d, :], [TMD], [ld["TG"]])

        load(0)
        for si, s in enumerate(segs):
            if si + 1 < len(segs):
                load(si + 1)
            ld = LD[si % 2]
            yo = YO[si % 2]
            if s == 0:
                if si > 0:
                    pass
            chunks = [3, 2, 1, 0] if d else [0, 1, 2, 3]
            fmv = ld["FM"].ap.rearrange("k j two n -> k (j two) n")
            def hv(pb, h, w):
                return pb[h // 4].ap[0:64, (h % 4) * w:(h % 4 + 1) * w]
            for c in chunks:
                kq = fmv[:, :, c * 128:(c + 1) * 128]
                kh = fmv[:, :, 512 + c * CH:512 + (c + 1) * CH]
                bh = fmv[:, :, 768 + c * CH:768 + (c + 1) * CH]
                AT1 = AT1s[c]; AT2 = AT2s[c]
                pa1 = [bank(), bank()]; pa2 = [bank(), bank()]; pa3 = bank()
                MMS([(hv(pa1, h, 128), [(kh[:, h, :], kq[:, h, :])]) for h in range(8)], [ld["FM"]], pa1)
                MMS([(hv(pa2, h, 128), [(bh[:, h, :], kq[:, h, :])]) for h in range(8)], [ld["FM"]], pa2)
                MMS([(pa3.ap[0:64, h * 64:(h + 1) * 64], [(kq[:, h, 0:64], bh[:, h, :])]) for h in range(8)], [ld["FM"]], [pa3])
                zf, nn = ZFs[c][0], NNs[c][0]
                for hf in range(2):
                    TTo("dve", AT1.ap[:, hf * 4:(hf + 1) * 4, :], pa1[hf].ap[0:64, :].rearrange("p (h x) -> p h x", x=128),
                        m1[:, 0:4, :], ALU.mult, [pa1[hf], cst], [AT1])
                    TTo("dve", AT2.ap[:, hf * 4:(hf + 1) * 4, :], pa2[hf].ap[0:64, :].rearrange("p (h x) -> p h x", x=128),
                        m1[:, 0:4, :], ALU.mult, [pa2[hf], cst], [AT2])
                    TTo("dve", zf.ap[:, hf * 4:(hf + 1) * 4, 0:64], pa2[hf].ap[0:64, :].rearrange("p (h x) -> p h x", x=128)[:, :, 0:64],
                        m1[:, 0:4, 0:64], ALU.mult, [pa2[hf], cst], [zf])
                TTo("dve", nn.ap, pa3.ap[0:64, :].rearrange("p (h x) -> p h x", x=64), m3, ALU.mult, [pa3, cst], [nn])
                TTo("pool", zf.ap[:, :, 64:128], id64, zf.ap[:, :, 0:64], ALU.subtract, [cst, zf], [zf])
            for lev in range(6):
                cur_i = lev % 2
                last = (lev == 5)
                for c in chunks:
                    zf = ZFs[c][cur_i]; nn = NNs[c][cur_i]; zf2 = ZFs[c][1 - cur_i]; nn2 = NNs[c][1 - cur_i]
                    pz = [bank(), bank()]
                    if lev == 0:
                        MMS([(hv(pz, h, 128)[:, 0:64], [(nn.ap[:, h, :], zf.ap[:, h, 0:64])]) for h in range(8)], [nn, zf], pz)
                    elif not last:
                        MMS([(hv(pz, h, 128), [(nn.ap[:, h, :], zf.ap[:, h, :])]) for h in range(8)], [nn, zf], pz)
                    else:
                        MMS([(hv(pz, h, 128)[:, 64:128], [(nn.ap[:, h, :], zf.ap[:, h, 64:128])]) for h in range(8)], [nn, zf], pz)
                    if not last:
                        pn = bank()
                        MMS([(pn.ap[0:64, h * 64:(h + 1) * 64], [(zf.ap[:, h, 0:64], nn.ap[:, h, :])]) for h in range(8)], [nn, zf], [pn])
                        CP("act", nn2.ap, pn.ap[0:64, :].rearrange("p (h x) -> p h x", x=64), [pn], [nn2])
                    for hf in range(2):
                        pzv = pz[hf].ap[0:64, :].rearrange("p (h x) -> p h x", x=128)
                        hs = slice(hf * 4, (hf + 1) * 4)
                        if not last:
                            CP("act", zf2.ap[:, hs, 0:64], pzv[:, :, 0:64], [pz[hf]], [zf2])
                        if lev == 0:
                            CP("pool", zf2.ap[:, hs, 64:128], zf.ap[:, hs, 64:128], [zf], [zf2])
                        else:
                            TTo("dve", zf2.ap[:, hs, 64:128], pzv[:, :, 64:128], zf.ap[:, hs, 64:128], ALU.add, [pz[hf], zf], [zf2])
            for c in chunks:
                kq = fmv[:, :, c * 128:(c + 1) * 128]
                vt = ld["TV"].ap[:, c, :].rearrange("s (h v) -> s h v", v=64)
                kg = ld["TG"].ap[:, c, 0, :].rearrange("s (h v) -> s h v", v=64)
                bg = ld["TG"].ap[:, c, 1, :].rearrange("s (h v) -> s h v", v=64)
                AT1 = AT1s[c]; AT2 = AT2s[c]
                Fm = ZFs[c][0]
                pr = bank()
                MMS([(pr.ap[0:64, h * 64:(h + 1) * 64], [(kq[:, h, 0:64], Hh.ap[:, h, :]), (AT1.ap[:, h, 0:64], vt[:, h, :])])
                     for h in range(8)], [ld["FM"], Hh, AT1, ld["TV"]], [pr])
                CP("act", RHSb.ap, pr.ap[0:64, :].rearrange("p (h x) -> p h x", x=64), [pr], [RHSb])
                pu = bank()
                MMS([(pu.ap[0:64, h * 64:(h + 1) * 64], [(Fm.ap[:, h, 64:128], RHSb.ap[:, h, :])]) for h in range(8)], [Fm, RHSb], [pu])
                S.op("act", lambda e, pu=pu: e.mul(out=UNb.ap, in_=pu.ap[0:64, :].rearrange("p (h x) -> p h x", x=64), mul=-1.0),
                     reads=[pu.b], writes=[UNb.b])
                py = bank()
                MMS([(py.ap[0:64, h * 64:(h + 1) * 64],
                      [(Hh.ap[:, h, :], kq[:, h, 64:128]), (vt[:, h, :], AT1.ap[:, h, 64:128]), (UNb.ap[:, h, :], AT2.ap[:, h, 64:128])])
                     for h in range(8)], [Hh, ld["FM"], ld["TV"], AT1, UNb, AT2], [py])
                CP("act", yo.ap[:, :, c * CH:(c + 1) * CH], py.ap[0:64, :].rearrange("p (h x) -> p h x", x=64), [py], [yo])
                ph = bank()
                MMS([(ph.ap[0:64, h * 64:(h + 1) * 64], [(kg[:, h, :], vt[:, h, :]), (bg[:, h, :], UNb.ap[:, h, :])]) for h in range(8)],
                    [ld["TG"], ld["TV"], UNb], [ph])
                gcb = ld["GC"].ap.rearrange("k j two c -> k (j two) c")[:, :, c:c + 1].to_broadcast([64, 8, 64])
                TTo("dve", Hf.ap, Hf.ap, gcb, ALU.mult, [Hf, ld["GC"]], [Hf])
                TTo("dve", Hf.ap, Hf.ap, ph.ap[0:64, :].rearrange("p (h x) -> p h x", x=64), ALU.add, [Hf, ph], [Hf])
                CP("pool", Hh.ap, Hf.ap, [Hf], [Hh])
            if s > 0 or store_ctx:
                t0 = segcol(s)
                for hh in range(2):
                    DMA("sp", YD[d].ap[s, hh * 64:(hh + 1) * 64, :].rearrange("v (j t) -> v j t", j=4),
                        yo.ap.rearrange("v (j two) t -> v j two t", two=2)[:, :, hh, :], [yo], [YD[d]])

    def pass_E3(L, segs):
        i = L // 2
        load_w(WB, ev_w_out[i].rearrange("(k p) n -> p k n", p=128), D, 8)
        setoff(W_BASE)
        Y0 = tile([128, 4, SEG], F32, "E3_Y0"); Y1 = tile([128, 4, SEG], F32, "E3_Y1")
        BGt = tile([128, 2, 4, SEG], BF16, "E3_BG")
        YC = tile([128, 4, SEG], F32, "E3_YC")
        MIX = ACTF
        for s in segs:
            xt = XT[s % 2]
            col = 0 if s > 0 else 1
            t0 = segcol(s)
            load_x(xt, s)
            DMA("sp", Y0.ap.rearrange("p k t -> p (k t)"), YD[0].ap[s], [YD[0]], [Y0])
            DMA("sp", Y1.ap.rearrange("p k t -> p (k t)"), YD[1].ap[s], [YD[1]], [Y1])
            DMA("sp", BGt.ap.rearrange("p a k t -> p (a k t)"), BGD.ap[s], [BGD], [BGt])
            DMA("sp", MIX.ap[:, 4:8, :].rearrange("p k t -> p (k t)"), YP.ap[s], [YP], [MIX])
            TTo("pool", Y0.ap, Y0.ap, Y1.ap, ALU.add, [Y0, Y1], [Y0])
            for c4 in range(4):
                CP("act", SQ.ap[:, c4, :], Y0.ap[:, c4, :], [Y0], [SQ])
                bk = bank()
                MM(bk.ap[:, 0:SEG], [(bdb.ap, SQ.ap[:, c4, :])], [bdb, SQ], [bk])
                STT("dve", YC.ap[:, c4, :], bk.ap[:, 0:SEG], -1.0 / 64, Y0.ap[:, c4, :], ALU.mult, ALU.add, [bk, Y0], [YC])
                ACT(SQ.ap[:, c4, :], YC.ap[:, c4, :], AF.Square, [YC], [SQ])
                bk = bank()
                MM(bk.ap[:, 0:SEG], [(bdb.ap, SQ.ap[:, c4, :])], [bdb, SQ], [bk])
                t1 = tmp()
                TS("dve", t1.ap, bk.ap[:, 0:SEG], 1.0 / 64, GN_EPS, ALU.mult, ALU.add, [bk], [t1])
                ACT(t1.ap, t1.ap, AF.Sqrt, [t1], [t1])
                RECIP(t1.ap, t1.ap, [t1], [t1])
                TTo("dve", YC.ap[:, c4, :], YC.ap[:, c4, :], t1.ap, ALU.mult, [YC, t1], [YC])
                ACT(YC.ap[:, c4, :], YC.ap[:, c4, :], AF.Identity, [YC, pv], [YC], scale=pvc(f"gnw{L}", c4), bias=pvc(f"gnb{L}", c4))
                TTo("pool", YC.ap[:, c4, :], YC.ap[:, c4, :], BGt.ap[:, 0, c4, :], ALU.add, [YC, BGt], [YC])
                TTo("dve", MIX.ap[:, c4, :], YC.ap[:, c4, :], BGt.ap[:, 1, c4, :], ALU.mult, [YC, BGt], [MIX])
            for oc in range(8):
                bk = bank()
                MM(bk.ap[:, 0:SEG], [(WB.ap[:, k, oc * 128:(oc + 1) * 128], MIX.ap[:, k, :]) for k in range(8)], [WB, MIX], [bk])
                CP("act", Yt.ap[:, oc, :], bk.ap[:, 0:SEG], [bk], [Yt])
            residual(xt, L, 2, col)
            store_x(xt, s)

    def pass_O(L, segs):
        i = L // 2
        load_w(WA, od_w_in[i].rearrange("(k p) n -> p k n", p=128), 3 * D, 8)
        load_w(WB, od_w_out[i].rearrange("(k p) n -> p k n", p=128), D, 8)
        MIX = ACTF
        for s in segs:
            xt = XT[s % 2]
            col = 0 if s > 0 else 1
            nr, rl = grid(s)
            load_x(xt, s)
            norm_mod(xt, L, 0, 1, col)
            for j in range(8):
                pb = bank(); pc = bank(); pu = bank()
                for (bk, c0) in ((pb, j * 128), (pc, D + j * 128), (pu, 2 * D + j * 128)):
                    MM(bk.ap[:, 0:SEG], [(WA.ap[:, k, c0:c0 + 128], Hb.ap[:, k, :]) for k in range(8)], [WA, Hb], [bk])
                t1 = tmp(); t2 = tmp(); t3 = tmp()
                CP("act", t1.ap, pu.ap[:, 0:SEG], [pu], [t1])
                TTo("dve", t2.ap, pc.ap[:, 0:SEG], t1.ap, ALU.mult, [pc, t1], [t2])
                conv3(t3, t2.ap, [t2, pv], False, pvc(f"oconv{L}_0", j), pvc(f"oconv{L}_1", j), pvc(f"oconv{L}_2", j), nr, rl)
                TTo("dve", MIX.ap[:, j, :], pb.ap[:, 0:SEG], t3.ap, ALU.mult, [pb, t3], [MIX])
            for oc in range(8):
                bk = bank()
                MM(bk.ap[:, 0:SEG], [(WB.ap[:, k, oc * 128:(oc + 1) * 128], MIX.ap[:, k, :]) for k in range(8)], [WB, MIX], [bk])
                CP("act", Yt.ap[:, oc, :], bk.ap[:, 0:SEG], [bk], [Yt])
            residual(xt, L, 2, col)
            store_x(xt, s)

    def pass_F(L, segs):
        load_w(WA, ffn_w_up[L].rearrange("(k p) n -> p k n", p=128), 2 * DFF, 8)
        load_w(WB, ffn_w_down[L].rearrange("(k p) n -> p k n", p=128), D, NFF)
        for s in segs:
            xt = XT[s % 2]
            col = 0 if s > 0 else 1
            nr, rl = grid(s)
            load_x(xt, s)
            norm_mod(xt, L, 3, 4, col)
            for j in range(NFF):
                pc = bank(); pg = bank()
                for (bk, c0) in ((pc, j * 128), (pg, DFF + j * 128)):
                    MM(bk.ap[:, 0:SEG], [(WA.ap[:, k, c0:c0 + 128], Hb.ap[:, k, :]) for k in range(8)], [WA, Hb], [bk])
                t1 = tmp(); t2 = tmp()
                conv3(t1, pc.ap[:, 0:SEG], [pc, pv], True, pvc(f"fconv{L}_0", j), pvc(f"fconv{L}_1", j), pvc(f"fconv{L}_2", j), nr, rl)
                ACT(t2.ap, t1.ap, AF.Silu, [t1], [t2])
                TTo("dve", ACTF.ap[:, j, :], pg.ap[:, 0:SEG], t2.ap, ALU.mult, [pg, t2], [ACTF])
            for oc in range(8):
                bk = bank()
                MM(bk.ap[:, 0:SEG], [(WB.ap[:, k, oc * 128:(oc + 1) * 128], ACTF.ap[:, k, :]) for k in range(NFF)], [WB, ACTF], [bk])
                CP("act", Yt.ap[:, oc, :], bk.ap[:, 0:SEG], [bk], [Yt])
            residual(xt, L, 5, col)
            store_x(xt, s)

    def pass_out():
        setoff(W_BASE)
        OT = [tile([128, 2, D], F32, f"O_OT{k}") for k in range(2)]
        for s in range(1, NSEG):
            xt = XT[s % 2]
            ot = OT[s % 2]
            load_x(xt, s)
            for hh in range(2):
                for q in range(2):
                    bk = bank()
                    TRS([(bk.ap[:, f * 128:(f + 1) * 128], xt.ap[:, q * 4 + f, hh * 128:(hh + 1) * 128]) for f in range(4)],
                        ident, [xt, cst], [bk])
                    CP("act" if q else "dve", ot.ap[:, hh, q * 512:(q + 1) * 512], bk.ap, [bk], [ot])
            DMA("sp", out_d[(s - 1) * SEG:s * SEG].rearrange("(h p) f -> p h f", p=128), ot.ap, [ot], [OUTB])

    ALLS = list(range(NSEG)); LAT = list(range(1, NSEG))
    plist = []
    for L in range(DEPTH):
        ctx_later = L in (0, 1)
        if L % 2 == 0:
            plist.append((f"E1_{L}", lambda L=L: pass_E1(L, ALLS)))
            plist.append((f"prep_{L}", lambda L=L: pass_prep(L, ALLS)))
            plist.append((f"scan0_{L}", lambda L=L, c=ctx_later: pass_scan(L, 0, c)))
            plist.append((f"scan1_{L}", lambda L=L, c=ctx_later: pass_scan(L, 1, c)))
            plist.append((f"E3_{L}", lambda L=L, c=ctx_later: pass_E3(L, ALLS if c else LAT)))
        else:
            plist.append((f"O_{L}", lambda L=L, c=ctx_later: pass_O(L, ALLS if c else LAT)))
        plist.append((f"F_{L}", lambda L=L, c=ctx_later: pass_F(L, ALLS if c else LAT)))
    plist.append(("out", pass_out))
    if debug_stop in ("mod", "mod1"):
        plist = []
    if debug_stop == "prep_only":
        plist = [("prep_only", lambda: pass_prep(0, list(range(int(os.environ.get("PSEGS", "2"))))))]
    if debug_stop == "mini":
        plist = [("a", lambda: pass_E1(0, [0, 1])), ("b", lambda: pass_prep(0, [0, 1])),
                 ("c", lambda: pass_scan(0, 0, True, [0, 1])), ("mini", lambda: pass_scan(0, 1, True, [0]))]
    if debug_stop == "mini2":
        plist = [("a", lambda: pass_E1(0, [0, 1])), ("b", lambda: pass_prep(0, [0, 1])),
                 ("c", lambda: pass_scan(0, 0, True, [0, 1])), ("d", lambda: pass_scan(0, 1, True, [0])),
                 ("e", lambda: pass_E3(0, [0])), ("f", lambda: pass_F(0, [0])), ("g", lambda: pass_O(1, [0])), ("mini2", lambda: pass_F(1, [0]))]
    if debug_stop == "testB":
        plist = [("f", lambda: pass_F(0, [1])), ("g", lambda: pass_O(1, [1])), ("testB", lambda: pass_F(1, [1]))]
    if debug_stop == "testC":
        plist = [("testC", pass_out)]
    if debug_stop == "scan_only":
        plist = [("scan_only", lambda: pass_scan(0, 0, True))]
    snaps = {}
    for name, fn in plist:
        fn()
        if debug_stop in ("mini2", "testB") and name in ("e", "f", "g", "mini2", "testB"):
            sn = nc.dram_tensor("snap_" + name, [128, 8 * SEG], F32, kind="Internal").ap()
            sb = T(sn, Buf("snap_" + name))
            DMA("sp", sn, XS.ap[0 if debug_stop == "mini2" else 1], [XS], [sb])
            snaps[name] = sb
        if debug_stop == name:
            break
    if debug_stop:
        dbg = nc.dram_tensor("dbg_mdv", [128, DEPTH * 6 * 8 * 2], F32, kind="ExternalOutput").ap()
        dbt = T(dbg, Buf("dbg"))
        DMA("sp", dbg, mdv.ap.rearrange("p a b c d -> p (a b c d)"), [mdv], [dbt])
        fin = [t.b for t in scr_list if t.b.lw is not None and t.b.name in debug_outs] + [dbt.b]
        if OUTB.b.lw is not None:
            fin.append(OUTB.b)
        S.finish("sp", fin)
    else:
        S.finish("sp", [OUTB.b])
    S.emit()
    return nc

_NC_CACHE = {}

def kernel(**inputs):
    inp = {k: np.asarray(v) for k, v in inputs.items()}
    if "nc" not in _NC_CACHE:
        _NC_CACHE["nc"] = build_program()
    nc = _NC_CACHE["nc"]
    f = lambda a: np.ascontiguousarray(a, dtype=np.float32)
    shared = {k: f(inp[k]) for k in ("w_mod", "ffn_w_up", "ffn_w_down", "ev_w_in", "ev_w_out", "ev_w_up", "ev_a_up",
                                     "ev_g_up", "ev_pool_w", "od_w_in", "od_w_out")}
    shared["consts"] = CONSTS
    in_maps = []
    for b in range(8):
        m = dict(shared)
        m["x"] = f(inp["x"][b]); m["ctx"] = f(inp["ctx"][b]); m["pvec"] = build_pvec(inp, b)
        in_maps.append(m)
    res = run_bass_kernel_spmd(nc, in_maps, core_ids=list(range(8)))
    return np.stack([np.asarray(r["out"], dtype=np.float32) for r in res.results], axis=0)
```

```python
import os
import numpy as np
import concourse.bass as bass
import concourse.mybir as mybir
from concourse.bass_utils import run_bass_kernel_spmd

F32 = mybir.dt.float32
BF16 = mybir.dt.bfloat16
AF = mybir.ActivationFunctionType
ALU = mybir.AluOpType

D = 1024
SEQ = 4096
CTX = 256
TALL = SEQ + CTX
SEG = 256
NSEG = TALL // SEG
CH = 64
NCH = TALL // CH
DFF = 2816
NFF = DFF // 128
DEPTH = 4
RMS_EPS = 1e-6
GN_EPS = 64e-5
EM05 = float(np.exp(-0.5))
KLIM = int(os.environ.get('KLIM', '99'))

PGRP = [(i * 128, 128) for i in range(12)] + [(1536, 64), (1600, 64), (1664, 64), (1728, 96)]

CO = {}
def _build_consts():
    cols = []
    def add(name, arr):
        CO[name] = sum(a.shape[1] for a in cols)
        cols.append(arr.astype(np.float32))
    p = np.arange(128)[:, None]
    j = np.arange(128)[None, :]
    add("ident", (p == j))
    add("bd", ((p // 64) == (j // 64)))
    add("ones", np.ones((128, 128)))
    add("restart", np.tile(((np.arange(256) % 64) != 0)[None, :], (128, 1)))
    def inv(row_len):
        pos = np.arange(row_len)
        out = []
        for win in (2, 4, 8, 16):
            lo = np.clip(pos - win // 2, 0, row_len)
            hi = np.clip(pos + win // 2, 0, row_len)
            out.append(1.0 / (hi - lo))
        return np.tile(np.concatenate(out)[None, :], (128, 1))
    add("inv_lat", inv(64))
    add("inv_ctx", inv(256))
    s = np.arange(128)[:, None] % 64
    t = np.arange(64)[None, :]
    add("m1f", np.concatenate([(s < t), (s <= t)], 1))
    add("m1b", np.concatenate([(s > t), (s >= t)], 1))
    add("m3f", (t < s))
    add("m3b", (t > s))
    add("id64", (s == t))
    return np.concatenate(cols, 1)
CONSTS = _build_consts()
NCONST = CONSTS.shape[1]

PV = {}
def _pv_layout():
    n = 0
    def add(name, w):
        nonlocal n
        PV[name] = n
        n += w
    add("c", 8); add("cctx", 8)
    for L in range(DEPTH):
        for jn in range(4):
            add(f"g{L}_{jn}", 8)
        add(f"bmod{L}", 48)
        for tp in range(3):
            add(f"fconv{L}_{tp}", NFF)
        if L % 2 == 0:
            add(f"mu{L}_0", 16); add(f"mu{L}_1", 16)
            for d in range(2):
                add(f"w0{L}_{d}", 4); add(f"a0{L}_{d}", 4)
            for nm in ("kk", "ka", "rk", "gnw", "gnb", "psc"):
                add(f"{nm}{L}", 4)
        else:
            for tp in range(3):
                add(f"oconv{L}_{tp}", 8)
    return n
NPV = _pv_layout()

def _colmajor(v):
    return np.ascontiguousarray(np.asarray(v, np.float32).reshape(-1, 128).T)

def build_pvec(inp, b):
    t = np.zeros((128, NPV), np.float32)
    def put(name, arr):
        t[:, PV[name]:PV[name] + arr.shape[1]] = arr
    put("c", _colmajor(inp["c"][b])); put("cctx", _colmajor(inp["c_ctx"]))
    for L in range(DEPTH):
        i = L // 2
        for jn in range(4):
            put(f"g{L}_{jn}", _colmajor(inp["norm_g"][L, jn]))
        put(f"bmod{L}", _colmajor(inp["b_mod"][L]))
        for tp in range(3):
            put(f"fconv{L}_{tp}", _colmajor(inp["ffn_conv"][L, tp]))
        if L % 2 == 0:
            for m in range(2):
                a = np.zeros((128, 16), np.float32)
                for gi, (st, sz) in enumerate(PGRP):
                    a[:sz, gi] = inp["ev_mu"][i, m, st:st + sz]
                put(f"mu{L}_{m}", a)
            for d in range(2):
                put(f"w0{L}_{d}", _colmajor(inp["ev_w0"][i, d]))
                put(f"a0{L}_{d}", _colmajor(inp["ev_a0"][i, d]))
            put(f"kk{L}", _colmajor(inp["ev_k_k"][i])); put(f"ka{L}", _colmajor(inp["ev_k_a"][i]))
            put(f"rk{L}", _colmajor(inp["ev_r_k"][i].reshape(-1)))
            put(f"gnw{L}", _colmajor(inp["ev_gn_w"][i])); put(f"gnb{L}", _colmajor(inp["ev_gn_b"][i]))
            put(f"psc{L}", _colmajor(inp["ev_pool_scale"][i]))
        else:
            for tp in range(3):
                put(f"oconv{L}_{tp}", _colmajor(inp["od_conv"][i, tp]))
    return t

class Buf:
    __slots__ = ("name", "lw", "rd", "dsem", "dcnt", "rng", "al")
    def __init__(self, name, rng=None):
        self.name = name; self.lw = None; self.rd = []; self.dsem = None; self.dcnt = 0
        self.rng = rng; self.al = []

class Sched:
    ENGS = ("pe", "act", "dve", "pool", "sp")
    def __init__(self, nc):
        self.nc = nc
        self.ops = {e: [] for e in self.ENGS}
        self.cnt = {e: 0 for e in self.ENGS}
        self.waited = {e: {} for e in self.ENGS}
        self.sems = {e: nc.alloc_semaphore("s_" + e) for e in self.ENGS}
        self.key = {e: e for e in self.ENGS}
        self.epoch = {e: 0 for e in self.ENGS}
        self.nsem = 0
        self.sb = []
    def sbuf(self, name, lo, hi):
        b = Buf(name, (lo, hi))
        for o in self.sb:
            if o.rng[0] < hi and lo < o.rng[1]:
                o.al.append(b); b.al.append(o)
        self.sb.append(b)
        return b
    def _waits(self, eng, evs, pe_self=False):
        w = self.waited[eng]; best = {}
        for ev in evs:
            if ev is None:
                continue
            k, v = ev
            if pe_self and k.startswith("pe"):
                continue
            if w.get(k, 0) >= v:
                continue
            if best.get(k, 0) < v:
                best[k] = v
        for k, v in best.items():
            w[k] = v
        return list(best.items())
    def _gather(self, reads, writes):
        evs = []
        for b in reads:
            evs.append(b.lw)
            for o in b.al:
                evs.append(o.lw)
        for b in writes:
            evs.append(b.lw); evs.extend(b.rd)
            for o in b.al:
                evs.append(o.lw); evs.extend(o.rd)
        return evs
    def op(self, eng, fn, reads=(), writes=()):
        waits = self._waits(eng, self._gather(reads, writes), pe_self=(eng == "pe"))
        if self.cnt[eng] >= 12000:
            self.epoch[eng] += 1
            self.cnt[eng] = 0
            self.key[eng] = "%s#%d" % (eng, self.epoch[eng])
            self.sems[self.key[eng]] = self.nc.alloc_semaphore("s_%s_%d" % (eng, self.epoch[eng]))
        self.cnt[eng] += 1
        ev = (self.key[eng], self.cnt[eng])
        for b in reads:
            b.rd.append(ev)
        for b in writes:
            b.lw = ev; b.rd = []
        self.ops[eng].append((fn, waits, (self.key[eng], 1)))
    def dma(self, q, fn, reads=(), writes=(), n=1):
        (dst,) = writes
        if dst.dsem is None:
            key = "d%d" % self.nsem
            self.nsem += 1
            self.sems[key] = self.nc.alloc_semaphore(key)
            dst.dsem = key
        waits = self._waits(q, self._gather(reads, writes))
        dst.dcnt += 16 * n
        ev = (dst.dsem, dst.dcnt)
        for b in reads:
            b.rd.append(ev)
        dst.lw = ev; dst.rd = []
        self.ops[q].append((fn, waits, (dst.dsem, 16)))
    def finish(self, eng, bufs):
        self.ops[eng].append((None, self._waits(eng, [b.lw for b in bufs]), None))
    def emit(self):
        nc = self.nc; sems = self.sems
        with nc.Block() as block:
            def run(name, e):
                for fn, waits, inc in self.ops[name]:
                    for k, v in waits:
                        e.wait_ge(sems[k], v)
                    if fn is None:
                        continue
                    r = fn(e)
                    if isinstance(r, (list, tuple)):
                        for ins in r:
                            ins.then_inc(sems[inc[0]], inc[1])
                    else:
                        r.then_inc(sems[inc[0]], inc[1])
            @block.tensor
            def _(e): run("pe", e)
            @block.scalar
            def _(e): run("act", e)
            @block.vector
            def _(e): run("dve", e)
            @block.gpsimd
            def _(e): run("pool", e)
            @block.sync
            def _(e): run("sp", e)

_DS = {F32: 4, BF16: 2}

class T:
    __slots__ = ("ap", "b")
    def __init__(self, ap, b):
        self.ap = ap; self.b = b

def build_program(debug_stop=None, debug_outs=()):
    nc = bass.Bass("TRN2", target_bir_lowering=False)
    S = Sched(nc)
    dram = {}
    def din(name, shape, dt=F32):
        dram[name] = nc.dram_tensor(name, list(shape), dt, kind="ExternalInput").ap()
        return dram[name]
    x_in = din("x", [SEQ, D]); ctx_in = din("ctx", [CTX, D])
    consts_in = din("consts", [128, NCONST]); pvec_in = din("pvec", [128, NPV])
    w_mod = din("w_mod", [DEPTH, D, 6 * D]); ffn_w_up = din("ffn_w_up", [DEPTH, D, 2 * DFF])
    ffn_w_down = din("ffn_w_down", [DEPTH, DFF, D])
    ev_w_in = din("ev_w_in", [2, D, 2336]); ev_w_out = din("ev_w_out", [2, D, D])
    ev_w_up = din("ev_w_up", [2, 2, 32, 512]); ev_a_up = din("ev_a_up", [2, 2, 64, 512])
    ev_g_up = din("ev_g_up", [2, 96, 512]); ev_pool_w = din("ev_pool_w", [2, 4, 128, 128])
    od_w_in = din("od_w_in", [2, D, 3 * D]); od_w_out = din("od_w_out", [2, D, D])
    out_d = nc.dram_tensor("out", [SEQ, D], F32, kind="ExternalOutput").ap()
    IN = Buf("inputs")
    OUTB = T(None, Buf("out"))
    scr_list = []
    def dscr(name, shape, dt):
        t = T(nc.dram_tensor(name, list(shape), dt, kind=("ExternalOutput" if name in debug_outs else "Internal")).ap(), Buf(name))
        scr_list.append(t)
        return t
    XS = dscr("XS", [NSEG, 128, 8 * SEG], F32)
    PR = dscr("PR", [NSEG, 128, 16 * SEG], F32)
    YP = dscr("YP", [NSEG, 128, 4 * SEG], BF16)
    FM = [dscr(f"FM{d}", [NSEG, 128, 4 * 1024], BF16) for d in range(2)]
    TMD = dscr("TMD", [TALL, 5, 512], BF16)
    GC = [dscr(f"GC{d}", [NSEG, 128, 16], F32) for d in range(2)]
    BGD = dscr("BGD", [NSEG, 128, 2 * 4 * SEG], BF16)
    YD = [dscr(f"YD{d}", [NSEG, 128, 4 * SEG], F32) for d in range(2)]

    cur = [0]
    def setoff(o):
        cur[0] = o
    tcache = {}
    def tile(shape, dt, name):
        nbytes = int(np.prod(shape[1:])) * _DS[dt]
        nbytes = (nbytes + 31) // 32 * 32
        off = cur[0]
        cur[0] += nbytes
        assert cur[0] <= 229376, (name, cur[0])
        if name in tcache:
            assert tcache[name][1] == off, name
            return tcache[name][0]
        h = nc.alloc_sbuf_tensor_at(name, list(shape), dt, offset=off)
        t = T(h.ap(), S.sbuf(name, off, off + nbytes))
        tcache[name] = (t, off)
        return t
    banks = []
    for i in range(7):
        banks.append(T(nc.alloc_psum_tensor(f"bank{i}", [128, 512], F32).ap(), Buf(f"bank{i}")))
    bankb_ap = nc.alloc_psum_tensor("bankb", [128, 1024], BF16).ap()
    _bb = Buf("bankb")
    bankb = [T(bankb_ap[:, 0:512], _bb), T(bankb_ap[:, 512:1024], _bb)]
    bki = [0, 0]
    def bank():
        bki[0] = (bki[0] + 1) % 7
        return banks[bki[0]]
    def bankbf():
        bki[1] = (bki[1] + 1) % 2
        return bankb[bki[1]]

    def bl(ts):
        return [t.b for t in ts]
    def TTo(eng, out, a, b, op, r, w):
        S.op(eng, lambda e: e.tensor_tensor(out=out, in0=a, in1=b, op=op), reads=bl(r), writes=bl(w))
    def STT(eng, out, in0, scalar, in1, op0, op1, r, w):
        S.op(eng, lambda e: e.scalar_tensor_tensor(out=out, in0=in0, scalar=scalar, in1=in1, op0=op0, op1=op1),
             reads=bl(r), writes=bl(w))
    def TS(eng, out, in0, s1, s2, op0, op1, r, w):
        if op1 is None:
            S.op(eng, lambda e: e.tensor_scalar(out=out, in0=in0, scalar1=s1, scalar2=None, op0=op0),
                 reads=bl(r), writes=bl(w))
        else:
            S.op(eng, lambda e: e.tensor_scalar(out=out, in0=in0, scalar1=s1, scalar2=s2, op0=op0, op1=op1),
                 reads=bl(r), writes=bl(w))
    def ACT(out, in_, func, r, w, scale=None, bias=None):
        kw = {}
        if scale is not None:
            kw["scale"] = scale
        if bias is not None:
            kw["bias"] = bias
        S.op("act", lambda e: e.activation(out=out, in_=in_, func=func, **kw), reads=bl(r), writes=bl(w))
    def CP(eng, out, in_, r, w):
        if eng == "act":
            S.op("act", lambda e: e.copy(out=out, in_=in_), reads=bl(r), writes=bl(w))
        else:
            S.op(eng, lambda e: e.tensor_copy(out=out, in_=in_), reads=bl(r), writes=bl(w))
    def RECIP(out, in_, r, w):
        S.op("dve", lambda e: e.reciprocal(out=out, in_=in_), reads=bl(r), writes=bl(w))
    def MM(out, pairs, r, w):
        def fn(e):
            ins = None
            n = len(pairs)
            for i, (l, rh) in enumerate(pairs):
                ins = e.matmul(out, lhsT=l, rhs=rh, start=(i == 0), stop=(i == n - 1))
            return ins
        S.op("pe", fn, reads=bl(r), writes=bl(w))
    def MMS(items, r, w):
        def fn(e):
            ins = None
            for out, pairs in items:
                n = len(pairs)
                for i, (l, rh) in enumerate(pairs):
                    ins = e.matmul(out, lhsT=l, rhs=rh, start=(i == 0), stop=(i == n - 1))
            return ins
        S.op("pe", fn, reads=bl(r), writes=bl(w))
    def TRS(items, ident, r, w):
        def fn(e):
            ins = None
            for out, in_ in items:
                ins = e.transpose(out, in_, ident)
            return ins
        S.op("pe", fn, reads=bl(r), writes=bl(w))
    def DMA(q, out, in_, r, w, **kw):
        S.dma(q, lambda e: e.dma_start(out=out, in_=in_, **kw), reads=bl(r), writes=bl(w))
    def DMAS(q, pairs, r, w):
        S.dma(q, lambda e: [e.dma_start(out=o, in_=i) for (o, i) in pairs], reads=bl(r), writes=bl(w), n=len(pairs))
    TIN = T(None, IN)

    setoff(16640)
    cst = tile([128, NCONST], F32, "cst")
    pv = tile([128, NPV], F32, "pv")
    identb = tile([128, 128], BF16, "identb")
    bdb = tile([128, 128], BF16, "bdb")
    onesb = tile([128, 128], BF16, "onesb")
    modt = tile([128, DEPTH, 48, 2], F32, "modt")
    HALO = tile([128, 16, NSEG, 2], F32, "HALO")
    mdv = tile([128, DEPTH, 6, 8, 2], F32, "mdv")
    dvec = tile([128, 2, 48], F32, "dvec")
    silc = tile([128, 8, 2], F32, "silc")
    wupb = tile([64, 512], BF16, "wupb")
    aupb = [tile([64, 512], BF16, f"aupb{d}") for d in range(2)]
    gupb = tile([96, 512], BF16, "gupb")
    poolwb = tile([128, 4, 128], BF16, "poolwb")
    CONST_END = cur[0]
    W_BASE = (CONST_END + 63) // 64 * 64
    WA_BYTES = 8 * 5632 * 2
    WB_BYTES = NFF * 1024 * 2
    setoff(W_BASE)
    WA = tile([128, 8, 5632], BF16, "WA")
    WB = tile([128, NFF, 1024], BF16, "WB")
    A_BASE = cur[0]

    def cc(name, n=128):
        return cst.ap[:, CO[name]:CO[name] + n]
    def pvc(name, j, parts=128):
        return pv.ap[0:parts, PV[name] + j:PV[name] + j + 1]

    DMA("sp", cst.ap, consts_in, [TIN], [cst])
    DMA("sp", pv.ap, pvec_in, [TIN], [pv])
    CP("dve", identb.ap, cc("ident"), [cst], [identb])
    CP("dve", bdb.ap, cc("bd"), [cst], [bdb])
    CP("dve", onesb.ap, cc("ones"), [cst], [onesb])
    ident = cc("ident")

    setoff(W_BASE)
    wm = [tile([128, 8, 768], F32, f"wm{i}") for i in range(2)]
    ACT(silc.ap[:, :, 0], pv.ap[:, PV["c"]:PV["c"] + 8], AF.Silu, [pv], [silc])
    ACT(silc.ap[:, :, 1], pv.ap[:, PV["cctx"]:PV["cctx"] + 8], AF.Silu, [pv], [silc])
    pi = 0
    for L in range(1 if debug_stop in ("mod1", "prep_only", "scan_only", "mini") else (2 if debug_stop in ("mini2", "testB") else (1 if debug_stop == "testC" else DEPTH))):
        wv = w_mod[L].rearrange("(k p) n -> p k n", p=128)
        bk = bank()
        for pc in range(8):
            wt = wm[pi % 2]; pi += 1
            DMA("sp", wt.ap, wv[:, :, pc * 768:(pc + 1) * 768], [TIN], [wt])
            items = []
            for j in range(6):
                nchunk = pc * 6 + j
                items.append((bk.ap[:, nchunk * 2:nchunk * 2 + 2],
                              [(wt.ap[:, k, j * 128:(j + 1) * 128], silc.ap[:, k, :]) for k in range(8)]))
            MMS(items, [wt, silc], [bk])
        bm = pv.ap[:, PV[f"bmod{L}"]:PV[f"bmod{L}"] + 48].unsqueeze(2).to_broadcast([128, 48, 2])
        TTo("dve", modt.ap[:, L], bk.ap[:, 0:96].rearrange("p (n c) -> p n c", c=2), bm, ALU.add, [bk, pv], [modt])
        def gb(jn):
            return pv.ap[:, PV[f"g{L}_{jn}"]:PV[f"g{L}_{jn}"] + 8].unsqueeze(2).to_broadcast([128, 8, 2])
        def mo(k):
            return modt.ap[:, L, k * 8:(k + 1) * 8, :]
        for (dst, g, sc) in ((0, 0, 1), (3, 2, 4)):
            STT("dve", mdv.ap[:, L, dst], mo(sc), 1.0, gb(g), ALU.add, ALU.mult, [modt, pv], [mdv])
        for (dst, src) in ((1, 0), (4, 3)):
            CP("dve", mdv.ap[:, L, dst], mo(src), [modt], [mdv])
        for (dst, g, gt) in ((2, 1, 2), (5, 3, 5)):
            TTo("dve", mdv.ap[:, L, dst], mo(gt), gb(g), ALU.mult, [modt, pv], [mdv])
    def MV(L, kind, fc, col):
        return mdv.ap[:, L, kind, fc, col:col + 1]

    setoff(A_BASE)
    XT = [tile([128, 8, SEG], F32, f"XT{i}") for i in range(2)]
    Hb = tile([128, 8, SEG], BF16, "Hb")
    SQ = tile([128, 8, SEG], BF16, "SQ")
    Yt = tile([128, 8, SEG], F32, "Yt")
    ACTF = tile([128, NFF, SEG], BF16, "ACTF")
    RSTD = tile([128, SEG], F32, "RSTD")
    TMP = [tile([128, SEG], F32, f"TMP{i}") for i in range(6)]
    ACT_END = cur[0]
    tmpi = [0]
    def tmp():
        tmpi[0] = (tmpi[0] + 1) % 6
        return TMP[tmpi[0]]

    def segcol(s):
        return s * SEG

    def rms_rstd(src, srcbufs, nchunks, lhs_ones, scale, eps, dstr):
        bk = bank()
        for fc in range(nchunks):
            ACT(SQ.ap[:, fc, :], src[:, fc, :], AF.Square, srcbufs, [SQ])
        MM(bk.ap[:, 0:SEG], [(lhs_ones, SQ.ap[:, fc, :]) for fc in range(nchunks)], [SQ, onesb, bdb], [bk])
        t1 = tmp()
        TS("dve", t1.ap, bk.ap[:, 0:SEG], scale, eps, ALU.mult, ALU.add, [bk], [t1])
        ACT(t1.ap, t1.ap, AF.Sqrt, [t1], [t1])
        RECIP(dstr.ap, t1.ap, [t1], [dstr])

    def norm_mod(xt, L, ka, kb, col):
        rms_rstd(xt.ap, [xt], 8, onesb.ap, 1.0 / D, RMS_EPS, RSTD)
        for fc in range(8):
            t1 = tmp()
            TTo("dve", t1.ap, xt.ap[:, fc, :], RSTD.ap, ALU.mult, [xt, RSTD], [t1])
            ACT(Hb.ap[:, fc, :], t1.ap, AF.Identity, [t1, mdv], [Hb], scale=MV(L, ka, fc, col), bias=MV(L, kb, fc, col))

    def residual(xt, L, kg, col):
        rms_rstd(Yt.ap, [Yt], 8, onesb.ap, 1.0 / D, RMS_EPS, RSTD)
        for fc in range(8):
            t1 = tmp()
            STT("dve", t1.ap, Yt.ap[:, fc, :], MV(L, kg, fc, col), RSTD.ap, ALU.mult, ALU.mult, [Yt, mdv, RSTD], [t1])
            TTo("pool", xt.ap[:, fc, :], xt.ap[:, fc, :], t1.ap, ALU.add, [xt, t1], [xt])

    def load_w(dst, dview, ncols, kch):
        c0 = 0
        while c0 < ncols:
            c1 = min(ncols, c0 + 2048)
            for k0 in range(0, kch, 4):
                k1 = min(kch, k0 + 4)
                DMA("pool", dst.ap[:, k0:k1, c0:c1], dview[:, k0:k1, c0:c1], [TIN], [dst])
            c0 = c1

    def load_x(xt, s):
        DMA("sp", xt.ap.rearrange("p k t -> p (k t)"), XS.ap[s], [XS], [xt])
    def store_x(xt, s):
        DMA("sp", XS.ap[s], xt.ap.rearrange("p k t -> p (k t)"), [xt], [XS])

    def conv3(dst, src, srcbufs, src_is_psum, w0, w1, w2, nrows, rl):
        ACT(dst.ap, src, AF.Identity, srcbufs, [dst], scale=w1)
        dv = dst.ap.rearrange("p (r c) -> p r c", c=rl)
        sv = src.rearrange("p (r c) -> p r c", c=rl)
        STT("dve", dv[:, :, 1:rl], sv[:, :, 0:rl - 1], w0, dv[:, :, 1:rl], ALU.mult, ALU.add, srcbufs + [dst], [dst])
        STT("dve", dv[:, :, 0:rl - 1], sv[:, :, 1:rl], w2, dv[:, :, 0:rl - 1], ALU.mult, ALU.add, srcbufs + [dst], [dst])

    def grid(s):
        return (1, 256) if s == 0 else (4, 64)

    def pass_E1(L, segs):
        i = L // 2
        load_w(WA, ev_w_in[i].rearrange("(k p) n -> p k n", p=128), 2336, 8)
        DMA("pool", poolwb.ap, ev_pool_w[i].rearrange("g c d -> c g d"), [TIN], [poolwb])
        setoff(W_BASE + WA_BYTES)
        TM = tile([128, 2, D], F32, "E1_TM")
        PRS = tile([128, 16, SEG], F32, "E1_PRS")
        XP = tile([128, 4, 384], F32, "E1_XP")
        S2 = tile([128, 384], F32, "E1_S2"); S4 = tile([128, 384], F32, "E1_S4")
        S8 = tile([128, 384], F32, "E1_S8"); S16 = tile([128, 384], F32, "E1_S16")
        DB = tile([128, SEG], BF16, "E1_DB")
        YPT = tile([128, 4, SEG], BF16, "E1_YPT")
        stg_i = 0
        for s in segs:
            if s in (0, 1):
                S.op("pool", lambda e: e.memset(XP.ap, 0.0), writes=[XP.b])
            if s == 0:
                S.op("pool", lambda e: e.memset(PRS.ap, 0.0), writes=[PRS.b])
                S.op("pool", lambda e: e.memset(HALO.ap, 0.0), writes=[HALO.b])
            xt = XT[s % 2]
            col = 0 if s > 0 else 1
            if L == 0:
                src = ctx_in if s == 0 else x_in[(s - 1) * SEG:s * SEG]
                DMA("sp", TM.ap, src.rearrange("(h p) f -> p h f", p=128), [TIN], [TM])
                for fc in range(8):
                    bk = bank()
                    TRS([(bk.ap[:, hh * 128:(hh + 1) * 128], TM.ap[:, hh, fc * 128:(fc + 1) * 128]) for hh in range(2)],
                        ident, [TM, cst], [bk])
                    CP("act" if fc % 2 else "dve", xt.ap[:, fc, :], bk.ap[:, 0:SEG], [bk], [xt])
                store_x(xt, s)
            else:
                load_x(xt, s)
            norm_mod(xt, L, 0, 1, col)
            for gi, (st, sz) in enumerate(PGRP):
                bk = bank()
                MM(bk.ap[0:sz, 0:SEG], [(WA.ap[:, k, st:st + sz], Hb.ap[:, k, :]) for k in range(8)], [WA, Hb], [bk])
                CP("act", PRS.ap[0:sz, gi, :], bk.ap[0:sz, 0:SEG], [bk], [PRS])
            CP("pool", HALO.ap[:, :, s, 0], PRS.ap[:, :, 0], [PRS], [HALO])
            CP("pool", HALO.ap[:, :, s, 1], PRS.ap[:, :, SEG - 1], [PRS], [HALO])
            DMA("sp", PR.ap[s], PRS.ap.rearrange("p g t -> p (g t)"), [PRS], [PR])
            nr, rl = grid(s)
            pw = rl + 32
            invn = "inv_ctx" if s == 0 else "inv_lat"
            for gi in range(4):
                win = (2, 4, 8, 16)[gi]
                bk = bank()
                c0 = 1824 + gi * 128
                MM(bk.ap[:, 0:SEG], [(WA.ap[:, k, c0:c0 + 128], Hb.ap[:, k, :]) for k in range(8)], [WA, Hb], [bk])
                xpv = XP.ap[:, gi, 0:nr * pw].rearrange("p (r c) -> p r c", c=pw)
                CP("act", xpv[:, :, 16:16 + rl], bk.ap[:, 0:SEG].rearrange("p (r c) -> p r c", c=rl), [bk], [XP])
                prev = xpv; prevb = XP; sh = 1
                for (St, wn) in ((S2, 2), (S4, 4), (S8, 8), (S16, 16)):
                    if wn > win:
                        break
                    sv = St.ap[:, 0:nr * pw].rearrange("p (r c) -> p r c", c=pw)
                    lo = wn - 1
                    TTo("pool" if gi % 2 else "dve", sv[:, :, lo:pw], prev[:, :, lo:pw], prev[:, :, lo - sh:pw - sh], ALU.add, [prevb], [St])
                    prev = sv; prevb = St; sh = wn
                o = 16 + win // 2 - 1
                t1 = tmp()
                t1v = t1.ap.rearrange("p (r c) -> p r c", c=rl)
                iv = cst.ap[:, CO[invn] + gi * rl:CO[invn] + (gi + 1) * rl].unsqueeze(1).to_broadcast([128, nr, rl])
                TTo("dve", t1v, prev[:, :, o:o + rl], iv, ALU.mult, [prevb, cst], [t1])
                TTo("dve", DB.ap.rearrange("p (r c) -> p r c", c=rl), t1v, xpv[:, :, 16:16 + rl], ALU.subtract, [t1, XP], [DB])
                bk2 = bank()
                MM(bk2.ap[:, 0:SEG], [(poolwb.ap[:, gi, :], DB.ap)], [poolwb, DB], [bk2])
                ACT(YPT.ap[:, gi, :], bk2.ap[:, 0:SEG], AF.Identity, [bk2, pv], [YPT], scale=pvc(f"psc{L}", gi))
            DMA("sp", YP.ap[s], YPT.ap.rearrange("p k t -> p (k t)"), [YPT], [YP])

    def pass_prep(L, segs):
        i = L // 2
        DMA("pool", wupb.ap, ev_w_up[i].rearrange("d r c -> (d r) c"), [TIN], [wupb])
        for d in range(2):
            DMA("pool", aupb[d].ap, ev_a_up[i, d], [TIN], [aupb[d]])
        DMA("pool", gupb.ap, ev_g_up[i], [TIN], [gupb])
        m0 = pv.ap[:, PV[f"mu{L}_0"]:PV[f"mu{L}_0"] + 16]
        m1 = pv.ap[:, PV[f"mu{L}_1"]:PV[f"mu{L}_1"] + 16]
        dv = dvec.ap[:, i]
        TTo("dve", dv[:, 0:16], m0, m1, ALU.add, [pv], [dvec])
        TS("dve", dv[:, 0:16], dv[:, 0:16], -1.0, 1.0, ALU.mult, ALU.add, [dvec], [dvec])
        TS("dve", dv[:, 16:20], pv.ap[:, PV[f"ka{L}"]:PV[f"ka{L}"] + 4], -1.0, 1.0, ALU.mult, ALU.add, [pv], [dvec])
        setoff(W_BASE)
        PRM = [tile([128, 16, SEG], F32, f"P_PRM{k}") for k in range(2)]
        PRT1 = tile([128, 16, SEG + 2], F32, "P_PRT")
        SH = tile([128, 16, SEG], F32, "P_SH")
        KKt = tile([128, 4, SEG], F32, "P_KK")
        At = [tile([128, 4, SEG], F32, f"P_A{d}") for d in range(2)]
        KDt = [tile([128, 4, SEG], F32, f"P_KD{d}") for d in range(2)]
        Bt = [tile([128, 4, SEG], F32, f"P_B{d}") for d in range(2)]
        SG = [tile([128, 4, SEG], F32, f"P_SG{d}") for d in range(2)]
        CUM = tile([128, SEG], F32, "P_CUM")
        E = [tile([128, SEG], F32, f"P_E{k}") for k in range(4)]
        TLb = tile([64, SEG], BF16, "P_TLb")
        LAb = [tile([64, SEG], BF16, f"P_LAb{d}") for d in range(2)]
        LGb = tile([96, SEG], BF16, "P_LGb")
        SQb = tile([128, SEG], BF16, "P_SQb")
        FMo = [tile([128, 4, 1024], BF16, f"P_FMo{d}") for d in range(2)]
        KGo = [tile([128, 4, SEG], BF16, f"P_KGo{d}") for d in range(2)]
        BGo = [tile([128, 4, SEG], BF16, f"P_BGo{d}") for d in range(2)]
        Vb = tile([128, 4, SEG], BF16, "P_Vb")
        GCo = [tile([128, 4, 4], F32, f"P_GCo{d}") for d in range(2)]
        BGo2 = tile([128, 2, 4, SEG], BF16, "P_BGo2")
        TMo = [tile([128, 5, 512], BF16, f"P_TMo{k}") for k in range(2)]
        tmo_i = [0]

        def load_pr(s):
            DMA("sp", PRM[s % 2].ap.rearrange("p g t -> p (g t)"), PR.ap[s], [PR], [PRM[s % 2]])

        load_pr(segs[0])
        for si, s in enumerate(segs):
            if si + 1 < len(segs):
                load_pr(segs[si + 1])
            pt = PRT1
            t0 = segcol(s)
            c0ch = t0 // CH
            first = (s == 0 or s == 1); last = (s == 0 or s == NSEG - 1)
            CP("pool", pt.ap[:, :, 1:SEG + 1], PRM[s % 2].ap, [PRM[s % 2]], [pt])
            if first:
                S.op("pool", lambda e: e.memset(pt.ap[:, :, 0:1], 0.0), writes=[pt.b])
            else:
                CP("pool", pt.ap[:, :, 0], HALO.ap[:, :, s - 1, 1], [HALO], [pt])
            if last:
                S.op("pool", lambda e: e.memset(pt.ap[:, :, SEG + 1:SEG + 2], 0.0), writes=[pt.b])
            else:
                CP("pool", pt.ap[:, :, SEG + 1], HALO.ap[:, :, s + 1, 0], [HALO], [pt])
            for gi, (st, sz) in enumerate(PGRP):
                dst = SH.ap[0:sz, gi, :]
                ACT(dst, pt.ap[0:sz, gi, 1:SEG + 1], AF.Identity, [pt, dvec], [SH], scale=dvec.ap[0:sz, i, gi:gi + 1])
                STT("dve", dst, pt.ap[0:sz, gi, 0:SEG], pvc(f"mu{L}_0", gi, sz), dst, ALU.mult, ALU.add, [pt, pv, SH], [SH])
                STT("dve", dst, pt.ap[0:sz, gi, 2:SEG + 2], pvc(f"mu{L}_1", gi, sz), dst, ALU.mult, ALU.add, [pt, pv, SH], [SH])
            if KLIM <= 1:
                continue
            Rv = SH.ap[:, 0:4, :]; Kv = SH.ap[:, 4:8, :]; Vv = SH.ap[:, 8:12, :]
            ACT(TLb.ap, SH.ap[0:64, 12, :], AF.Tanh, [SH], [TLb])
            CP("pool", LAb[0].ap, SH.ap[0:64, 13, :], [SH], [LAb[0]])
            CP("pool", LAb[1].ap, SH.ap[0:64, 14, :], [SH], [LAb[1]])
            ACT(LGb.ap, SH.ap[0:96, 15, :], AF.Sigmoid, [SH], [LGb])
            CP("pool", Vb.ap, Vv, [SH], [Vb])
            for c4 in range(4):
                csl = slice(c4 * 128, (c4 + 1) * 128)
                bk = bank()
                MM(bk.ap[:, 0:SEG], [(gupb.ap[:, csl], LGb.ap)], [gupb, LGb], [bk])
                CP("act", BGo2.ap[:, 1, c4, :], bk.ap[:, 0:SEG], [bk], [BGo2])
                t1 = tmp()
                TS("dve", t1.ap, Kv[:, c4, :], pvc(f"kk{L}", c4), None, ALU.mult, None, [SH, pv], [t1])
                ACT(SQb.ap, t1.ap, AF.Square, [t1], [SQb])
                bk = bank()
                MM(bk.ap[:, 0:SEG], [(bdb.ap, SQb.ap)], [bdb, SQb], [bk])
                t2 = tmp()
                TS("dve", t2.ap, bk.ap[:, 0:SEG], 1e-24, None, ALU.max, None, [bk], [t2])
                ACT(t2.ap, t2.ap, AF.Sqrt, [t2], [t2])
                RECIP(t2.ap, t2.ap, [t2], [t2])
                TTo("dve", KKt.ap[:, c4, :], t1.ap, t2.ap, ALU.mult, [t1, t2], [KKt])
                for d in range(2):
                    bk = bank()
                    MM(bk.ap[:, 0:SEG], [(wupb.ap[32 * d:32 * d + 32, csl], TLb.ap[32 * d:32 * d + 32, :])], [wupb, TLb], [bk])
                    ACT(SG[d].ap[:, c4, :], bk.ap[:, 0:SEG], AF.Sigmoid, [bk, pv], [SG[d]], bias=pvc(f"w0{L}_{d}", c4))
                    bk = bank()
                    MM(bk.ap[:, 0:SEG], [(aupb[d].ap[:, csl], LAb[d].ap)], [aupb[d], LAb[d]], [bk])
                    ACT(At[d].ap[:, c4, :], bk.ap[:, 0:SEG], AF.Sigmoid, [bk, pv], [At[d]], bias=pvc(f"a0{L}_{d}", c4))
                    t3 = tmp()
                    TS("dve", t3.ap, At[d].ap[:, c4, :], pvc(f"ka{L}", c4), dvec.ap[:, i, 16 + c4:17 + c4], ALU.mult, ALU.add,
                       [At[d], pv, dvec], [t3])
                    TTo("pool", KDt[d].ap[:, c4, :], Kv[:, c4, :], t3.ap, ALU.mult, [SH, t3], [KDt[d]])
                    TTo("pool", Bt[d].ap[:, c4, :], KKt.ap[:, c4, :], At[d].ap[:, c4, :], ALU.mult, [KKt, At[d]], [Bt[d]])
                    S.op("dve", lambda e, d=d, c4=c4: e.tensor_tensor_scan(out=CUM.ap, data0=cc("restart", 256), data1=SG[d].ap[:, c4, :],
                                                                            initial=0.0, op0=ALU.mult, op1=ALU.add),
                         reads=[cst.b, SG[d].b], writes=[CUM.b])
                    cumv = CUM.ap.rearrange("p (c t) -> p c t", t=CH)
                    sgv = SG[d].ap[:, c4, :].rearrange("p (c t) -> p c t", t=CH)
                    totb = cumv[:, :, CH - 1:CH].to_broadcast([128, 4, CH])
                    e0, e1, e2, e3 = E
                    if d == 1:
                        t4 = tmp()
                        t4v = t4.ap.rearrange("p (c t) -> p c t", t=CH)
                        TTo("dve", t4v, totb, cumv, ALU.subtract, [CUM], [t4])
                        ACT(GCo[d].ap[:, c4, :], cumv[:, :, CH - 1], AF.Exp, [CUM], [GCo[d]], scale=-EM05)
                        TTo("dve", CUM.ap, t4.ap, SG[d].ap[:, c4, :], ALU.add, [t4, SG[d]], [CUM])
                        t5 = tmp()
                        t5v = t5.ap.rearrange("p (c t) -> p c t", t=CH)
                        TTo("dve", t5v, cumv[:, :, 0:1].to_broadcast([128, 4, CH]), cumv, ALU.subtract, [CUM], [t5])
                        tmc = t5
                    else:
                        ACT(GCo[d].ap[:, c4, :], cumv[:, :, CH - 1], AF.Exp, [CUM], [GCo[d]], scale=-EM05)
                        t5 = tmp()
                        t5v = t5.ap.rearrange("p (c t) -> p c t", t=CH)
                        TTo("dve", t5v, totb, cumv, ALU.subtract, [CUM], [t5])
                        tmc = t5
                    t6 = tmp()
                    TTo("pool", t6.ap, CUM.ap, SG[d].ap[:, c4, :], ALU.subtract, [CUM, SG[d]], [t6])
                    ACT(e0.ap, CUM.ap, AF.Exp, [CUM], [e0], scale=-EM05)
                    ACT(e1.ap, t6.ap, AF.Exp, [t6], [e1], scale=-EM05)
                    ACT(e2.ap, CUM.ap, AF.Exp, [CUM], [e2], scale=EM05)
                    ACT(e3.ap, tmc.ap, AF.Exp, [tmc], [e3], scale=-EM05)
                    kqv = FMo[d].ap[:, c4, 0:512].rearrange("p (c a t) -> p c a t", a=2, t=CH)
                    TTo("dve", kqv[:, :, 0, :], KKt.ap[:, c4, :].rearrange("p (c t) -> p c t", t=CH),
                        e1.ap.rearrange("p (c t) -> p c t", t=CH), ALU.mult, [KKt, e1], [FMo[d]])
                    TTo("dve", kqv[:, :, 1, :], Rv[:, c4, :].rearrange("p (c t) -> p c t", t=CH),
                        e0.ap.rearrange("p (c t) -> p c t", t=CH), ALU.mult, [SH, e0], [FMo[d]])
                    TTo("pool", FMo[d].ap[:, c4, 512:768], KDt[d].ap[:, c4, :], e2.ap, ALU.mult, [KDt[d], e2], [FMo[d]])
                    TTo("pool", FMo[d].ap[:, c4, 768:1024], Bt[d].ap[:, c4, :], e2.ap, ALU.mult, [Bt[d], e2], [FMo[d]])
                    TTo("dve", KGo[d].ap[:, c4, :], KDt[d].ap[:, c4, :], e3.ap, ALU.mult, [KDt[d], e3], [KGo[d]])
                    TTo("pool", BGo[d].ap[:, c4, :], Bt[d].ap[:, c4, :], e3.ap, ALU.mult, [Bt[d], e3], [BGo[d]])
                t7 = tmp()
                TTo("pool", t7.ap, KDt[0].ap[:, c4, :], KDt[1].ap[:, c4, :], ALU.add, [KDt[0], KDt[1]], [t7])
                STT("dve", SQb.ap, t7.ap, pvc(f"rk{L}", c4), Rv[:, c4, :], ALU.mult, ALU.mult, [t7, pv, SH], [SQb])
                bk = bank()
                MM(bk.ap[:, 0:SEG], [(bdb.ap, SQb.ap)], [bdb, SQb], [bk])
                TTo("dve", BGo2.ap[:, 0, c4, :], bk.ap[:, 0:SEG], Vv[:, c4, :], ALU.mult, [bk, SH], [BGo2])
            if KLIM <= 2:
                continue
            for d in range(2):
                DMA("sp", FM[d].ap[s], FMo[d].ap.rearrange("p j n -> p (j n)"), [FMo[d]], [FM[d]])
                DMA("sp", GC[d].ap[s], GCo[d].ap.rearrange("p j c -> p (j c)"), [GCo[d]], [GC[d]])
            DMA("sp", BGD.ap[s], BGo2.ap.rearrange("p a j t -> p (a j t)"), [BGo2], [BGD])
            if KLIM <= 3:
                continue
            for hh in range(2):
                to = TMo[tmo_i[0] % 2]; tmo_i[0] += 1
                for qi, srct in enumerate((Vb, KGo[0], BGo[0], KGo[1], BGo[1])):
                    bb = bankbf()
                    TRS([(bb.ap[:, c4 * 128:(c4 + 1) * 128], srct.ap[:, c4, hh * 128:(hh + 1) * 128]) for c4 in range(4)],
                        identb.ap, [srct, identb], [bb])
                    CP("act" if qi % 2 else "dve", to.ap[:, qi, :], bb.ap, [bb], [to])
                DMA("sp", TMD.ap[t0 + hh * 128:t0 + (hh + 1) * 128].rearrange("t q n -> t (q n)"),
                    to.ap.rearrange("p q n -> p (q n)"), [to], [TMD])

    def pass_scan(L, d, store_ctx, segs_override=None):
        setoff(W_BASE)
        LD = []
        for k in range(2):
            LD.append(dict(FM=tile([64, 4, 2, 1024], BF16, f"S_FM{k}"), TV=tile([64, 4, 512], BF16, f"S_TV{k}"),
                           TG=tile([64, 4, 2, 512], BF16, f"S_TG{k}"), GC=tile([64, 4, 2, 4], F32, f"S_GC{k}")))
        AT1s = [tile([64, 8, 128], BF16, f"S_AT1_{c}") for c in range(4)]
        AT2s = [tile([64, 8, 128], BF16, f"S_AT2_{c}") for c in range(4)]
        ZFs = [[tile([64, 8, 128], F32, f"S_ZF{c}_{k}") for k in range(2)] for c in range(4)]
        NNs = [[tile([64, 8, 64], F32, f"S_NN{c}_{k}") for k in range(2)] for c in range(4)]
        RHSb = tile([64, 8, 64], F32, "S_RHS"); UNb = tile([64, 8, 64], BF16, "S_UN")
        Hf = tile([64, 8, 64], F32, "S_H"); Hh = tile([64, 8, 64], BF16, "S_Hb")
        YO = [tile([64, 8, SEG], F32, f"S_YO{k}") for k in range(2)]
        S.op("dve", lambda e: e.memset(Hf.ap, 0.0), writes=[Hf.b])
        S.op("pool", lambda e: e.memset(Hh.ap, 0.0), writes=[Hh.b])
        m1 = cst.ap[0:64, CO["m1b" if d else "m1f"]:CO["m1b" if d else "m1f"] + 128].unsqueeze(1).to_broadcast([64, 8, 128])
        m3 = cst.ap[0:64, CO["m3b" if d else "m3f"]:CO["m3b" if d else "m3f"] + 64].unsqueeze(1).to_broadcast([64, 8, 64])
        id64 = cst.ap[0:64, CO["id64"]:CO["id64"] + 64].unsqueeze(1).to_broadcast([64, 8, 64])
        segs = [0] + (list(range(NSEG - 1, 0, -1)) if d else list(range(1, NSEG)))
        if segs_override is not None:
            segs = segs_override

        def load(si):
            s = segs[si]; ld = LD[si % 2]
            t0 = segcol(s)
            for hh in range(2):
                DMA("sp", ld["FM"].ap[:, :, hh, :], FM[d].ap[s, hh * 64:(hh + 1) * 64, :].rearrange("k (j n) -> k j n", j=4),
                    [FM[d]], [ld["FM"]])
                DMA("sp", ld["GC"].ap[:, :, hh, :], GC[d].ap[s, hh * 64:(hh + 1) * 64, :].rearrange("k (j c) -> k j c", j=4),
                    [GC[d]], [ld["GC"]])
            tmv = TMD.ap[t0:t0 + SEG].rearrange("(c s) q n -> s c q n", s=64)
            DMA("sp", ld["TV"].ap, tmv[:, :, 0, :], [TMD], [ld["TV"]])
            DMA("sp", ld["TG"].ap, tmv[:, :, 1 + 2 * d:3 + 2 * d, :], [TMD], [ld["TG"]])

        load(0)
        for si, s in enumerate(segs):
            if si + 1 < len(segs):
                load(si + 1)
            ld = LD[si % 2]
            yo = YO[si % 2]
            if s == 0:
                if si > 0:
                    pass
            chunks = [3, 2, 1, 0] if d else [0, 1, 2, 3]
            fmv = ld["FM"].ap.rearrange("k j two n -> k (j two) n")
            def hv(pb, h, w):
                return pb[h // 4].ap[0:64, (h % 4) * w:(h % 4 + 1) * w]
            for c in chunks:
                kq = fmv[:, :, c * 128:(c + 1) * 128]
                kh = fmv[:, :, 512 + c * CH:512 + (c + 1) * CH]
                bh = fmv[:, :, 768 + c * CH:768 + (c + 1) * CH]
                AT1 = AT1s[c]; AT2 = AT2s[c]
                pa1 = [bank(), bank()]; pa2 = [bank(), bank()]; pa3 = bank()
                MMS([(hv(pa1, h, 128), [(kh[:, h, :], kq[:, h, :])]) for h in range(8)], [ld["FM"]], pa1)
                MMS([(hv(pa2, h, 128), [(bh[:, h, :], kq[:, h, :])]) for h in range(8)], [ld["FM"]], pa2)
                MMS([(pa3.ap[0:64, h * 64:(h + 1) * 64], [(kq[:, h, 0:64], bh[:, h, :])]) for h in range(8)], [ld["FM"]], [pa3])
                zf, nn = ZFs[c][0], NNs[c][0]
                for hf in range(2):
                    TTo("dve", AT1.ap[:, hf * 4:(hf + 1) * 4, :], pa1[hf].ap[0:64, :].rearrange("p (h x) -> p h x", x=128),
                        m1[:, 0:4, :], ALU.mult, [pa1[hf], cst], [AT1])
                    TTo("dve", AT2.ap[:, hf * 4:(hf + 1) * 4, :], pa2[hf].ap[0:64, :].rearrange("p (h x) -> p h x", x=128),
                        m1[:, 0:4, :], ALU.mult, [pa2[hf], cst], [AT2])
                    TTo("dve", zf.ap[:, hf * 4:(hf + 1) * 4, 0:64], pa2[hf].ap[0:64, :].rearrange("p (h x) -> p h x", x=128)[:, :, 0:64],
                        m1[:, 0:4, 0:64], ALU.mult, [pa2[hf], cst], [zf])
                TTo("dve", nn.ap, pa3.ap[0:64, :].rearrange("p (h x) -> p h x", x=64), m3, ALU.mult, [pa3, cst], [nn])
                TTo("pool", zf.ap[:, :, 64:128], id64, zf.ap[:, :, 0:64], ALU.subtract, [cst, zf], [zf])
            for lev in range(6):
                cur_i = lev % 2
                last = (lev == 5)
                for c in chunks:
                    zf = ZFs[c][cur_i]; nn = NNs[c][cur_i]; zf2 = ZFs[c][1 - cur_i]; nn2 = NNs[c][1 - cur_i]
                    pz = [bank(), bank()]
                    if lev == 0:
                        MMS([(hv(pz, h, 128)[:, 0:64], [(nn.ap[:, h, :], zf.ap[:, h, 0:64])]) for h in range(8)], [nn, zf], pz)
                    elif not last:
                        MMS([(hv(pz, h, 128), [(nn.ap[:, h, :], zf.ap[:, h, :])]) for h in range(8)], [nn, zf], pz)
                    else:
                        MMS([(hv(pz, h, 128)[:, 64:128], [(nn.ap[:, h, :], zf.ap[:, h, 64:128])]) for h in range(8)], [nn, zf], pz)
                    if not last:
                        pn = bank()
                        MMS([(pn.ap[0:64, h * 64:(h + 1) * 64], [(zf.ap[:, h, 0:64], nn.ap[:, h, :])]) for h in range(8)], [nn, zf], [pn])
                        CP("act", nn2.ap, pn.ap[0:64, :].rearrange("p (h x) -> p h x", x=64), [pn], [nn2])
                    for hf in range(2):
                        pzv = pz[hf].ap[0:64, :].rearrange("p (h x) -> p h x", x=128)
                        hs = slice(hf * 4, (hf + 1) * 4)
                        if not last:
                            CP("act", zf2.ap[:, hs, 0:64], pzv[:, :, 0:64], [pz[hf]], [zf2])
                        if lev == 0:
                            CP("pool", zf2.ap[:, hs, 64:128], zf.ap[:, hs, 64:128], [zf], [zf2])
                        else:
                            TTo("dve", zf2.ap[:, hs, 64:128], pzv[:, :, 64:128], zf.ap[:, hs, 64:128], ALU.add, [pz[hf], zf], [zf2])
            for c in chunks:
                kq = fmv[:, :, c * 128:(c + 1) * 128]
                vt = ld["TV"].ap[:, c, :].rearrange("s (h v) -> s h v", v=64)
                kg = ld["TG"].ap[:, c, 0, :].rearrange("s (h v) -> s h v", v=64)
                bg = ld["TG"].ap[:, c, 1, :].rearrange("s (h v) -> s h v", v=64)
                AT1 = AT1s[c]; AT2 = AT2s[c]
                Fm = ZFs[c][0]
                pr = bank()
                MMS([(pr.ap[0:64, h * 64:(h + 1) * 64], [(kq[:, h, 0:64], Hh.ap[:, h, :]), (AT1.ap[:, h, 0:64], vt[:, h, :])])
                     for h in range(8)], [ld["FM"], Hh, AT1, ld["TV"]], [pr])
                CP("act", RHSb.ap, pr.ap[0:64, :].rearrange("p (h x) -> p h x", x=64), [pr], [RHSb])
                pu = bank()
                MMS([(pu.ap[0:64, h * 64:(h + 1) * 64], [(Fm.ap[:, h, 64:128], RHSb.ap[:, h, :])]) for h in range(8)], [Fm, RHSb], [pu])
                S.op("act", lambda e, pu=pu: e.mul(out=UNb.ap, in_=pu.ap[0:64, :].rearrange("p (h x) -> p h x", x=64), mul=-1.0),
                     reads=[pu.b], writes=[UNb.b])
                py = bank()
                MMS([(py.ap[0:64, h * 64:(h + 1) * 64],
                      [(Hh.ap[:, h, :], kq[:, h, 64:128]), (vt[:, h, :], AT1.ap[:, h, 64:128]), (UNb.ap[:, h, :], AT2.ap[:, h, 64:128])])
                     for h in range(8)], [Hh, ld["FM"], ld["TV"], AT1, UNb, AT2], [py])
                CP("act", yo.ap[:, :, c * CH:(c + 1) * CH], py.ap[0:64, :].rearrange("p (h x) -> p h x", x=64), [py], [yo])
                ph = bank()
                MMS([(ph.ap[0:64, h * 64:(h + 1) * 64], [(kg[:, h, :], vt[:, h, :]), (bg[:, h, :], UNb.ap[:, h, :])]) for h in range(8)],
                    [ld["TG"], ld["TV"], UNb], [ph])
                gcb = ld["GC"].ap.rearrange("k j two c -> k (j two) c")[:, :, c:c + 1].to_broadcast([64, 8, 64])
                TTo("dve", Hf.ap, Hf.ap, gcb, ALU.mult, [Hf, ld["GC"]], [Hf])
                TTo("dve", Hf.ap, Hf.ap, ph.ap[0:64, :].rearrange("p (h x) -> p h x", x=64), ALU.add, [Hf, ph], [Hf])
                CP("pool", Hh.ap, Hf.ap, [Hf], [Hh])
            if s > 0 or store_ctx:
                t0 = segcol(s)
                for hh in range(2):
                    DMA("sp", YD[d].ap[s, hh * 64:(hh + 1) * 64, :].rearrange("v (j t) -> v j t", j=4),
                        yo.ap.rearrange("v (j two) t -> v j two t", two=2)[:, :, hh, :], [yo], [YD[d]])

    def pass_E3(L, segs):
        i = L // 2
        load_w(WB, ev_w_out[i].rearrange("(k p) n -> p k n", p=128), D, 8)
        setoff(W_BASE)
        Y0 = tile([128, 4, SEG], F32, "E3_Y0"); Y1 = tile([128, 4, SEG], F32, "E3_Y1")
        BGt = tile([128, 2, 4, SEG], BF16, "E3_BG")
        YC = tile([128, 4, SEG], F32, "E3_YC")
        MIX = ACTF
        for s in segs:
            xt = XT[s % 2]
            col = 0 if s > 0 else 1
            t0 = segcol(s)
            load_x(xt, s)
            DMA("sp", Y0.ap.rearrange("p k t -> p (k t)"), YD[0].ap[s], [YD[0]], [Y0])
            DMA("sp", Y1.ap.rearrange("p k t -> p (k t)"), YD[1].ap[s], [YD[1]], [Y1])
            DMA("sp", BGt.ap.rearrange("p a k t -> p (a k t)"), BGD.ap[s], [BGD], [BGt])
            DMA("sp", MIX.ap[:, 4:8, :].rearrange("p k t -> p (k t)"), YP.ap[s], [YP], [MIX])
            TTo("pool", Y0.ap, Y0.ap, Y1.ap, ALU.add, [Y0, Y1], [Y0])
            for c4 in range(4):
                CP("act", SQ.ap[:, c4, :], Y0.ap[:, c4, :], [Y0], [SQ])
                bk = bank()
                MM(bk.ap[:, 0:SEG], [(bdb.ap, SQ.ap[:, c4, :])], [bdb, SQ], [bk])
                STT("dve", YC.ap[:, c4, :], bk.ap[:, 0:SEG], -1.0 / 64, Y0.ap[:, c4, :], ALU.mult, ALU.add, [bk, Y0], [YC])
                ACT(SQ.ap[:, c4, :], YC.ap[:, c4, :], AF.Square, [YC], [SQ])
                bk = bank()
                MM(bk.ap[:, 0:SEG], [(bdb.ap, SQ.ap[:, c4, :])], [bdb, SQ], [bk])
                t1 = tmp()
                TS("dve", t1.ap, bk.ap[:, 0:SEG], 1.0 / 64, GN_EPS, ALU.mult, ALU.add, [bk], [t1])
                ACT(t1.ap, t1.ap, AF.Sqrt, [t1], [t1])
                RECIP(t1.ap, t1.ap, [t1], [t1])
                TTo("dve", YC.ap[:, c4, :], YC.ap[:, c4, :], t1.ap, ALU.mult, [YC, t1], [YC])
                ACT(YC.ap[:, c4, :], YC.ap[:, c4, :], AF.Identity, [YC, pv], [YC], scale=pvc(f"gnw{L}", c4), bias=pvc(f"gnb{L}", c4))
                TTo("pool", YC.ap[:, c4, :], YC.ap[:, c4, :], BGt.ap[:, 0, c4, :], ALU.add, [YC, BGt], [YC])
                TTo("dve", MIX.ap[:, c4, :], YC.ap[:, c4, :], BGt.ap[:, 1, c4, :], ALU.mult, [YC, BGt], [MIX])
            for oc in range(8):
                bk = bank()
                MM(bk.ap[:, 0:SEG], [(WB.ap[:, k, oc * 128:(oc + 1) * 128], MIX.ap[:, k, :]) for k in range(8)], [WB, MIX], [bk])
                CP("act", Yt.ap[:, oc, :], bk.ap[:, 0:SEG], [bk], [Yt])
            residual(xt, L, 2, col)
            store_x(xt, s)

    def pass_O(L, segs):
        i = L // 2
        load_w(WA, od_w_in[i].rearrange("(k p) n -> p k n", p=128), 3 * D, 8)
        load_w(WB, od_w_out[i].rearrange("(k p) n -> p k n", p=128), D, 8)
        MIX = ACTF
        load_x(XT[segs[0] % 2], segs[0])
        norm_mod(XT[segs[0] % 2], L, 0, 1, 0 if segs[0] > 0 else 1)
        for si, s in enumerate(segs):
            xt = XT[s % 2]
            col = 0 if s > 0 else 1
            nr, rl = grid(s)
            for j in range(8):
                pb = bank(); pc = bank(); pu = bank()
                for (bk, c0) in ((pb, j * 128), (pc, D + j * 128), (pu, 2 * D + j * 128)):
                    MM(bk.ap[:, 0:SEG], [(WA.ap[:, k, c0:c0 + 128], Hb.ap[:, k, :]) for k in range(8)], [WA, Hb], [bk])
                t1 = tmp(); t2 = tmp(); t3 = tmp()
                CP("act", t1.ap, pu.ap[:, 0:SEG], [pu], [t1])
                TTo("dve", t2.ap, pc.ap[:, 0:SEG], t1.ap, ALU.mult, [pc, t1], [t2])
                conv3(t3, t2.ap, [t2, pv], False, pvc(f"oconv{L}_0", j), pvc(f"oconv{L}_1", j), pvc(f"oconv{L}_2", j), nr, rl)
                TTo("dve", MIX.ap[:, j, :], pb.ap[:, 0:SEG], t3.ap, ALU.mult, [pb, t3], [MIX])
            if si + 1 < len(segs):
                sn = segs[si + 1]
                load_x(XT[sn % 2], sn)
                norm_mod(XT[sn % 2], L, 0, 1, 0 if sn > 0 else 1)
            for oc in range(8):
                bk = bank()
                MM(bk.ap[:, 0:SEG], [(WB.ap[:, k, oc * 128:(oc + 1) * 128], MIX.ap[:, k, :]) for k in range(8)], [WB, MIX], [bk])
                CP("act", Yt.ap[:, oc, :], bk.ap[:, 0:SEG], [bk], [Yt])
            residual(xt, L, 2, col)
            store_x(xt, s)

    def pass_F(L, segs):
        load_w(WA, ffn_w_up[L].rearrange("(k p) n -> p k n", p=128), 2 * DFF, 8)
        load_w(WB, ffn_w_down[L].rearrange("(k p) n -> p k n", p=128), D, NFF)
        load_x(XT[segs[0] % 2], segs[0])
        norm_mod(XT[segs[0] % 2], L, 3, 4, 0 if segs[0] > 0 else 1)
        for si, s in enumerate(segs):
            xt = XT[s % 2]
            col = 0 if s > 0 else 1
            nr, rl = grid(s)
            for j in range(NFF):
                pc = bank(); pg = bank()
                for (bk, c0) in ((pc, j * 128), (pg, DFF + j * 128)):
                    MM(bk.ap[:, 0:SEG], [(WA.ap[:, k, c0:c0 + 128], Hb.ap[:, k, :]) for k in range(8)], [WA, Hb], [bk])
                t1 = tmp(); t2 = tmp()
                conv3(t1, pc.ap[:, 0:SEG], [pc, pv], True, pvc(f"fconv{L}_0", j), pvc(f"fconv{L}_1", j), pvc(f"fconv{L}_2", j), nr, rl)
                ACT(t2.ap, t1.ap, AF.Silu, [t1], [t2])
                TTo("dve", ACTF.ap[:, j, :], pg.ap[:, 0:SEG], t2.ap, ALU.mult, [pg, t2], [ACTF])
            if si + 1 < len(segs):
                sn = segs[si + 1]
                load_x(XT[sn % 2], sn)
                norm_mod(XT[sn % 2], L, 3, 4, 0 if sn > 0 else 1)
            for oc in range(8):
                bk = bank()
                MM(bk.ap[:, 0:SEG], [(WB.ap[:, k, oc * 128:(oc + 1) * 128], ACTF.ap[:, k, :]) for k in range(NFF)], [WB, ACTF], [bk])
                CP("act", Yt.ap[:, oc, :], bk.ap[:, 0:SEG], [bk], [Yt])
            residual(xt, L, 5, col)
            store_x(xt, s)

    def pass_out():
        setoff(W_BASE)
        OT = [tile([128, 2, D], F32, f"O_OT{k}") for k in range(2)]
        for s in range(1, NSEG):
            xt = XT[s % 2]
            ot = OT[s % 2]
            load_x(xt, s)
            for hh in range(2):
                for q in range(2):
                    bk = bank()
                    TRS([(bk.ap[:, f * 128:(f + 1) * 128], xt.ap[:, q * 4 + f, hh * 128:(hh + 1) * 128]) for f in range(4)],
                        ident, [xt, cst], [bk])
                    CP("act" if q else "dve", ot.ap[:, hh, q * 512:(q + 1) * 512], bk.ap, [bk], [ot])
            DMA("sp", out_d[(s - 1) * SEG:s * SEG].rearrange("(h p) f -> p h f", p=128), ot.ap, [ot], [OUTB])

    ALLS = list(range(NSEG)); LAT = list(range(1, NSEG))
    plist = []
    for L in range(DEPTH):
        ctx_later = L in (0, 1)
        if L % 2 == 0:
            plist.append((f"E1_{L}", lambda L=L: pass_E1(L, ALLS)))
            plist.append((f"prep_{L}", lambda L=L: pass_prep(L, ALLS)))
            plist.append((f"scan0_{L}", lambda L=L, c=ctx_later: pass_scan(L, 0, c)))
            plist.append((f"scan1_{L}", lambda L=L, c=ctx_later: pass_scan(L, 1, c)))
            plist.append((f"E3_{L}", lambda L=L, c=ctx_later: pass_E3(L, ALLS if c else LAT)))
        else:
            plist.append((f"O_{L}", lambda L=L, c=ctx_later: pass_O(L, ALLS if c else LAT)))
        plist.append((f"F_{L}", lambda L=L, c=ctx_later: pass_F(L, ALLS if c else LAT)))
    plist.append(("out", pass_out))
    if debug_stop in ("mod", "mod1"):
        plist = []
    if debug_stop == "prep_only":
        plist = [("prep_only", lambda: pass_prep(0, list(range(int(os.environ.get("PSEGS", "2"))))))]
    if debug_stop == "mini":
        plist = [("a", lambda: pass_E1(0, [0, 1])), ("b", lambda: pass_prep(0, [0, 1])),
                 ("c", lambda: pass_scan(0, 0, True, [0, 1])), ("mini", lambda: pass_scan(0, 1, True, [0]))]
    if debug_stop == "mini2":
        plist = [("a", lambda: pass_E1(0, [0, 1])), ("b", lambda: pass_prep(0, [0, 1])),
                 ("c", lambda: pass_scan(0, 0, True, [0, 1])), ("d", lambda: pass_scan(0, 1, True, [0])),
                 ("e", lambda: pass_E3(0, [0])), ("f", lambda: pass_F(0, [0])), ("g", lambda: pass_O(1, [0])), ("mini2", lambda: pass_F(1, [0]))]
    if debug_stop == "testB":
        plist = [("f", lambda: pass_F(0, [1])), ("g", lambda: pass_O(1, [1])), ("testB", lambda: pass_F(1, [1]))]
    if debug_stop == "testC":
        plist = [("testC", pass_out)]
    if debug_stop == "scan_only":
        plist = [("scan_only", lambda: pass_scan(0, 0, True))]
    snaps = {}
    for name, fn in plist:
        fn()
        if debug_stop in ("mini2", "testB") and name in ("e", "f", "g", "mini2", "testB"):
            sn = nc.dram_tensor("snap_" + name, [128, 8 * SEG], F32, kind="Internal").ap()
            sb = T(sn, Buf("snap_" + name))
            DMA("sp", sn, XS.ap[0 if debug_stop == "mini2" else 1], [XS], [sb])
            snaps[name] = sb
        if debug_stop == name:
            break
    if debug_stop:
        dbg = nc.dram_tensor("dbg_mdv", [128, DEPTH * 6 * 8 * 2], F32, kind="ExternalOutput").ap()
        dbt = T(dbg, Buf("dbg"))
        DMA("sp", dbg, mdv.ap.rearrange("p a b c d -> p (a b c d)"), [mdv], [dbt])
        fin = [t.b for t in scr_list if t.b.lw is not None and t.b.name in debug_outs] + [dbt.b]
        if OUTB.b.lw is not None:
            fin.append(OUTB.b)
        S.finish("sp", fin)
    else:
        S.finish("sp", [OUTB.b])
    S.emit()
    return nc

_NC_CACHE = {}

def kernel(**inputs):
    inp = {k: np.asarray(v) for k, v in inputs.items()}
    if "nc" not in _NC_CACHE:
        _NC_CACHE["nc"] = build_program()
    nc = _NC_CACHE["nc"]
    f = lambda a: np.ascontiguousarray(a, dtype=np.float32)
    shared = {k: f(inp[k]) for k in ("w_mod", "ffn_w_up", "ffn_w_down", "ev_w_in", "ev_w_out", "ev_w_up", "ev_a_up",
                                     "ev_g_up", "ev_pool_w", "od_w_in", "od_w_out")}
    shared["consts"] = CONSTS
    in_maps = []
    for b in range(8):
        m = dict(shared)
        m["x"] = f(inp["x"][b]); m["ctx"] = f(inp["ctx"][b]); m["pvec"] = build_pvec(inp, b)
        in_maps.append(m)
    res = run_bass_kernel_spmd(nc, in_maps, core_ids=list(range(8)))
    return np.stack([np.asarray(r["out"], dtype=np.float32) for r in res.results], axis=0)
```

```python
import os
import numpy as np
import concourse.bass as bass
import concourse.mybir as mybir
from concourse.bass_utils import run_bass_kernel_spmd

F32 = mybir.dt.float32
BF16 = mybir.dt.bfloat16
AF = mybir.ActivationFunctionType
ALU = mybir.AluOpType

D = 1024
SEQ = 4096
CTX = 256
TALL = SEQ + CTX
SEG = 256
NSEG = TALL // SEG
CH = 64
NCH = TALL // CH
DFF = 2816
NFF = DFF // 128
DEPTH = 4
RMS_EPS = 1e-6
GN_EPS = 64e-5
EM05 = float(np.exp(-0.5))
KLIM = int(os.environ.get('KLIM', '99'))

PGRP = [(i * 128, 128) for i in range(12)] + [(1536, 64), (1600, 64), (1664, 64), (1728, 96)]

CO = {}
def _build_consts():
    cols = []
    def add(name, arr):
        CO[name] = sum(a.shape[1] for a in cols)
        cols.append(arr.astype(np.float32))
    p = np.arange(128)[:, None]
    j = np.arange(128)[None, :]
    add("ident", (p == j))
    add("bd", ((p // 64) == (j // 64)))
    add("ones", np.ones((128, 128)))
    add("restart", np.tile(((np.arange(256) % 64) != 0)[None, :], (128, 1)))
    def inv(row_len):
        pos = np.arange(row_len)
        out = []
        for win in (2, 4, 8, 16):
            lo = np.clip(pos - win // 2, 0, row_len)
            hi = np.clip(pos + win // 2, 0, row_len)
            out.append(1.0 / (hi - lo))
        return np.tile(np.concatenate(out)[None, :], (128, 1))
    add("inv_lat", inv(64))
    add("inv_ctx", inv(256))
    s = np.arange(128)[:, None] % 64
    t = np.arange(64)[None, :]
    add("m1f", np.concatenate([(s < t), (s <= t)], 1))
    add("m1b", np.concatenate([(s > t), (s >= t)], 1))
    add("m3f", (t < s))
    add("m3b", (t > s))
    add("id64", (s == t))
    return np.concatenate(cols, 1)
CONSTS = _build_consts()
NCONST = CONSTS.shape[1]

PV = {}
def _pv_layout():
    n = 0
    def add(name, w):
        nonlocal n
        PV[name] = n
        n += w
    add("c", 8); add("cctx", 8)
    for L in range(DEPTH):
        for jn in range(4):
            add(f"g{L}_{jn}", 8)
        add(f"bmod{L}", 48)
        for tp in range(3):
            add(f"fconv{L}_{tp}", NFF)
        if L % 2 == 0:
            add(f"mu{L}_0", 16); add(f"mu{L}_1", 16)
            for d in range(2):
                add(f"w0{L}_{d}", 4); add(f"a0{L}_{d}", 4)
            for nm in ("kk", "ka", "rk", "gnw", "gnb", "psc"):
                add(f"{nm}{L}", 4)
        else:
            for tp in range(3):
                add(f"oconv{L}_{tp}", 8)
    return n
NPV = _pv_layout()

def _colmajor(v):
    return np.ascontiguousarray(np.asarray(v, np.float32).reshape(-1, 128).T)

def build_pvec(inp, b):
    t = np.zeros((128, NPV), np.float32)
    def put(name, arr):
        t[:, PV[name]:PV[name] + arr.shape[1]] = arr
    put("c", _colmajor(inp["c"][b])); put("cctx", _colmajor(inp["c_ctx"]))
    for L in range(DEPTH):
        i = L // 2
        for jn in range(4):
            put(f"g{L}_{jn}", _colmajor(inp["norm_g"][L, jn]))
        put(f"bmod{L}", _colmajor(inp["b_mod"][L]))
        for tp in range(3):
            put(f"fconv{L}_{tp}", _colmajor(inp["ffn_conv"][L, tp]))
        if L % 2 == 0:
            for m in range(2):
                a = np.zeros((128, 16), np.float32)
                for gi, (st, sz) in enumerate(PGRP):
                    a[:sz, gi] = inp["ev_mu"][i, m, st:st + sz]
                put(f"mu{L}_{m}", a)
            for d in range(2):
                put(f"w0{L}_{d}", _colmajor(inp["ev_w0"][i, d]))
                put(f"a0{L}_{d}", _colmajor(inp["ev_a0"][i, d]))
            put(f"kk{L}", _colmajor(inp["ev_k_k"][i])); put(f"ka{L}", _colmajor(inp["ev_k_a"][i]))
            put(f"rk{L}", _colmajor(inp["ev_r_k"][i].reshape(-1)))
            put(f"gnw{L}", _colmajor(inp["ev_gn_w"][i])); put(f"gnb{L}", _colmajor(inp["ev_gn_b"][i]))
            put(f"psc{L}", _colmajor(inp["ev_pool_scale"][i]))
        else:
            for tp in range(3):
                put(f"oconv{L}_{tp}", _colmajor(inp["od_conv"][i, tp]))
    return t

class Buf:
    __slots__ = ("name", "lw", "rd", "dsem", "dcnt", "rng", "al")
    def __init__(self, name, rng=None):
        self.name = name; self.lw = None; self.rd = []; self.dsem = None; self.dcnt = 0
        self.rng = rng; self.al = []

class Sched:
    ENGS = ("pe", "act", "dve", "pool", "sp")
    def __init__(self, nc):
        self.nc = nc
        self.ops = {e: [] for e in self.ENGS}
        self.cnt = {e: 0 for e in self.ENGS}
        self.waited = {e: {} for e in self.ENGS}
        self.sems = {e: nc.alloc_semaphore("s_" + e) for e in self.ENGS}
        self.key = {e: e for e in self.ENGS}
        self.epoch = {e: 0 for e in self.ENGS}
        self.nsem = 0
        self.sb = []
    def sbuf(self, name, lo, hi):
        b = Buf(name, (lo, hi))
        for o in self.sb:
            if o.rng[0] < hi and lo < o.rng[1]:
                o.al.append(b); b.al.append(o)
        self.sb.append(b)
        return b
    def _waits(self, eng, evs, pe_self=False):
        w = self.waited[eng]; best = {}
        for ev in evs:
            if ev is None:
                continue
            k, v = ev
            if pe_self and k.startswith("pe"):
                continue
            if w.get(k, 0) >= v:
                continue
            if best.get(k, 0) < v:
                best[k] = v
        for k, v in best.items():
            w[k] = v
        return list(best.items())
    def _gather(self, reads, writes):
        evs = []
        for b in reads:
            evs.append(b.lw)
            for o in b.al:
                evs.append(o.lw)
        for b in writes:
            evs.append(b.lw); evs.extend(b.rd)
            for o in b.al:
                evs.append(o.lw); evs.extend(o.rd)
        return evs
    def op(self, eng, fn, reads=(), writes=()):
        waits = self._waits(eng, self._gather(reads, writes), pe_self=(eng == "pe"))
        if self.cnt[eng] >= 12000:
            self.epoch[eng] += 1
            self.cnt[eng] = 0
            self.key[eng] = "%s#%d" % (eng, self.epoch[eng])
            self.sems[self.key[eng]] = self.nc.alloc_semaphore("s_%s_%d" % (eng, self.epoch[eng]))
        self.cnt[eng] += 1
        ev = (self.key[eng], self.cnt[eng])
        for b in reads:
            b.rd.append(ev)
        for b in writes:
            b.lw = ev; b.rd = []
        self.ops[eng].append((fn, waits, (self.key[eng], 1)))
    def dma(self, q, fn, reads=(), writes=(), n=1):
        (dst,) = writes
        if dst.dsem is None:
            key = "d%d" % self.nsem
            self.nsem += 1
            self.sems[key] = self.nc.alloc_semaphore(key)
            dst.dsem = key
        waits = self._waits(q, self._gather(reads, writes))
        dst.dcnt += 16 * n
        ev = (dst.dsem, dst.dcnt)
        for b in reads:
            b.rd.append(ev)
        dst.lw = ev; dst.rd = []
        self.ops[q].append((fn, waits, (dst.dsem, 16)))
    def finish(self, eng, bufs):
        self.ops[eng].append((None, self._waits(eng, [b.lw for b in bufs]), None))
    def emit(self):
        nc = self.nc; sems = self.sems
        with nc.Block() as block:
            def run(name, e):
                for fn, waits, inc in self.ops[name]:
                    for k, v in waits:
                        e.wait_ge(sems[k], v)
                    if fn is None:
                        continue
                    r = fn(e)
                    if isinstance(r, (list, tuple)):
                        for ins in r:
                            ins.then_inc(sems[inc[0]], inc[1])
                    else:
                        r.then_inc(sems[inc[0]], inc[1])
            @block.tensor
            def _(e): run("pe", e)
            @block.scalar
            def _(e): run("act", e)
            @block.vector
            def _(e): run("dve", e)
            @block.gpsimd
            def _(e): run("pool", e)
            @block.sync
            def _(e): run("sp", e)

_DS = {F32: 4, BF16: 2}

class T:
    __slots__ = ("ap", "b")
    def __init__(self, ap, b):
        self.ap = ap; self.b = b

def build_program(debug_stop=None, debug_outs=()):
    nc = bass.Bass("TRN2", target_bir_lowering=False)
    S = Sched(nc)
    dram = {}
    def din(name, shape, dt=F32):
        dram[name] = nc.dram_tensor(name, list(shape), dt, kind="ExternalInput").ap()
        return dram[name]
    x_in = din("x", [SEQ, D]); ctx_in = din("ctx", [CTX, D])
    consts_in = din("consts", [128, NCONST]); pvec_in = din("pvec", [128, NPV])
    w_mod = din("w_mod", [DEPTH, D, 6 * D]); ffn_w_up = din("ffn_w_up", [DEPTH, D, 2 * DFF])
    ffn_w_down = din("ffn_w_down", [DEPTH, DFF, D])
    ev_w_in = din("ev_w_in", [2, D, 2336]); ev_w_out = din("ev_w_out", [2, D, D])
    ev_w_up = din("ev_w_up", [2, 2, 32, 512]); ev_a_up = din("ev_a_up", [2, 2, 64, 512])
    ev_g_up = din("ev_g_up", [2, 96, 512]); ev_pool_w = din("ev_pool_w", [2, 4, 128, 128])
    od_w_in = din("od_w_in", [2, D, 3 * D]); od_w_out = din("od_w_out", [2, D, D])
    out_d = nc.dram_tensor("out", [SEQ, D], F32, kind="ExternalOutput").ap()
    IN = Buf("inputs")
    OUTB = T(None, Buf("out"))
    scr_list = []
    def dscr(name, shape, dt):
        t = T(nc.dram_tensor(name, list(shape), dt, kind=("ExternalOutput" if name in debug_outs else "Internal")).ap(), Buf(name))
        scr_list.append(t)
        return t
    XS = dscr("XS", [NSEG, 128, 8 * SEG], F32)
    PR = dscr("PR", [NSEG, 128, 16 * SEG], F32)
    YP = dscr("YP", [NSEG, 128, 4 * SEG], BF16)
    FM = [dscr(f"FM{d}", [NSEG, 128, 4 * 1024], BF16) for d in range(2)]
    TMD = dscr("TMD", [TALL, 5, 512], BF16)
    GC = [dscr(f"GC{d}", [NSEG, 128, 16], F32) for d in range(2)]
    BGD = dscr("BGD", [NSEG, 128, 2 * 4 * SEG], BF16)
    YD = [dscr(f"YD{d}", [NSEG, 128, 4 * SEG], F32) for d in range(2)]

    cur = [0]
    def setoff(o):
        cur[0] = o
    tcache = {}
    def tile(shape, dt, name):
        nbytes = int(np.prod(shape[1:])) * _DS[dt]
        nbytes = (nbytes + 31) // 32 * 32
        off = cur[0]
        cur[0] += nbytes
        assert cur[0] <= 229376, (name, cur[0])
        if name in tcache:
            assert tcache[name][1] == off, name
            return tcache[name][0]
        h = nc.alloc_sbuf_tensor_at(name, list(shape), dt, offset=off)
        t = T(h.ap(), S.sbuf(name, off, off + nbytes))
        tcache[name] = (t, off)
        return t
    banks = []
    for i in range(7):
        banks.append(T(nc.alloc_psum_tensor(f"bank{i}", [128, 512], F32).ap(), Buf(f"bank{i}")))
    bankb_ap = nc.alloc_psum_tensor("bankb", [128, 1024], BF16).ap()
    _bb = Buf("bankb")
    bankb = [T(bankb_ap[:, 0:512], _bb), T(bankb_ap[:, 512:1024], _bb)]
    bki = [0, 0]
    def bank():
        bki[0] = (bki[0] + 1) % 7
        return banks[bki[0]]
    def bankbf():
        bki[1] = (bki[1] + 1) % 2
        return bankb[bki[1]]

    def bl(ts):
        return [t.b for t in ts]
    def TTo(eng, out, a, b, op, r, w):
        S.op(eng, lambda e: e.tensor_tensor(out=out, in0=a, in1=b, op=op), reads=bl(r), writes=bl(w))
    def STT(eng, out, in0, scalar, in1, op0, op1, r, w):
        S.op(eng, lambda e: e.scalar_tensor_tensor(out=out, in0=in0, scalar=scalar, in1=in1, op0=op0, op1=op1),
             reads=bl(r), writes=bl(w))
    def TS(eng, out, in0, s1, s2, op0, op1, r, w):
        if op1 is None:
            S.op(eng, lambda e: e.tensor_scalar(out=out, in0=in0, scalar1=s1, scalar2=None, op0=op0),
                 reads=bl(r), writes=bl(w))
        else:
            S.op(eng, lambda e: e.tensor_scalar(out=out, in0=in0, scalar1=s1, scalar2=s2, op0=op0, op1=op1),
                 reads=bl(r), writes=bl(w))
    def ACT(out, in_, func, r, w, scale=None, bias=None):
        kw = {}
        if scale is not None:
            kw["scale"] = scale
        if bias is not None:
            kw["bias"] = bias
        S.op("act", lambda e: e.activation(out=out, in_=in_, func=func, **kw), reads=bl(r), writes=bl(w))
    def CP(eng, out, in_, r, w):
        if eng == "act":
            S.op("act", lambda e: e.copy(out=out, in_=in_), reads=bl(r), writes=bl(w))
        else:
            S.op(eng, lambda e: e.tensor_copy(out=out, in_=in_), reads=bl(r), writes=bl(w))
    def RECIP(out, in_, r, w):
        S.op("dve", lambda e: e.reciprocal(out=out, in_=in_), reads=bl(r), writes=bl(w))
    def MM(out, pairs, r, w):
        def fn(e):
            ins = None
            n = len(pairs)
            for i, (l, rh) in enumerate(pairs):
                ins = e.matmul(out, lhsT=l, rhs=rh, start=(i == 0), stop=(i == n - 1))
            return ins
        S.op("pe", fn, reads=bl(r), writes=bl(w))
    def MMS(items, r, w):
        def fn(e):
            ins = None
            for out, pairs in items:
                n = len(pairs)
                for i, (l, rh) in enumerate(pairs):
                    ins = e.matmul(out, lhsT=l, rhs=rh, start=(i == 0), stop=(i == n - 1))
            return ins
        S.op("pe", fn, reads=bl(r), writes=bl(w))
    def TRS(items, ident, r, w):
        def fn(e):
            ins = None
            for out, in_ in items:
                ins = e.transpose(out, in_, ident)
            return ins
        S.op("pe", fn, reads=bl(r), writes=bl(w))
    def DMA(q, out, in_, r, w, **kw):
        S.dma(q, lambda e: e.dma_start(out=out, in_=in_, **kw), reads=bl(r), writes=bl(w))
    def DMAS(q, pairs, r, w):
        S.dma(q, lambda e: [e.dma_start(out=o, in_=i) for (o, i) in pairs], reads=bl(r), writes=bl(w), n=len(pairs))
    TIN = T(None, IN)

    setoff(16640)
    cst = tile([128, NCONST], F32, "cst")
    pv = tile([128, NPV], F32, "pv")
    identb = tile([128, 128], BF16, "identb")
    bdb = tile([128, 128], BF16, "bdb")
    onesb = tile([128, 128], BF16, "onesb")
    modt = tile([128, DEPTH, 48, 2], F32, "modt")
    HALO = tile([128, 16, NSEG, 2], F32, "HALO")
    mdv = tile([128, DEPTH, 6, 8, 2], F32, "mdv")
    dvec = tile([128, 2, 48], F32, "dvec")
    silc = tile([128, 8, 2], F32, "silc")
    wupb = tile([64, 512], BF16, "wupb")
    aupb = [tile([64, 512], BF16, f"aupb{d}") for d in range(2)]
    gupb = tile([96, 512], BF16, "gupb")
    poolwb = tile([128, 4, 128], BF16, "poolwb")
    CONST_END = cur[0]
    W_BASE = (CONST_END + 63) // 64 * 64
    WA_BYTES = 8 * 5632 * 2
    WB_BYTES = NFF * 1024 * 2
    setoff(W_BASE)
    WA = tile([128, 8, 5632], BF16, "WA")
    WB = tile([128, NFF, 1024], BF16, "WB")
    A_BASE = cur[0]

    def cc(name, n=128):
        return cst.ap[:, CO[name]:CO[name] + n]
    def pvc(name, j, parts=128):
        return pv.ap[0:parts, PV[name] + j:PV[name] + j + 1]

    DMA("sp", cst.ap, consts_in, [TIN], [cst])
    DMA("sp", pv.ap, pvec_in, [TIN], [pv])
    CP("dve", identb.ap, cc("ident"), [cst], [identb])
    CP("dve", bdb.ap, cc("bd"), [cst], [bdb])
    CP("dve", onesb.ap, cc("ones"), [cst], [onesb])
    ident = cc("ident")

    setoff(W_BASE)
    wm = [tile([128, 8, 768], F32, f"wm{i}") for i in range(2)]
    ACT(silc.ap[:, :, 0], pv.ap[:, PV["c"]:PV["c"] + 8], AF.Silu, [pv], [silc])
    ACT(silc.ap[:, :, 1], pv.ap[:, PV["cctx"]:PV["cctx"] + 8], AF.Silu, [pv], [silc])
    pi = 0
    for L in range(1 if debug_stop in ("mod1", "prep_only", "scan_only", "mini") else (2 if debug_stop in ("mini2", "testB") else (1 if debug_stop == "testC" else DEPTH))):
        wv = w_mod[L].rearrange("(k p) n -> p k n", p=128)
        bk = bank()
        for pc in range(8):
            wt = wm[pi % 2]; pi += 1
            DMA("sp", wt.ap, wv[:, :, pc * 768:(pc + 1) * 768], [TIN], [wt])
            items = []
            for j in range(6):
                nchunk = pc * 6 + j
                items.append((bk.ap[:, nchunk * 2:nchunk * 2 + 2],
                              [(wt.ap[:, k, j * 128:(j + 1) * 128], silc.ap[:, k, :]) for k in range(8)]))
            MMS(items, [wt, silc], [bk])
        bm = pv.ap[:, PV[f"bmod{L}"]:PV[f"bmod{L}"] + 48].unsqueeze(2).to_broadcast([128, 48, 2])
        TTo("dve", modt.ap[:, L], bk.ap[:, 0:96].rearrange("p (n c) -> p n c", c=2), bm, ALU.add, [bk, pv], [modt])
        def gb(jn):
            return pv.ap[:, PV[f"g{L}_{jn}"]:PV[f"g{L}_{jn}"] + 8].unsqueeze(2).to_broadcast([128, 8, 2])
        def mo(k):
            return modt.ap[:, L, k * 8:(k + 1) * 8, :]
        for (dst, g, sc) in ((0, 0, 1), (3, 2, 4)):
            STT("dve", mdv.ap[:, L, dst], mo(sc), 1.0, gb(g), ALU.add, ALU.mult, [modt, pv], [mdv])
        for (dst, src) in ((1, 0), (4, 3)):
            CP("dve", mdv.ap[:, L, dst], mo(src), [modt], [mdv])
        for (dst, g, gt) in ((2, 1, 2), (5, 3, 5)):
            TTo("dve", mdv.ap[:, L, dst], mo(gt), gb(g), ALU.mult, [modt, pv], [mdv])
    def MV(L, kind, fc, col):
        return mdv.ap[:, L, kind, fc, col:col + 1]

    setoff(A_BASE)
    XT = [tile([128, 8, SEG], F32, f"XT{i}") for i in range(2)]
    Hb = tile([128, 8, SEG], BF16, "Hb")
    SQ = tile([128, 8, SEG], BF16, "SQ")
    Yt = tile([128, 8, SEG], F32, "Yt")
    ACTF = tile([128, NFF, SEG], BF16, "ACTF")
    RSTD = tile([128, SEG], F32, "RSTD")
    TMP = [tile([128, SEG], F32, f"TMP{i}") for i in range(6)]
    ACT_END = cur[0]
    tmpi = [0]
    def tmp():
        tmpi[0] = (tmpi[0] + 1) % 6
        return TMP[tmpi[0]]

    def segcol(s):
        return s * SEG

    def rms_rstd(src, srcbufs, nchunks, lhs_ones, scale, eps, dstr):
        bk = bank()
        for fc in range(nchunks):
            ACT(SQ.ap[:, fc, :], src[:, fc, :], AF.Square, srcbufs, [SQ])
        MM(bk.ap[:, 0:SEG], [(lhs_ones, SQ.ap[:, fc, :]) for fc in range(nchunks)], [SQ, onesb, bdb], [bk])
        t1 = tmp()
        TS("dve", t1.ap, bk.ap[:, 0:SEG], scale, eps, ALU.mult, ALU.add, [bk], [t1])
        ACT(t1.ap, t1.ap, AF.Sqrt, [t1], [t1])
        RECIP(dstr.ap, t1.ap, [t1], [dstr])

    def norm_mod(xt, L, ka, kb, col):
        rms_rstd(xt.ap, [xt], 8, onesb.ap, 1.0 / D, RMS_EPS, RSTD)
        for fc in range(8):
            t1 = tmp()
            TTo("dve", t1.ap, xt.ap[:, fc, :], RSTD.ap, ALU.mult, [xt, RSTD], [t1])
            ACT(Hb.ap[:, fc, :], t1.ap, AF.Identity, [t1, mdv], [Hb], scale=MV(L, ka, fc, col), bias=MV(L, kb, fc, col))

    def residual(xt, L, kg, col):
        rms_rstd(Yt.ap, [Yt], 8, onesb.ap, 1.0 / D, RMS_EPS, RSTD)
        for fc in range(8):
            t1 = tmp()
            STT("dve", t1.ap, Yt.ap[:, fc, :], MV(L, kg, fc, col), RSTD.ap, ALU.mult, ALU.mult, [Yt, mdv, RSTD], [t1])
            TTo("pool", xt.ap[:, fc, :], xt.ap[:, fc, :], t1.ap, ALU.add, [xt, t1], [xt])

    def load_w(dst, dview, ncols, kch):
        c0 = 0
        while c0 < ncols:
            c1 = min(ncols, c0 + 2048)
            for k0 in range(0, kch, 4):
                k1 = min(kch, k0 + 4)
                DMA("pool", dst.ap[:, k0:k1, c0:c1], dview[:, k0:k1, c0:c1], [TIN], [dst])
            c0 = c1

    def load_x(xt, s):
        DMA("sp", xt.ap.rearrange("p k t -> p (k t)"), XS.ap[s], [XS], [xt])
    def store_x(xt, s):
        DMA("sp", XS.ap[s], xt.ap.rearrange("p k t -> p (k t)"), [xt], [XS])

    def conv3(dst, src, srcbufs, src_is_psum, w0, w1, w2, nrows, rl):
        ACT(dst.ap, src, AF.Identity, srcbufs, [dst], scale=w1)
        dv = dst.ap.rearrange("p (r c) -> p r c", c=rl)
        sv = src.rearrange("p (r c) -> p r c", c=rl)
        STT("dve", dv[:, :, 1:rl], sv[:, :, 0:rl - 1], w0, dv[:, :, 1:rl], ALU.mult, ALU.add, srcbufs + [dst], [dst])
        STT("dve", dv[:, :, 0:rl - 1], sv[:, :, 1:rl], w2, dv[:, :, 0:rl - 1], ALU.mult, ALU.add, srcbufs + [dst], [dst])

    def grid(s):
        return (1, 256) if s == 0 else (4, 64)

    def pass_E1(L, segs):
        i = L // 2
        load_w(WA, ev_w_in[i].rearrange("(k p) n -> p k n", p=128), 2336, 8)
        DMA("pool", poolwb.ap, ev_pool_w[i].rearrange("g c d -> c g d"), [TIN], [poolwb])
        setoff(W_BASE + WA_BYTES)
        TM = tile([128, 2, D], F32, "E1_TM")
        PRS = tile([128, 16, SEG], F32, "E1_PRS")
        XP = tile([128, 4, 384], F32, "E1_XP")
        S2 = tile([128, 384], F32, "E1_S2"); S4 = tile([128, 384], F32, "E1_S4")
        S8 = tile([128, 384], F32, "E1_S8"); S16 = tile([128, 384], F32, "E1_S16")
        DB = tile([128, SEG], BF16, "E1_DB")
        YPT = tile([128, 4, SEG], BF16, "E1_YPT")
        stg_i = 0
        for s in segs:
            if s in (0, 1):
                S.op("pool", lambda e: e.memset(XP.ap, 0.0), writes=[XP.b])
            if s == 0:
                S.op("pool", lambda e: e.memset(PRS.ap, 0.0), writes=[PRS.b])
                S.op("pool", lambda e: e.memset(HALO.ap, 0.0), writes=[HALO.b])
            xt = XT[s % 2]
            col = 0 if s > 0 else 1
            if L == 0:
                src = ctx_in if s == 0 else x_in[(s - 1) * SEG:s * SEG]
                DMA("sp", TM.ap, src.rearrange("(h p) f -> p h f", p=128), [TIN], [TM])
                for fc in range(8):
                    bk = bank()
                    TRS([(bk.ap[:, hh * 128:(hh + 1) * 128], TM.ap[:, hh, fc * 128:(fc + 1) * 128]) for hh in range(2)],
                        ident, [TM, cst], [bk])
                    CP("act" if fc % 2 else "dve", xt.ap[:, fc, :], bk.ap[:, 0:SEG], [bk], [xt])
                store_x(xt, s)
            else:
                load_x(xt, s)
            norm_mod(xt, L, 0, 1, col)
            for gi, (st, sz) in enumerate(PGRP):
                bk = bank()
                MM(bk.ap[0:sz, 0:SEG], [(WA.ap[:, k, st:st + sz], Hb.ap[:, k, :]) for k in range(8)], [WA, Hb], [bk])
                CP("act", PRS.ap[0:sz, gi, :], bk.ap[0:sz, 0:SEG], [bk], [PRS])
            CP("pool", HALO.ap[:, :, s, 0], PRS.ap[:, :, 0], [PRS], [HALO])
            CP("pool", HALO.ap[:, :, s, 1], PRS.ap[:, :, SEG - 1], [PRS], [HALO])
            DMA("sp", PR.ap[s], PRS.ap.rearrange("p g t -> p (g t)"), [PRS], [PR])
            nr, rl = grid(s)
            pw = rl + 32
            invn = "inv_ctx" if s == 0 else "inv_lat"
            for gi in range(4):
                win = (2, 4, 8, 16)[gi]
                bk = bank()
                c0 = 1824 + gi * 128
                MM(bk.ap[:, 0:SEG], [(WA.ap[:, k, c0:c0 + 128], Hb.ap[:, k, :]) for k in range(8)], [WA, Hb], [bk])
                xpv = XP.ap[:, gi, 0:nr * pw].rearrange("p (r c) -> p r c", c=pw)
                CP("act", xpv[:, :, 16:16 + rl], bk.ap[:, 0:SEG].rearrange("p (r c) -> p r c", c=rl), [bk], [XP])
                prev = xpv; prevb = XP; sh = 1
                for (St, wn) in ((S2, 2), (S4, 4), (S8, 8), (S16, 16)):
                    if wn > win:
                        break
                    sv = St.ap[:, 0:nr * pw].rearrange("p (r c) -> p r c", c=pw)
                    lo = wn - 1
                    TTo("pool" if gi % 2 else "dve", sv[:, :, lo:pw], prev[:, :, lo:pw], prev[:, :, lo - sh:pw - sh], ALU.add, [prevb], [St])
                    prev = sv; prevb = St; sh = wn
                o = 16 + win // 2 - 1
                t1 = tmp()
                t1v = t1.ap.rearrange("p (r c) -> p r c", c=rl)
                iv = cst.ap[:, CO[invn] + gi * rl:CO[invn] + (gi + 1) * rl].unsqueeze(1).to_broadcast([128, nr, rl])
                TTo("dve", t1v, prev[:, :, o:o + rl], iv, ALU.mult, [prevb, cst], [t1])
                TTo("dve", DB.ap.rearrange("p (r c) -> p r c", c=rl), t1v, xpv[:, :, 16:16 + rl], ALU.subtract, [t1, XP], [DB])
                bk2 = bank()
                MM(bk2.ap[:, 0:SEG], [(poolwb.ap[:, gi, :], DB.ap)], [poolwb, DB], [bk2])
                ACT(YPT.ap[:, gi, :], bk2.ap[:, 0:SEG], AF.Identity, [bk2, pv], [YPT], scale=pvc(f"psc{L}", gi))
            DMA("sp", YP.ap[s], YPT.ap.rearrange("p k t -> p (k t)"), [YPT], [YP])

    def pass_prep(L, segs):
        i = L // 2
        DMA("pool", wupb.ap, ev_w_up[i].rearrange("d r c -> (d r) c"), [TIN], [wupb])
        for d in range(2):
            DMA("pool", aupb[d].ap, ev_a_up[i, d], [TIN], [aupb[d]])
        DMA("pool", gupb.ap, ev_g_up[i], [TIN], [gupb])
        m0 = pv.ap[:, PV[f"mu{L}_0"]:PV[f"mu{L}_0"] + 16]
        m1 = pv.ap[:, PV[f"mu{L}_1"]:PV[f"mu{L}_1"] + 16]
        dv = dvec.ap[:, i]
        TTo("dve", dv[:, 0:16], m0, m1, ALU.add, [pv], [dvec])
        TS("dve", dv[:, 0:16], dv[:, 0:16], -1.0, 1.0, ALU.mult, ALU.add, [dvec], [dvec])
        TS("dve", dv[:, 16:20], pv.ap[:, PV[f"ka{L}"]:PV[f"ka{L}"] + 4], -1.0, 1.0, ALU.mult, ALU.add, [pv], [dvec])
        setoff(W_BASE)
        PRM = [tile([128, 16, SEG], F32, f"P_PRM{k}") for k in range(2)]
        PRT1 = tile([128, 16, SEG + 2], F32, "P_PRT")
        SH = tile([128, 16, SEG], F32, "P_SH")
        KKt = tile([128, 4, SEG], F32, "P_KK")
        At = [tile([128, 4, SEG], F32, f"P_A{d}") for d in range(2)]
        KDt = [tile([128, 4, SEG], F32, f"P_KD{d}") for d in range(2)]
        Bt = [tile([128, 4, SEG], F32, f"P_B{d}") for d in range(2)]
        SG = [tile([128, 4, SEG], F32, f"P_SG{d}") for d in range(2)]
        CUM = tile([128, SEG], F32, "P_CUM")
        E = [tile([128, SEG], F32, f"P_E{k}") for k in range(4)]
        TLb = tile([64, SEG], BF16, "P_TLb")
        LAb = [tile([64, SEG], BF16, f"P_LAb{d}") for d in range(2)]
        LGb = tile([96, SEG], BF16, "P_LGb")
        SQb = tile([128, SEG], BF16, "P_SQb")
        FMo = [tile([128, 4, 1024], BF16, f"P_FMo{d}") for d in range(2)]
        KGo = [tile([128, 4, SEG], BF16, f"P_KGo{d}") for d in range(2)]
        BGo = [tile([128, 4, SEG], BF16, f"P_BGo{d}") for d in range(2)]
        Vb = tile([128, 4, SEG], BF16, "P_Vb")
        GCo = [tile([128, 4, 4], F32, f"P_GCo{d}") for d in range(2)]
        BGo2 = tile([128, 2, 4, SEG], BF16, "P_BGo2")
        TMo = [tile([128, 5, 512], BF16, f"P_TMo{k}") for k in range(2)]
        tmo_i = [0]

        def load_pr(s):
            DMA("sp", PRM[s % 2].ap.rearrange("p g t -> p (g t)"), PR.ap[s], [PR], [PRM[s % 2]])

        load_pr(segs[0])
        for si, s in enumerate(segs):
            if si + 1 < len(segs):
                load_pr(segs[si + 1])
            pt = PRT1
            t0 = segcol(s)
            c0ch = t0 // CH
            first = (s == 0 or s == 1); last = (s == 0 or s == NSEG - 1)
            CP("pool", pt.ap[:, :, 1:SEG + 1], PRM[s % 2].ap, [PRM[s % 2]], [pt])
            if first:
                S.op("pool", lambda e: e.memset(pt.ap[:, :, 0:1], 0.0), writes=[pt.b])
            else:
                CP("pool", pt.ap[:, :, 0], HALO.ap[:, :, s - 1, 1], [HALO], [pt])
            if last:
                S.op("pool", lambda e: e.memset(pt.ap[:, :, SEG + 1:SEG + 2], 0.0), writes=[pt.b])
            else:
                CP("pool", pt.ap[:, :, SEG + 1], HALO.ap[:, :, s + 1, 0], [HALO], [pt])
            for gi, (st, sz) in enumerate(PGRP):
                dst = SH.ap[0:sz, gi, :]
                ACT(dst, pt.ap[0:sz, gi, 1:SEG + 1], AF.Identity, [pt, dvec], [SH], scale=dvec.ap[0:sz, i, gi:gi + 1])
                STT("dve", dst, pt.ap[0:sz, gi, 0:SEG], pvc(f"mu{L}_0", gi, sz), dst, ALU.mult, ALU.add, [pt, pv, SH], [SH])
                STT("dve", dst, pt.ap[0:sz, gi, 2:SEG + 2], pvc(f"mu{L}_1", gi, sz), dst, ALU.mult, ALU.add, [pt, pv, SH], [SH])
            if KLIM <= 1:
                continue
            Rv = SH.ap[:, 0:4, :]; Kv = SH.ap[:, 4:8, :]; Vv = SH.ap[:, 8:12, :]
            ACT(TLb.ap, SH.ap[0:64, 12, :], AF.Tanh, [SH], [TLb])
            CP("pool", LAb[0].ap, SH.ap[0:64, 13, :], [SH], [LAb[0]])
            CP("pool", LAb[1].ap, SH.ap[0:64, 14, :], [SH], [LAb[1]])
            ACT(LGb.ap, SH.ap[0:96, 15, :], AF.Sigmoid, [SH], [LGb])
            CP("pool", Vb.ap, Vv, [SH], [Vb])
            for c4 in range(4):
                csl = slice(c4 * 128, (c4 + 1) * 128)
                bk = bank()
                MM(bk.ap[:, 0:SEG], [(gupb.ap[:, csl], LGb.ap)], [gupb, LGb], [bk])
                CP("act", BGo2.ap[:, 1, c4, :], bk.ap[:, 0:SEG], [bk], [BGo2])
                t1 = tmp()
                TS("dve", t1.ap, Kv[:, c4, :], pvc(f"kk{L}", c4), None, ALU.mult, None, [SH, pv], [t1])
                ACT(SQb.ap, t1.ap, AF.Square, [t1], [SQb])
                bk = bank()
                MM(bk.ap[:, 0:SEG], [(bdb.ap, SQb.ap)], [bdb, SQb], [bk])
                t2 = tmp()
                TS("dve", t2.ap, bk.ap[:, 0:SEG], 1e-24, None, ALU.max, None, [bk], [t2])
                ACT(t2.ap, t2.ap, AF.Sqrt, [t2], [t2])
                RECIP(t2.ap, t2.ap, [t2], [t2])
                TTo("dve", KKt.ap[:, c4, :], t1.ap, t2.ap, ALU.mult, [t1, t2], [KKt])
                for d in range(2):
                    bk = bank()
                    MM(bk.ap[:, 0:SEG], [(wupb.ap[32 * d:32 * d + 32, csl], TLb.ap[32 * d:32 * d + 32, :])], [wupb, TLb], [bk])
                    ACT(SG[d].ap[:, c4, :], bk.ap[:, 0:SEG], AF.Sigmoid, [bk, pv], [SG[d]], bias=pvc(f"w0{L}_{d}", c4))
                    bk = bank()
                    MM(bk.ap[:, 0:SEG], [(aupb[d].ap[:, csl], LAb[d].ap)], [aupb[d], LAb[d]], [bk])
                    ACT(At[d].ap[:, c4, :], bk.ap[:, 0:SEG], AF.Sigmoid, [bk, pv], [At[d]], bias=pvc(f"a0{L}_{d}", c4))
                    t3 = tmp()
                    TS("dve", t3.ap, At[d].ap[:, c4, :], pvc(f"ka{L}", c4), dvec.ap[:, i, 16 + c4:17 + c4], ALU.mult, ALU.add,
                       [At[d], pv, dvec], [t3])
                    TTo("pool", KDt[d].ap[:, c4, :], Kv[:, c4, :], t3.ap, ALU.mult, [SH, t3], [KDt[d]])
                    TTo("pool", Bt[d].ap[:, c4, :], KKt.ap[:, c4, :], At[d].ap[:, c4, :], ALU.mult, [KKt, At[d]], [Bt[d]])
                    S.op("dve", lambda e, d=d, c4=c4: e.tensor_tensor_scan(out=CUM.ap, data0=cc("restart", 256), data1=SG[d].ap[:, c4, :],
                                                                            initial=0.0, op0=ALU.mult, op1=ALU.add),
                         reads=[cst.b, SG[d].b], writes=[CUM.b])
                    cumv = CUM.ap.rearrange("p (c t) -> p c t", t=CH)
                    sgv = SG[d].ap[:, c4, :].rearrange("p (c t) -> p c t", t=CH)
                    totb = cumv[:, :, CH - 1:CH].to_broadcast([128, 4, CH])
                    e0, e1, e2, e3 = E
                    if d == 1:
                        t4 = tmp()
                        t4v = t4.ap.rearrange("p (c t) -> p c t", t=CH)
                        TTo("dve", t4v, totb, cumv, ALU.subtract, [CUM], [t4])
                        ACT(GCo[d].ap[:, c4, :], cumv[:, :, CH - 1], AF.Exp, [CUM], [GCo[d]], scale=-EM05)
                        TTo("dve", CUM.ap, t4.ap, SG[d].ap[:, c4, :], ALU.add, [t4, SG[d]], [CUM])
                        t5 = tmp()
                        t5v = t5.ap.rearrange("p (c t) -> p c t", t=CH)
                        TTo("dve", t5v, cumv[:, :, 0:1].to_broadcast([128, 4, CH]), cumv, ALU.subtract, [CUM], [t5])
                        tmc = t5
                    else:
                        ACT(GCo[d].ap[:, c4, :], cumv[:, :, CH - 1], AF.Exp, [CUM], [GCo[d]], scale=-EM05)
                        t5 = tmp()
                        t5v = t5.ap.rearrange("p (c t) -> p c t", t=CH)
                        TTo("dve", t5v, totb, cumv, ALU.subtract, [CUM], [t5])
                        tmc = t5
                    t6 = tmp()
                    TTo("pool", t6.ap, CUM.ap, SG[d].ap[:, c4, :], ALU.subtract, [CUM, SG[d]], [t6])
                    ACT(e0.ap, CUM.ap, AF.Exp, [CUM], [e0], scale=-EM05)
                    ACT(e1.ap, t6.ap, AF.Exp, [t6], [e1], scale=-EM05)
                    ACT(e2.ap, CUM.ap, AF.Exp, [CUM], [e2], scale=EM05)
                    ACT(e3.ap, tmc.ap, AF.Exp, [tmc], [e3], scale=-EM05)
                    kqv = FMo[d].ap[:, c4, 0:512].rearrange("p (c a t) -> p c a t", a=2, t=CH)
                    TTo("dve", kqv[:, :, 0, :], KKt.ap[:, c4, :].rearrange("p (c t) -> p c t", t=CH),
                        e1.ap.rearrange("p (c t) -> p c t", t=CH), ALU.mult, [KKt, e1], [FMo[d]])
                    TTo("dve", kqv[:, :, 1, :], Rv[:, c4, :].rearrange("p (c t) -> p c t", t=CH),
                        e0.ap.rearrange("p (c t) -> p c t", t=CH), ALU.mult, [SH, e0], [FMo[d]])
                    TTo("pool", FMo[d].ap[:, c4, 512:768], KDt[d].ap[:, c4, :], e2.ap, ALU.mult, [KDt[d], e2], [FMo[d]])
                    TTo("pool", FMo[d].ap[:, c4, 768:1024], Bt[d].ap[:, c4, :], e2.ap, ALU.mult, [Bt[d], e2], [FMo[d]])
                    TTo("dve", KGo[d].ap[:, c4, :], KDt[d].ap[:, c4, :], e3.ap, ALU.mult, [KDt[d], e3], [KGo[d]])
                    TTo("pool", BGo[d].ap[:, c4, :], Bt[d].ap[:, c4, :], e3.ap, ALU.mult, [Bt[d], e3], [BGo[d]])
                t7 = tmp()
                TTo("pool", t7.ap, KDt[0].ap[:, c4, :], KDt[1].ap[:, c4, :], ALU.add, [KDt[0], KDt[1]], [t7])
                STT("dve", SQb.ap, t7.ap, pvc(f"rk{L}", c4), Rv[:, c4, :], ALU.mult, ALU.mult, [t7, pv, SH], [SQb])
                bk = bank()
                MM(bk.ap[:, 0:SEG], [(bdb.ap, SQb.ap)], [bdb, SQb], [bk])
                TTo("dve", BGo2.ap[:, 0, c4, :], bk.ap[:, 0:SEG], Vv[:, c4, :], ALU.mult, [bk, SH], [BGo2])
            if KLIM <= 2:
                continue
            for d in range(2):
                DMA("sp", FM[d].ap[s], FMo[d].ap.rearrange("p j n -> p (j n)"), [FMo[d]], [FM[d]])
                DMA("sp", GC[d].ap[s], GCo[d].ap.rearrange("p j c -> p (j c)"), [GCo[d]], [GC[d]])
            DMA("sp", BGD.ap[s], BGo2.ap.rearrange("p a j t -> p (a j t)"), [BGo2], [BGD])
            if KLIM <= 3:
                continue
            for hh in range(2):
                to = TMo[tmo_i[0] % 2]; tmo_i[0] += 1
                for qi, srct in enumerate((Vb, KGo[0], BGo[0], KGo[1], BGo[1])):
                    bb = bankbf()
                    TRS([(bb.ap[:, c4 * 128:(c4 + 1) * 128], srct.ap[:, c4, hh * 128:(hh + 1) * 128]) for c4 in range(4)],
                        identb.ap, [srct, identb], [bb])
                    CP("act" if qi % 2 else "dve", to.ap[:, qi, :], bb.ap, [bb], [to])
                DMA("sp", TMD.ap[t0 + hh * 128:t0 + (hh + 1) * 128].rearrange("t q n -> t (q n)"),
                    to.ap.rearrange("p q n -> p (q n)"), [to], [TMD])

    def pass_scan(L, d, store_ctx, segs_override=None):
        setoff(W_BASE)
        LD = []
        for k in range(2):
            LD.append(dict(FM=tile([64, 4, 2, 1024], BF16, f"S_FM{k}"), TV=tile([64, 4, 512], BF16, f"S_TV{k}"),
                           TG=tile([64, 4, 2, 512], BF16, f"S_TG{k}"), GC=tile([64, 4, 2, 4], F32, f"S_GC{k}")))
        AT1s = [tile([64, 8, 128], BF16, f"S_AT1_{c}") for c in range(4)]
        AT2s = [tile([64, 8, 128], BF16, f"S_AT2_{c}") for c in range(4)]
        ZFs = [[tile([64, 8, 128], F32, f"S_ZF{c}_{k}") for k in range(2)] for c in range(4)]
        NNs = [[tile([64, 8, 64], F32, f"S_NN{c}_{k}") for k in range(2)] for c in range(4)]
        RHSb = tile([64, 8, 64], F32, "S_RHS"); UNb = tile([64, 8, 64], BF16, "S_UN")
        Hf = tile([64, 8, 64], F32, "S_H"); Hh = tile([64, 8, 64], BF16, "S_Hb")
        YO = [tile([64, 8, SEG], F32, f"S_YO{k}") for k in range(2)]
        S.op("dve", lambda e: e.memset(Hf.ap, 0.0), writes=[Hf.b])
        S.op("pool", lambda e: e.memset(Hh.ap, 0.0), writes=[Hh.b])
        m1 = cst.ap[0:64, CO["m1b" if d else "m1f"]:CO["m1b" if d else "m1f"] + 128].unsqueeze(1).to_broadcast([64, 8, 128])
        m3 = cst.ap[0:64, CO["m3b" if d else "m3f"]:CO["m3b" if d else "m3f"] + 64].unsqueeze(1).to_broadcast([64, 8, 64])
        id64 = cst.ap[0:64, CO["id64"]:CO["id64"] + 64].unsqueeze(1).to_broadcast([64, 8, 64])
        segs = [0] + (list(range(NSEG - 1, 0, -1)) if d else list(range(1, NSEG)))
        if segs_override is not None:
            segs = segs_override

        def load(si):
            s = segs[si]; ld = LD[si % 2]
            t0 = segcol(s)
            for hh in range(2):
                DMA("sp", ld["FM"].ap[:, :, hh, :], FM[d].ap[s, hh * 64:(hh + 1) * 64, :].rearrange("k (j n) -> k j n", j=4),
                    [FM[d]], [ld["FM"]])
                DMA("sp", ld["GC"].ap[:, :, hh, :], GC[d].ap[s, hh * 64:(hh + 1) * 64, :].rearrange("k (j c) -> k j c", j=4),
                    [GC[d]], [ld["GC"]])
            tmv = TMD.ap[t0:t0 + SEG].rearrange("(c s) q n -> s c q n", s=64)
            DMA("sp", ld["TV"].ap, tmv[:, :, 0, :], [TMD], [ld["TV"]])
            DMA("sp", ld["TG"].ap, tmv[:, :, 1 + 2 * d:3 + 2 * d, :], [TMD], [ld["TG"]])

        def hv(pb, h, w):
            return pb[h // 4].ap[0:64, (h % 4) * w:(h % 4 + 1) * w]
        def fmv_of(ld):
            return ld["FM"].ap.rearrange("k j two n -> k (j two) n")

        def stageA(ld, c):
            fmv = fmv_of(ld)
            kq = fmv[:, :, c * 128:(c + 1) * 128]
            kh = fmv[:, :, 512 + c * CH:512 + (c + 1) * CH]
            bh = fmv[:, :, 768 + c * CH:768 + (c + 1) * CH]
            AT1 = AT1s[c]; AT2 = AT2s[c]
            pa1 = [bank(), bank()]; pa2 = [bank(), bank()]; pa3 = bank()
            MMS([(hv(pa1, h, 128), [(kh[:, h, :], kq[:, h, :])]) for h in range(8)], [ld["FM"]], pa1)
            MMS([(hv(pa2, h, 128), [(bh[:, h, :], kq[:, h, :])]) for h in range(8)], [ld["FM"]], pa2)
            MMS([(pa3.ap[0:64, h * 64:(h + 1) * 64], [(kq[:, h, 0:64], bh[:, h, :])]) for h in range(8)], [ld["FM"]], [pa3])
            zf, nn = ZFs[c][0], NNs[c][0]
            for hf in range(2):
                TTo("dve", AT1.ap[:, hf * 4:(hf + 1) * 4, :], pa1[hf].ap[0:64, :].rearrange("p (h x) -> p h x", x=128),
                    m1[:, 0:4, :], ALU.mult, [pa1[hf], cst], [AT1])
                TTo("dve", AT2.ap[:, hf * 4:(hf + 1) * 4, :], pa2[hf].ap[0:64, :].rearrange("p (h x) -> p h x", x=128),
                    m1[:, 0:4, :], ALU.mult, [pa2[hf], cst], [AT2])
                TTo("dve", zf.ap[:, hf * 4:(hf + 1) * 4, 0:64], pa2[hf].ap[0:64, :].rearrange("p (h x) -> p h x", x=128)[:, :, 0:64],
                    m1[:, 0:4, 0:64], ALU.mult, [pa2[hf], cst], [zf])
            TTo("dve", nn.ap, pa3.ap[0:64, :].rearrange("p (h x) -> p h x", x=64), m3, ALU.mult, [pa3, cst], [nn])
            TTo("pool", zf.ap[:, :, 64:128], id64, zf.ap[:, :, 0:64], ALU.subtract, [cst, zf], [zf])

        def level(lev, c):
            cur_i = lev % 2
            last = (lev == 5)
            zf = ZFs[c][cur_i]; nn = NNs[c][cur_i]; zf2 = ZFs[c][1 - cur_i]; nn2 = NNs[c][1 - cur_i]
            pz = [bank(), bank()]
            if lev == 0:
                MMS([(hv(pz, h, 128)[:, 0:64], [(nn.ap[:, h, :], zf.ap[:, h, 0:64])]) for h in range(8)], [nn, zf], pz)
            elif not last:
                MMS([(hv(pz, h, 128), [(nn.ap[:, h, :], zf.ap[:, h, :])]) for h in range(8)], [nn, zf], pz)
            else:
                MMS([(hv(pz, h, 128)[:, 64:128], [(nn.ap[:, h, :], zf.ap[:, h, 64:128])]) for h in range(8)], [nn, zf], pz)
            if not last:
                pn = bank()
                MMS([(pn.ap[0:64, h * 64:(h + 1) * 64], [(zf.ap[:, h, 0:64], nn.ap[:, h, :])]) for h in range(8)], [nn, zf], [pn])
                CP("act", nn2.ap, pn.ap[0:64, :].rearrange("p (h x) -> p h x", x=64), [pn], [nn2])
            for hf in range(2):
                pzv = pz[hf].ap[0:64, :].rearrange("p (h x) -> p h x", x=128)
                hs = slice(hf * 4, (hf + 1) * 4)
                if not last:
                    CP("act", zf2.ap[:, hs, 0:64], pzv[:, :, 0:64], [pz[hf]], [zf2])
                if lev == 0:
                    CP("pool", zf2.ap[:, hs, 64:128], zf.ap[:, hs, 64:128], [zf], [zf2])
                else:
                    TTo("dve", zf2.ap[:, hs, 64:128], pzv[:, :, 64:128], zf.ap[:, hs, 64:128], ALU.add, [pz[hf], zf], [zf2])

        def p3a(ld, yo, c):
            kq = fmv_of(ld)[:, :, c * 128:(c + 1) * 128]
            vt = ld["TV"].ap[:, c, :].rearrange("s (h v) -> s h v", v=64)
            AT1 = AT1s[c]
            pr = bank()
            MMS([(pr.ap[0:64, h * 64:(h + 1) * 64], [(kq[:, h, 0:64], Hh.ap[:, h, :]), (AT1.ap[:, h, 0:64], vt[:, h, :])])
                 for h in range(8)], [ld["FM"], Hh, AT1, ld["TV"]], [pr])
            CP("act", RHSb.ap, pr.ap[0:64, :].rearrange("p (h x) -> p h x", x=64), [pr], [RHSb])
        def p3b(ld, yo, c):
            Fm = ZFs[c][0]
            pu = bank()
            MMS([(pu.ap[0:64, h * 64:(h + 1) * 64], [(Fm.ap[:, h, 64:128], RHSb.ap[:, h, :])]) for h in range(8)], [Fm, RHSb], [pu])
            S.op("act", lambda e, pu=pu: e.mul(out=UNb.ap, in_=pu.ap[0:64, :].rearrange("p (h x) -> p h x", x=64), mul=-1.0),
                 reads=[pu.b], writes=[UNb.b])
        def p3c(ld, yo, c):
            kq = fmv_of(ld)[:, :, c * 128:(c + 1) * 128]
            vt = ld["TV"].ap[:, c, :].rearrange("s (h v) -> s h v", v=64)
            kg = ld["TG"].ap[:, c, 0, :].rearrange("s (h v) -> s h v", v=64)
            bg = ld["TG"].ap[:, c, 1, :].rearrange("s (h v) -> s h v", v=64)
            AT1 = AT1s[c]; AT2 = AT2s[c]
            py = bank()
            MMS([(py.ap[0:64, h * 64:(h + 1) * 64],
                  [(Hh.ap[:, h, :], kq[:, h, 64:128]), (vt[:, h, :], AT1.ap[:, h, 64:128]), (UNb.ap[:, h, :], AT2.ap[:, h, 64:128])])
                 for h in range(8)], [Hh, ld["FM"], ld["TV"], AT1, UNb, AT2], [py])
            CP("act", yo.ap[:, :, c * CH:(c + 1) * CH], py.ap[0:64, :].rearrange("p (h x) -> p h x", x=64), [py], [yo])
            ph = bank()
            MMS([(ph.ap[0:64, h * 64:(h + 1) * 64], [(kg[:, h, :], vt[:, h, :]), (bg[:, h, :], UNb.ap[:, h, :])]) for h in range(8)],
                [ld["TG"], ld["TV"], UNb], [ph])
            gcb = ld["GC"].ap.rearrange("k j two c -> k (j two) c")[:, :, c:c + 1].to_broadcast([64, 8, 64])
            TTo("dve", Hf.ap, Hf.ap, gcb, ALU.mult, [Hf, ld["GC"]], [Hf])
            TTo("dve", Hf.ap, Hf.ap, ph.ap[0:64, :].rearrange("p (h x) -> p h x", x=64), ALU.add, [Hf, ph], [Hf])
            CP("pool", Hh.ap, Hf.ap, [Hf], [Hh])
        def store_y(si):
            s = segs[si]; yo = YO[si % 2]
            if s > 0 or store_ctx:
                for hh in range(2):
                    DMA("sp", YD[d].ap[s, hh * 64:(hh + 1) * 64, :].rearrange("v (j t) -> v j t", j=4),
                        yo.ap.rearrange("v (j two) t -> v j two t", two=2)[:, :, hh, :], [yo], [YD[d]])

        chs = [3, 2, 1, 0] if d else [0, 1, 2, 3]
        units = [(si, p, chs[2 * p:2 * p + 2]) for si in range(len(segs)) for p in range(2)]
        load(0)
        prev = None
        for (si, p, cc2) in units:
            if p == 1 and si + 1 < len(segs):
                load(si + 1)
            ld = LD[si % 2]
            subs = []
            if prev is not None:
                psi, pp, pcc = prev
                pld = LD[psi % 2]; pyo = YO[psi % 2]
                for c in pcc:
                    subs += [(p3a, pld, pyo, c), (p3b, pld, pyo, c), (p3c, pld, pyo, c)]
            groups = [lambda: [stageA(ld, c) for c in cc2]] + [(lambda lev=lev: [level(lev, c) for c in cc2]) for lev in range(6)]
            for gi, g in enumerate(groups):
                g()
                if gi < len(subs):
                    f, a, b, c = subs[gi]
                    f(a, b, c)
            if prev is not None and prev[1] == 1:
                store_y(prev[0])
            prev = (si, p, cc2)
        psi, pp, pcc = prev
        for c in pcc:
            p3a(LD[psi % 2], YO[psi % 2], c); p3b(LD[psi % 2], YO[psi % 2], c); p3c(LD[psi % 2], YO[psi % 2], c)
        store_y(psi)

    def pass_E3(L, segs):
        i = L // 2
        load_w(WB, ev_w_out[i].rearrange("(k p) n -> p k n", p=128), D, 8)
        setoff(W_BASE)
        Y0 = tile([128, 4, SEG], F32, "E3_Y0"); Y1 = tile([128, 4, SEG], F32, "E3_Y1")
        BGt = tile([128, 2, 4, SEG], BF16, "E3_BG")
        YC = tile([128, 4, SEG], F32, "E3_YC")
        MIX = ACTF
        for s in segs:
            xt = XT[s % 2]
            col = 0 if s > 0 else 1
            t0 = segcol(s)
            load_x(xt, s)
            DMA("sp", Y0.ap.rearrange("p k t -> p (k t)"), YD[0].ap[s], [YD[0]], [Y0])
            DMA("sp", Y1.ap.rearrange("p k t -> p (k t)"), YD[1].ap[s], [YD[1]], [Y1])
            DMA("sp", BGt.ap.rearrange("p a k t -> p (a k t)"), BGD.ap[s], [BGD], [BGt])
            DMA("sp", MIX.ap[:, 4:8, :].rearrange("p k t -> p (k t)"), YP.ap[s], [YP], [MIX])
            TTo("pool", Y0.ap, Y0.ap, Y1.ap, ALU.add, [Y0, Y1], [Y0])
            for c4 in range(4):
                CP("act", SQ.ap[:, c4, :], Y0.ap[:, c4, :], [Y0], [SQ])
                bk = bank()
                MM(bk.ap[:, 0:SEG], [(bdb.ap, SQ.ap[:, c4, :])], [bdb, SQ], [bk])
                STT("dve", YC.ap[:, c4, :], bk.ap[:, 0:SEG], -1.0 / 64, Y0.ap[:, c4, :], ALU.mult, ALU.add, [bk, Y0], [YC])
                ACT(SQ.ap[:, c4, :], YC.ap[:, c4, :], AF.Square, [YC], [SQ])
                bk = bank()
                MM(bk.ap[:, 0:SEG], [(bdb.ap, SQ.ap[:, c4, :])], [bdb, SQ], [bk])
                t1 = tmp()
                TS("dve", t1.ap, bk.ap[:, 0:SEG], 1.0 / 64, GN_EPS, ALU.mult, ALU.add, [bk], [t1])
                ACT(t1.ap, t1.ap, AF.Sqrt, [t1], [t1])
                RECIP(t1.ap, t1.ap, [t1], [t1])
                TTo("dve", YC.ap[:, c4, :], YC.ap[:, c4, :], t1.ap, ALU.mult, [YC, t1], [YC])
                ACT(YC.ap[:, c4, :], YC.ap[:, c4, :], AF.Identity, [YC, pv], [YC], scale=pvc(f"gnw{L}", c4), bias=pvc(f"gnb{L}", c4))
                TTo("pool", YC.ap[:, c4, :], YC.ap[:, c4, :], BGt.ap[:, 0, c4, :], ALU.add, [YC, BGt], [YC])
                TTo("dve", MIX.ap[:, c4, :], YC.ap[:, c4, :], BGt.ap[:, 1, c4, :], ALU.mult, [YC, BGt], [MIX])
            for oc in range(8):
                bk = bank()
                MM(bk.ap[:, 0:SEG], [(WB.ap[:, k, oc * 128:(oc + 1) * 128], MIX.ap[:, k, :]) for k in range(8)], [WB, MIX], [bk])
                CP("act", Yt.ap[:, oc, :], bk.ap[:, 0:SEG], [bk], [Yt])
            residual(xt, L, 2, col)
            store_x(xt, s)

    def pass_O(L, segs):
        i = L // 2
        load_w(WA, od_w_in[i].rearrange("(k p) n -> p k n", p=128), 3 * D, 8)
        load_w(WB, od_w_out[i].rearrange("(k p) n -> p k n", p=128), D, 8)
        MIX = ACTF
        load_x(XT[segs[0] % 2], segs[0])
        norm_mod(XT[segs[0] % 2], L, 0, 1, 0 if segs[0] > 0 else 1)
        for si, s in enumerate(segs):
            xt = XT[s % 2]
            col = 0 if s > 0 else 1
            nr, rl = grid(s)
            for j in range(8):
                pb = bank(); pc = bank(); pu = bank()
                for (bk, c0) in ((pb, j * 128), (pc, D + j * 128), (pu, 2 * D + j * 128)):
                    MM(bk.ap[:, 0:SEG], [(WA.ap[:, k, c0:c0 + 128], Hb.ap[:, k, :]) for k in range(8)], [WA, Hb], [bk])
                t1 = tmp(); t2 = tmp(); t3 = tmp()
                CP("act", t1.ap, pu.ap[:, 0:SEG], [pu], [t1])
                TTo("dve", t2.ap, pc.ap[:, 0:SEG], t1.ap, ALU.mult, [pc, t1], [t2])
                conv3(t3, t2.ap, [t2, pv], False, pvc(f"oconv{L}_0", j), pvc(f"oconv{L}_1", j), pvc(f"oconv{L}_2", j), nr, rl)
                TTo("dve", MIX.ap[:, j, :], pb.ap[:, 0:SEG], t3.ap, ALU.mult, [pb, t3], [MIX])
            if si + 1 < len(segs):
                sn = segs[si + 1]
                load_x(XT[sn % 2], sn)
                norm_mod(XT[sn % 2], L, 0, 1, 0 if sn > 0 else 1)
            for oc in range(8):
                bk = bank()
                MM(bk.ap[:, 0:SEG], [(WB.ap[:, k, oc * 128:(oc + 1) * 128], MIX.ap[:, k, :]) for k in range(8)], [WB, MIX], [bk])
                CP("act", Yt.ap[:, oc, :], bk.ap[:, 0:SEG], [bk], [Yt])
            residual(xt, L, 2, col)
            store_x(xt, s)

    def pass_F(L, segs):
        load_w(WA, ffn_w_up[L].rearrange("(k p) n -> p k n", p=128), 2 * DFF, 8)
        load_w(WB, ffn_w_down[L].rearrange("(k p) n -> p k n", p=128), D, NFF)
        load_x(XT[segs[0] % 2], segs[0])
        norm_mod(XT[segs[0] % 2], L, 3, 4, 0 if segs[0] > 0 else 1)
        for si, s in enumerate(segs):
            xt = XT[s % 2]
            col = 0 if s > 0 else 1
            nr, rl = grid(s)
            for j in range(NFF):
                pc = bank(); pg = bank()
                for (bk, c0) in ((pc, j * 128), (pg, DFF + j * 128)):
                    MM(bk.ap[:, 0:SEG], [(WA.ap[:, k, c0:c0 + 128], Hb.ap[:, k, :]) for k in range(8)], [WA, Hb], [bk])
                t1 = tmp(); t2 = tmp()
                conv3(t1, pc.ap[:, 0:SEG], [pc, pv], True, pvc(f"fconv{L}_0", j), pvc(f"fconv{L}_1", j), pvc(f"fconv{L}_2", j), nr, rl)
                ACT(t2.ap, t1.ap, AF.Silu, [t1], [t2])
                TTo("dve", ACTF.ap[:, j, :], pg.ap[:, 0:SEG], t2.ap, ALU.mult, [pg, t2], [ACTF])
            if si + 1 < len(segs):
                sn = segs[si + 1]
                load_x(XT[sn % 2], sn)
                norm_mod(XT[sn % 2], L, 3, 4, 0 if sn > 0 else 1)
            for oc in range(8):
                bk = bank()
                MM(bk.ap[:, 0:SEG], [(WB.ap[:, k, oc * 128:(oc + 1) * 128], ACTF.ap[:, k, :]) for k in range(NFF)], [WB, ACTF], [bk])
                CP("act", Yt.ap[:, oc, :], bk.ap[:, 0:SEG], [bk], [Yt])
            residual(xt, L, 5, col)
            store_x(xt, s)

    def pass_out():
        setoff(W_BASE)
        OT = [tile([128, 2, D], F32, f"O_OT{k}") for k in range(2)]
        for s in range(1, NSEG):
            xt = XT[s % 2]
            ot = OT[s % 2]
            load_x(xt, s)
            for hh in range(2):
                for q in range(2):
                    bk = bank()
                    TRS([(bk.ap[:, f * 128:(f + 1) * 128], xt.ap[:, q * 4 + f, hh * 128:(hh + 1) * 128]) for f in range(4)],
                        ident, [xt, cst], [bk])
                    CP("act" if q else "dve", ot.ap[:, hh, q * 512:(q + 1) * 512], bk.ap, [bk], [ot])
            DMA("sp", out_d[(s - 1) * SEG:s * SEG].rearrange("(h p) f -> p h f", p=128), ot.ap, [ot], [OUTB])

    ALLS = list(range(NSEG)); LAT = list(range(1, NSEG))
    plist = []
    for L in range(DEPTH):
        ctx_later = L in (0, 1)
        if L % 2 == 0:
            plist.append((f"E1_{L}", lambda L=L: pass_E1(L, ALLS)))
            plist.append((f"prep_{L}", lambda L=L: pass_prep(L, ALLS)))
            plist.append((f"scan0_{L}", lambda L=L, c=ctx_later: pass_scan(L, 0, c)))
            plist.append((f"scan1_{L}", lambda L=L, c=ctx_later: pass_scan(L, 1, c)))
            plist.append((f"E3_{L}", lambda L=L, c=ctx_later: pass_E3(L, ALLS if c else LAT)))
        else:
            plist.append((f"O_{L}", lambda L=L, c=ctx_later: pass_O(L, ALLS if c else LAT)))
        plist.append((f"F_{L}", lambda L=L, c=ctx_later: pass_F(L, ALLS if c else LAT)))
    plist.append(("out", pass_out))
    if debug_stop in ("mod", "mod1"):
        plist = []
    if debug_stop == "prep_only":
        plist = [("prep_only", lambda: pass_prep(0, list(range(int(os.environ.get("PSEGS", "2"))))))]
    if debug_stop == "mini":
        plist = [("a", lambda: pass_E1(0, [0, 1, 2, 3, 4])), ("b", lambda: pass_prep(0, [0, 1, 2, 3, 4])),
                 ("c", lambda: pass_scan(0, 0, True, [0, 1, 2, 3, 4])), ("mini", lambda: pass_scan(0, 1, True, [0]))]
    if debug_stop == "mini2":
        plist = [("a", lambda: pass_E1(0, [0, 1])), ("b", lambda: pass_prep(0, [0, 1])),
                 ("c", lambda: pass_scan(0, 0, True, [0, 1])), ("d", lambda: pass_scan(0, 1, True, [0])),
                 ("e", lambda: pass_E3(0, [0])), ("f", lambda: pass_F(0, [0])), ("g", lambda: pass_O(1, [0])), ("mini2", lambda: pass_F(1, [0]))]
    if debug_stop == "testB":
        plist = [("f", lambda: pass_F(0, [1])), ("g", lambda: pass_O(1, [1])), ("testB", lambda: pass_F(1, [1]))]
    if debug_stop == "testC":
        plist = [("testC", pass_out)]
    if debug_stop == "scan_only":
        plist = [("scan_only", lambda: pass_scan(0, 0, True))]
    snaps = {}
    for name, fn in plist:
        fn()
        if debug_stop in ("mini2", "testB") and name in ("e", "f", "g", "mini2", "testB"):
            sn = nc.dram_tensor("snap_" + name, [128, 8 * SEG], F32, kind="Internal").ap()
            sb = T(sn, Buf("snap_" + name))
            DMA("sp", sn, XS.ap[0 if debug_stop == "mini2" else 1], [XS], [sb])
            snaps[name] = sb
        if debug_stop == name:
            break
    if debug_stop:
        dbg = nc.dram_tensor("dbg_mdv", [128, DEPTH * 6 * 8 * 2], F32, kind="ExternalOutput").ap()
        dbt = T(dbg, Buf("dbg"))
        DMA("sp", dbg, mdv.ap.rearrange("p a b c d -> p (a b c d)"), [mdv], [dbt])
        fin = [t.b for t in scr_list if t.b.lw is not None and t.b.name in debug_outs] + [dbt.b]
        if OUTB.b.lw is not None:
            fin.append(OUTB.b)
        S.finish("sp", fin)
    else:
        S.finish("sp", [OUTB.b])
    S.emit()
    return nc

_NC_CACHE = {}

def kernel(**inputs):
    inp = {k: np.asarray(v) for k, v in inputs.items()}
    if "nc" not in _NC_CACHE:
        _NC_CACHE["nc"] = build_program()
    nc = _NC_CACHE["nc"]
    f = lambda a: np.ascontiguousarray(a, dtype=np.float32)
    shared = {k: f(inp[k]) for k in ("w_mod", "ffn_w_up", "ffn_w_down", "ev_w_in", "ev_w_out", "ev_w_up", "ev_a_up",
                                     "ev_g_up", "ev_pool_w", "od_w_in", "od_w_out")}
    shared["consts"] = CONSTS
    in_maps = []
    for b in range(8):
        m = dict(shared)
        m["x"] = f(inp["x"][b]); m["ctx"] = f(inp["ctx"][b]); m["pvec"] = build_pvec(inp, b)
        in_maps.append(m)
    res = run_bass_kernel_spmd(nc, in_maps, core_ids=list(range(8)))
    return np.stack([np.asarray(r["out"], dtype=np.float32) for r in res.results], axis=0)
```
